# Optimizing a Trainium2 kernel written in Bass

```python
import math
import jax, jax.numpy as jnp
from jax import lax
import numpy as np

D_MODEL = 1024
BATCH = 4
SEQ = 8192
DEPTH = 2
DEC_BATCH = 32
DEC_SEQ = 16
PAST_LEN = 4096

CHUNK = 64
HEAD_DIM = 64
Q_BLOCK = 128
H_A = 8
FORGET_BIAS_INIT = 3.0
H_B = 8
B_LEFT_CHUNKS = 8
B_REL_CLIP = 128
H_C = 16
HKV_C = 2
G_C = H_C // HKV_C
WINDOW = 128
C_LEFT_CHUNKS = WINDOW // CHUNK
T5_BUCKETS = 32
T5_MAX_DIST = 128
D_FF = -(-8 * D_MODEL // (3 * 256)) * 256
EPS = 1e-6
N_EVEN = (DEPTH + 1) // 2
N_ODD = DEPTH // 2
W_A = H_A * HEAD_DIM
W_B = H_B * HEAD_DIM
EVEN_IN = 3 * W_A + 3 * W_B + H_A
ODD_IN = (H_C + 2 * HKV_C) * HEAD_DIM

kernel_name = 'fox_chunkband_swa_sink_stream_step'


def rmsnorm(x, g):
    x32 = x.astype(jnp.float32)
    y = x32 * lax.rsqrt(jnp.mean(x32 * x32, axis=-1, keepdims=True) + EPS)
    return (y * g.astype(jnp.float32)).astype(x.dtype)


def swiglu(h, w_in, w_out):
    gate, up = jnp.split(h @ w_in, 2, axis=-1)
    return (jax.nn.silu(gate) * up) @ w_out


def heads(t, n):
    return t.reshape(t.shape[:-1] + (n, HEAD_DIM))


def fox_attend(q, k, v, cq, ck, qpos, kpos):
    s = jnp.einsum('bqhd,bkhd->bhqk', q, k, preferred_element_type=jnp.float32) * HEAD_DIM ** -0.5
    s = s + (jnp.swapaxes(cq, 1, 2)[..., :, None] - jnp.swapaxes(ck, 1, 2)[..., None, :])
    s = jnp.where(kpos[None, :] <= qpos[:, None], s, -jnp.inf)
    p = jax.nn.softmax(s, axis=-1)
    return jnp.einsum('bhqk,bkhd->bqhd', p.astype(v.dtype), v)


def fox_prompt(q, k, v, logf):
    b, s = q.shape[:2]
    nb = s // Q_BLOCK
    c = jnp.cumsum(logf, axis=1)
    pos = jnp.arange(s)
    qb = q.reshape(b, nb, Q_BLOCK, H_A, HEAD_DIM).swapaxes(0, 1)
    cb = c.reshape(b, nb, Q_BLOCK, H_A).swapaxes(0, 1)
    pb = pos.reshape(nb, Q_BLOCK)
    out = lax.map(lambda a: fox_attend(a[0], k, v, a[1], c, a[2], pos), (qb, cb, pb))
    return out.swapaxes(0, 1).reshape(b, s, H_A, HEAD_DIM)


def fox_sample(q, k, v, logf, cache_k, cache_v, cache_logf):
    past = cache_k.shape[1]
    t = q.shape[1]
    k_all = jnp.concatenate([cache_k.astype(k.dtype), k], axis=1)
    v_all = jnp.concatenate([cache_v.astype(v.dtype), v], axis=1)
    c = jnp.cumsum(jnp.concatenate([cache_logf.astype(jnp.float32), logf], axis=1), axis=1)
    return fox_attend(q, k_all, v_all, c[:, past:], c, past + jnp.arange(t), jnp.arange(past + t))


def band_attend(q, k, v, bias, valid, sinks=None):
    s = jnp.einsum('bnqhgd,bnkhd->bnhgqk', q, k, preferred_element_type=jnp.float32) * HEAD_DIM ** -0.5
    s = jnp.where(valid, s + bias, -jnp.inf)
    if sinks is None:
        p = jax.nn.softmax(s, axis=-1)
    else:
        sk = sinks.astype(jnp.float32)[..., None, None]
        m = jnp.maximum(jnp.max(s, axis=-1, keepdims=True), sk)
        e = jnp.exp(s - m)
        p = e / (jnp.sum(e, axis=-1, keepdims=True) + jnp.exp(sk - m))
    return jnp.einsum('bnhgqk,bnkhd->bnqhgd', p.astype(v.dtype), v)


def chunk_band(t, left):
    b, s, h, d = t.shape
    n = s // CHUNK
    tp = jnp.pad(t.reshape(b, n, CHUNK, h, d), ((0, 0), (left, 0), (0, 0), (0, 0), (0, 0)))
    band = jnp.stack([tp[:, j:j + n] for j in range(left + 1)], axis=2)
    return band.reshape(b, n, (left + 1) * CHUNK, h, d)


def band_positions(n, left):
    band_len = (left + 1) * CHUNK
    kpos = (jnp.arange(n)[:, None] - left) * CHUNK + jnp.arange(band_len)[None, :]
    rel = jnp.arange(CHUNK)[:, None] - jnp.arange(band_len)[None, :] + left * CHUNK
    return kpos, rel


def b_bias(table, rel):
    idx = jnp.clip(rel, -B_REL_CLIP, B_REL_CLIP) + B_REL_CLIP
    return jnp.transpose(table[idx].astype(jnp.float32), (2, 0, 1))[:, None]


def t5_bucket(rel_mem):
    nb = T5_BUCKETS // 2
    max_exact = nb // 2
    n = jnp.abs(rel_mem)
    large = max_exact + (jnp.log(jnp.maximum(n, 1).astype(jnp.float32) / max_exact)
                         / math.log(T5_MAX_DIST / max_exact) * (nb - max_exact)).astype(jnp.int32)
    large = jnp.minimum(large, nb - 1)
    return jnp.where(rel_mem > 0, nb, 0) + jnp.where(n < max_exact, n, large)


def c_bias(table, rel):
    q_len, k_len = rel.shape
    idx = t5_bucket(-rel)
    return jnp.transpose(table[idx].astype(jnp.float32), (2, 0, 1)).reshape(HKV_C, G_C, q_len, k_len)


def even_proj(h, w_in, b_forget):
    p = h @ w_in
    qa, ka, va, qb, kb, vb, fa = jnp.split(
        p, [W_A, 2 * W_A, 3 * W_A, 3 * W_A + W_B, 3 * W_A + 2 * W_B, 3 * W_A + 3 * W_B], axis=-1)
    logf = jax.nn.log_sigmoid(fa.astype(jnp.float32) + b_forget.astype(jnp.float32))
    return (heads(qa, H_A), heads(ka, H_A), heads(va, H_A), logf,
            heads(qb, H_B), heads(kb, H_B), heads(vb, H_B))


def even_prompt(h, w_in, b_forget, rel_tab, w_out):
    b, s, _ = h.shape
    n = s // CHUNK
    qa, ka, va, logf, qb, kb, vb = even_proj(h, w_in, b_forget)
    oa = fox_prompt(qa, ka, va, logf)
    kpos, rel = band_positions(n, B_LEFT_CHUNKS)
    ob = band_attend(qb.reshape(b, n, CHUNK, H_B, 1, HEAD_DIM),
                     chunk_band(kb, B_LEFT_CHUNKS), chunk_band(vb, B_LEFT_CHUNKS),
                     b_bias(rel_tab, rel), (kpos >= 0)[None, :, None, None, None, :])
    y = jnp.concatenate([oa.reshape(b, s, W_A), ob.reshape(b, s, W_B)], axis=-1) @ w_out
    lb = min(B_LEFT_CHUNKS * CHUNK, s)
    return y, (ka, va, logf, kb[:, s - lb:], vb[:, s - lb:])


def even_sample(h, cache_ak, cache_av, cache_alogf, cache_bk, cache_bv, w_in, b_forget, rel_tab, w_out):
    b, t, _ = h.shape
    past = cache_ak.shape[1]
    lb = cache_bk.shape[1]
    qa, ka, va, logf, qb, kb, vb = even_proj(h, w_in, b_forget)
    oa = fox_sample(qa, ka, va, logf, cache_ak, cache_av, cache_alogf)
    kpos = past - lb + jnp.arange(lb + t)
    rel = jnp.arange(t)[:, None] - jnp.arange(lb + t)[None, :] + lb
    k_band = jnp.concatenate([cache_bk.astype(kb.dtype), kb], axis=1)[:, None]
    v_band = jnp.concatenate([cache_bv.astype(vb.dtype), vb], axis=1)[:, None]
    ob = band_attend(qb.reshape(b, 1, t, H_B, 1, HEAD_DIM), k_band, v_band,
                     b_bias(rel_tab, rel), (kpos >= 0)[None, None, None, None, None, :])
    y = jnp.concatenate([oa.reshape(b, t, W_A), ob.reshape(b, t, W_B)], axis=-1) @ w_out
    return y, (ka, va, logf, kb, vb)


def odd_proj(h, w_in):
    q, k, v = jnp.split(h @ w_in, [H_C * HEAD_DIM, (H_C + HKV_C) * HEAD_DIM], axis=-1)
    return heads(q, H_C), heads(k, HKV_C), heads(v, HKV_C)


def odd_prompt(h, w_in, sinks, t5_tab, w_out):
    b, s, _ = h.shape
    n = s // CHUNK
    q, k, v = odd_proj(h, w_in)
    kpos, rel = band_positions(n, C_LEFT_CHUNKS)
    oc = band_attend(q.reshape(b, n, CHUNK, HKV_C, G_C, HEAD_DIM),
                     chunk_band(k, C_LEFT_CHUNKS), chunk_band(v, C_LEFT_CHUNKS),
                     c_bias(t5_tab, rel), (kpos >= 0)[None, :, None, None, None, :],
                     sinks.reshape(HKV_C, G_C))
    lc = min(C_LEFT_CHUNKS * CHUNK, s)
    return oc.reshape(b, s, H_C * HEAD_DIM) @ w_out, (k[:, s - lc:], v[:, s - lc:])


def odd_sample(h, cache_ck, cache_cv, w_in, sinks, t5_tab, w_out):
    b, t, _ = h.shape
    lc = cache_ck.shape[1]
    past = PAST_LEN if lc > PAST_LEN else lc + (0 * t)
    q, k, v = odd_proj(h, w_in)
    kpos = jnp.arange(lc + t) - lc + past - past
    rel = jnp.arange(t)[:, None] - jnp.arange(lc + t)[None, :] + lc
    k_band = jnp.concatenate([cache_ck.astype(k.dtype), k], axis=1)[:, None]
    v_band = jnp.concatenate([cache_cv.astype(v.dtype), v], axis=1)[:, None]
    oc = band_attend(q.reshape(b, 1, t, HKV_C, G_C, HEAD_DIM), k_band, v_band,
                     c_bias(t5_tab, rel), (kpos >= -lc)[None, None, None, None, None, :],
                     sinks.reshape(HKV_C, G_C))
    return oc.reshape(b, t, H_C * HEAD_DIM) @ w_out, (k, v)


def setup_inputs(seed: int = 0) -> dict:
    key = jax.random.key(seed)
    ks = iter(jax.random.split(key, 32))

    def nrm(shape, scale=1.0):
        return jax.random.normal(next(ks), shape, jnp.float32) * scale

    lb = min(B_LEFT_CHUNKS * CHUNK, PAST_LEN)
    lc = min(C_LEFT_CHUNKS * CHUNK, PAST_LEN)
    return {
        'x_prompt': nrm((BATCH, SEQ, D_MODEL)),
        'x_sample': nrm((DEC_BATCH, DEC_SEQ, D_MODEL)),
        'cache_a_k': nrm((N_EVEN, DEC_BATCH, PAST_LEN, H_A, HEAD_DIM)),
        'cache_a_v': nrm((N_EVEN, DEC_BATCH, PAST_LEN, H_A, HEAD_DIM)),
        'cache_a_logf': jax.nn.log_sigmoid(FORGET_BIAS_INIT + nrm((N_EVEN, DEC_BATCH, PAST_LEN, H_A))),
        'cache_b_k': nrm((N_EVEN, DEC_BATCH, lb, H_B, HEAD_DIM)),
        'cache_b_v': nrm((N_EVEN, DEC_BATCH, lb, H_B, HEAD_DIM)),
        'cache_c_k': nrm((N_ODD, DEC_BATCH, lc, HKV_C, HEAD_DIM)),
        'cache_c_v': nrm((N_ODD, DEC_BATCH, lc, HKV_C, HEAD_DIM)),
        'norm_mix': 1.0 + nrm((DEPTH, D_MODEL), 0.05),
        'norm_ffn': 1.0 + nrm((DEPTH, D_MODEL), 0.05),
        'norm_final': 1.0 + nrm((D_MODEL,), 0.05),
        'w_in_even': nrm((N_EVEN, D_MODEL, EVEN_IN), D_MODEL ** -0.5),
        'b_forget': FORGET_BIAS_INIT + nrm((N_EVEN, H_A), 0.5),
        'rel_bias_b': nrm((N_EVEN, 2 * B_REL_CLIP + 1, H_B), 0.1),
        'w_out_even': nrm((N_EVEN, W_A + W_B, D_MODEL), (W_A + W_B) ** -0.5),
        'w_in_odd': nrm((N_ODD, D_MODEL, ODD_IN), D_MODEL ** -0.5),
        'sinks_c': nrm((N_ODD, H_C), 0.5),
        'w_out_odd': nrm((N_ODD, H_C * HEAD_DIM, D_MODEL), (H_C * HEAD_DIM) ** -0.5),
        't5_bias': nrm((T5_BUCKETS, H_C), 0.1),
        'w_ffn_in': nrm((DEPTH, D_MODEL, 2 * D_FF), D_MODEL ** -0.5),
        'w_ffn_out': nrm((DEPTH, D_FF, D_MODEL), D_FF ** -0.5),
    }


def reference(x_prompt, x_sample, cache_a_k, cache_a_v, cache_a_logf, cache_b_k, cache_b_v,
              cache_c_k, cache_c_v, norm_mix, norm_ffn, norm_final, w_in_even, b_forget,
              rel_bias_b, w_out_even, w_in_odd, sinks_c, w_out_odd, t5_bias, w_ffn_in, w_ffn_out):
    xp, xs = x_prompt, x_sample
    ev_p, ev_s, od_p, od_s = [], [], [], []
    for layer in range(DEPTH):
        i = layer // 2
        hp = rmsnorm(xp, norm_mix[layer])
        hs = rmsnorm(xs, norm_mix[layer])
        if layer % 2 == 0:
            yp, st_p = even_prompt(hp, w_in_even[i], b_forget[i], rel_bias_b[i], w_out_even[i])
            ys, st_s = even_sample(hs, cache_a_k[i], cache_a_v[i], cache_a_logf[i], cache_b_k[i],
                                   cache_b_v[i], w_in_even[i], b_forget[i], rel_bias_b[i], w_out_even[i])
            ev_p.append(st_p)
            ev_s.append(st_s)
        else:
            yp, st_p = odd_prompt(hp, w_in_odd[i], sinks_c[i], t5_bias, w_out_odd[i])
            ys, st_s = odd_sample(hs, cache_c_k[i], cache_c_v[i], w_in_odd[i], sinks_c[i], t5_bias, w_out_odd[i])
            od_p.append(st_p)
            od_s.append(st_s)
        xp = xp + yp
        xs = xs + ys
        xp = xp + swiglu(rmsnorm(xp, norm_ffn[layer]), w_ffn_in[layer], w_ffn_out[layer])
        xs = xs + swiglu(rmsnorm(xs, norm_ffn[layer]), w_ffn_in[layer], w_ffn_out[layer])

    def stk(states, j):
        return jnp.stack([st[j] for st in states], axis=0)

    y_prompt = rmsnorm(xp, norm_final)
    y_sample = rmsnorm(xs, norm_final)
    return (y_prompt, y_sample,
            stk(ev_p, 0), stk(ev_p, 1), stk(ev_p, 2), stk(ev_p, 3), stk(ev_p, 4),
            stk(od_p, 0), stk(od_p, 1),
            stk(ev_s, 0), stk(ev_s, 1), stk(ev_s, 2), stk(ev_s, 3), stk(ev_s, 4),
            stk(od_s, 0), stk(od_s, 1))
```

```python
import os
import numpy as np
import concourse.bass as bass
import concourse.mybir as mybir
from concourse.bass_utils import run_bass_kernel_spmd

F32 = mybir.dt.float32
BF16 = mybir.dt.bfloat16
AF = mybir.ActivationFunctionType
ALU = mybir.AluOpType
AX = mybir.AxisListType

SAME_ENGINE_SYNC = True
NEG = -30000.0
LOADQ = "sp"
STOREQ = "pool"


class Buf:
    __slots__ = ("name", "t", "last_w", "readers")

    def __init__(self, name, t):
        self.name = name
        self.t = t
        self.last_w = None
        self.readers = []

    def __getitem__(self, idx):
        return self.t[idx]


class Op:
    __slots__ = ("eng", "meth", "kw", "deps", "is_dma", "sem", "val", "signal")

    def __init__(self, eng, meth, kw, is_dma):
        self.eng = eng
        self.meth = meth
        self.kw = kw
        self.is_dma = is_dma
        self.deps = []
        self.sem = None
        self.val = None
        self.signal = False


class Sched:
    ENGS = ("pe", "act", "dve", "pool", "sp")

    def __init__(self, nc):
        self.nc = nc
        self.ops = {e: [] for e in self.ENGS}
        self.dma_cnt = {}
        self.last_dma = {}
        self.pending = {e: [] for e in self.ENGS}
        self.nops = 0

    def dram(self, name, shape, dtype, kind="Internal"):
        return Buf(name, self.nc.dram_tensor(name, list(shape), dtype, kind=kind).ap())

    def op(self, eng, meth, reads=(), writes=(), dma=None, **kw):
        o = Op(eng, meth, kw, dma is not None)
        deps = list(self.pending[eng])
        self.pending[eng] = []
        for b in reads:
            if b.last_w is not None:
                deps.append(b.last_w)
        for b in writes:
            if b.last_w is not None:
                deps.append(b.last_w)
            deps.extend(b.readers)
        seen = set()
        for d in deps:
            if id(d) in seen or d is o:
                continue
            seen.add(id(d))
            if (not d.is_dma) and (not o.is_dma) and d.eng == eng:
                if eng == "pe" or not SAME_ENGINE_SYNC:
                    continue
            o.deps.append(d)
            d.signal = True
        for b in reads:
            b.readers.append(o)
        for b in writes:
            b.last_w = o
            b.readers = []
        if dma is not None:
            c = self.dma_cnt.get(dma, 0) + 16
            self.dma_cnt[dma] = c
            o.sem = ("dma", dma)
            o.val = c
            self.last_dma[dma] = o
        self.ops[eng].append(o)
        self.nops += 1
        return o

    def barrier(self):
        lasts = []
        for e in self.ENGS:
            for o in reversed(self.ops[e]):
                if not o.is_dma:
                    lasts.append(o)
                    break
        lasts.extend(self.last_dma.values())
        for e in self.ENGS:
            self.pending[e] = list(lasts)

    def emit(self):
        nc = self.nc
        sems = {}
        for e in self.ENGS:
            sems[("eng", e)] = nc.alloc_semaphore(name=f"s_{e}")
        for k in self.dma_cnt:
            sems[("dma", k)] = nc.alloc_semaphore(name=f"d_{k}")
        for e in self.ENGS:
            n = 0
            for o in self.ops[e]:
                if (not o.is_dma) and o.signal:
                    n += 1
                    o.sem = ("eng", e)
                    o.val = n
        final = dict(self.dma_cnt)

        def run(ename, eng):
            waited = {}
            for o in self.ops[ename]:
                need = {}
                for d in o.deps:
                    if need.get(d.sem, 0) < d.val:
                        need[d.sem] = d.val
                for k, v in need.items():
                    if waited.get(k, 0) < v:
                        eng.wait_ge(sems[k], v)
                        waited[k] = v
                ins = getattr(eng, o.meth)(**o.kw)
                if o.is_dma:
                    ins.then_inc(sems[o.sem], 16)
                elif o.signal:
                    ins.then_inc(sems[o.sem], 1)
            if ename == "sp":
                for k, v in final.items():
                    eng.wait_ge(sems[("dma", k)], v)

        with nc.Block() as block:
            @block.tensor
            def _(e):
                run("pe", e)

            @block.scalar
            def _(e):
                run("act", e)

            @block.vector
            def _(e):
                run("dve", e)

            @block.gpsimd
            def _(e):
                run("pool", e)

            @block.sync
            def _(e):
                run("sp", e)


class Arena:
    def __init__(self, nc, nbytes):
        self.t = nc.alloc_sbuf_tensor("arena", [128, nbytes // 2], BF16)
        self.cap = nbytes
        self.base = 0
        self.off = 0
        self.n = 0

    def mark_persistent(self):
        self.base = self.off

    def reset(self):
        self.off = self.base

    def alloc(self, name, shape, dtype):
        per = 1
        for s in shape[1:]:
            per *= s
        esz = 4 if dtype == F32 else 2
        nb = per * esz
        off = (self.off + 31) // 32 * 32
        assert off + nb <= self.cap, (name, off, nb, self.cap)
        self.off = off + nb
        v = self.t[0:shape[0], off // 2:(off + nb) // 2]
        if dtype == F32:
            v = v.bitcast(F32)
        if len(shape) == 3:
            v = v.rearrange("p (a b) -> p a b", a=shape[1])
        elif len(shape) == 4:
            v = v.rearrange("p (a b c) -> p a b c", a=shape[1], b=shape[2])
        self.n += 1
        return Buf(f"{name}_{self.n}", v)

    def ring(self, name, n, shape, dtype):
        return [self.alloc(f"{name}{i}", shape, dtype) for i in range(n)]


D = 1024
NCTX = 64
T0 = 31
NLOC = 33
TLOC = NLOC * 128
BT0 = 27
NBT = NCTX - BT0
DFF = 2816
SR = 64

C_ID = 0
C_TRIU = 128
C_ONE = 256
C_TRIM = 384
C_BMASK = 512
C_SMASK = 512 + 640
C_TRI16 = C_SMASK + 256
C_W = C_TRI16 + 16


def host_consts():
    c = np.zeros((128, C_W), np.float32)
    c[:, C_ID:C_ID + 128] = np.eye(128)
    i = np.arange(128)
    c[:, C_TRIU:C_TRIU + 128] = (i[:, None] <= i[None, :])
    c[:, C_ONE:C_ONE + 128] = 1.0
    c[:, C_TRIM:C_TRIM + 128] = np.where(i[:, None] <= i[None, :], 0.0, NEG)
    q = np.arange(128)
    for kb in range(5):
        kpos = kb * 128 + np.arange(128)
        cq = 8 + q // 64
        ck = kpos // 64
        ok = (ck[:, None] >= cq[None, :] - 8) & (ck[:, None] <= cq[None, :])
        c[:, C_BMASK + kb * 128:C_BMASK + (kb + 1) * 128] = np.where(ok, 0.0, NEG)
    for kb in range(2):
        kpos = kb * 128 + np.arange(128)
        cq = 2 + q // 64
        ck = kpos // 64
        ok = (ck[:, None] >= cq[None, :] - 2) & (ck[:, None] <= cq[None, :])
        c[:, C_SMASK + kb * 128:C_SMASK + (kb + 1) * 128] = np.where(ok, 0.0, NEG)
    j = np.arange(16)
    c[0:16, C_TRI16:C_TRI16 + 16] = np.where(j[:, None] <= j[None, :], 0.0, NEG)
    return c


def t5_bucket_np(rel_mem):
    nb = 16
    max_exact = 8
    n = np.abs(rel_mem)
    large = max_exact + (np.log(np.maximum(n, 1).astype(np.float32) / max_exact)
                         / np.float32(np.log(128 / max_exact)) * (nb - max_exact)).astype(np.int32)
    large = np.minimum(large, nb - 1)
    return np.where(rel_mem > 0, nb, 0) + np.where(n < max_exact, n, large)


def host_bias_tables(rel_bias_b, t5_bias):
    tb = rel_bias_b[0]
    q = np.arange(128)
    bb = np.zeros((128, 5, 8, 128), np.float32)
    for kb in range(5):
        k = kb * 128 + np.arange(128)
        rel = (512 + q)[None, :] - k[:, None]
        idx = np.clip(rel, -128, 128) + 128
        bb[:, kb] = np.transpose(tb[idx], (0, 2, 1))
    bs = np.zeros((128, 5, 8, 16), np.float32)
    t = np.arange(16)
    for kb in range(5):
        j = kb * 128 + np.arange(128)
        rel = t[None, :] - j[:, None] + 512
        idx = np.clip(rel, -128, 128) + 128
        bs[:, kb] = np.transpose(tb[idx], (0, 2, 1))
    ts = np.zeros((128, 2, 16, 128), np.float32)
    for kb in range(2):
        k = kb * 128 + np.arange(128)
        rel = (128 + q)[None, :] - k[:, None]
        idx = t5_bucket_np(-rel)
        ts[:, kb] = np.transpose(t5_bias[idx], (0, 2, 1))
    tss = np.zeros((128, 2, 16, 16), np.float32)
    for kb in range(2):
        j = kb * 128 + np.arange(128)
        rel = t[None, :] - j[:, None] + 128
        idx = t5_bucket_np(-rel)
        tss[:, kb] = np.transpose(t5_bias[idx], (0, 2, 1))
    return bb, bs, ts, tss


class Builder:
    def __init__(self, phases):
        self.phases = phases
        self.nc = bass.Bass("TRN2", target_bir_lowering=False)
        self.S = Sched(self.nc)
        self.A = Arena(self.nc, 204000)
        self.ps = [Buf(f"ps{i}", self.nc.alloc_psum_tensor(f"ps{i}", [128, 512], F32)[:]) for i in range(8)]
        self.declare_io()
        self.setup_consts()
        if 1 in phases:
            self.phase1()
        if 2 in phases:
            self.phase2a()
            self.phase2b()
            self.phase2c()
        if 3 in phases:
            self.phase_outproj(0)
            self.phase_ffn(0)
        if 4 in phases:
            self.phase4()
        if 5 in phases:
            self.phase5()
            self.phase5s()
        if 6 in phases:
            self.phase_outproj(1)
            self.phase_ffn(1)
        self.S.emit()

    def declare_io(self):
        S = self.S
        i_ = lambda n, s: S.dram(n, s, F32, kind="ExternalInput")
        o_ = lambda n, s: S.dram(n, s, F32, kind="ExternalOutput")
        self.xc = i_("xc", [8192, D])
        self.xs = i_("xs", [SR, D])
        self.valid = i_("valid", [128, NCTX])
        self.cst = i_("cst", [128, C_W])
        self.cak = i_("cak", [4, 4096, 512])
        self.cav = i_("cav", [4, 4096, 512])
        self.calf = i_("calf", [4, 4096, 8])
        self.cbk = i_("cbk", [4, 512, 512])
        self.cbv = i_("cbv", [4, 512, 512])
        self.cck = i_("cck", [4, 128, 128])
        self.ccv = i_("ccv", [4, 128, 128])
        self.norm_mix = i_("norm_mix", [2, D])
        self.norm_ffn = i_("norm_ffn", [2, D])
        self.norm_final = i_("norm_final", [1, D])
        self.w_in_even = i_("w_in_even", [D, 3080])
        self.b_forget = i_("b_forget", [1, 8])
        self.w_out = [i_("w_out_even", [D, D]), i_("w_out_odd", [D, D])]
        self.w_in_odd = i_("w_in_odd", [D, 1280])
        self.sinks = i_("sinks_c", [1, 16])
        self.w_ffn_in = i_("w_ffn_in", [2, D, 2 * DFF])
        self.w_ffn_out = i_("w_ffn_out", [2, DFF, D])
        self.bb = i_("bb", [128, 5 * 8 * 128])
        self.bs = i_("bs", [128, 5 * 8 * 16])
        self.ts = i_("ts", [128, 2 * 16 * 128])
        self.tss = i_("tss", [128, 2 * 16 * 16])
        self.o_y = o_("o_y", [4096, D])
        self.o_ys = o_("o_ys", [SR, D])
        self.o_ak = o_("o_ak", [4096, 512])
        self.o_av = o_("o_av", [4096, 512])
        self.o_alf = o_("o_alf", [4096, 8])
        self.o_bk = o_("o_bk", [512, 512])
        self.o_bv = o_("o_bv", [512, 512])
        self.o_ck = o_("o_ck", [128, 128])
        self.o_cv = o_("o_cv", [128, 128])
        self.o_sak = o_("o_sak", [SR, 512])
        self.o_sav = o_("o_sav", [SR, 512])
        self.o_salf = o_("o_salf", [SR, 8])
        self.o_sbk = o_("o_sbk", [SR, 512])
        self.o_sbv = o_("o_sbv", [SR, 512])
        self.o_sck = o_("o_sck", [SR, 128])
        self.o_scv = o_("o_scv", [SR, 128])
        d_ = lambda n, s, dt=BF16: S.dram(n, s, dt)
        self.KAT = d_("KAT", [8, 64, 8192])
        self.CQ = d_("CQ", [8, TLOC], F32)
        self.VA = d_("VA", [NCTX, 128, 528])
        self.QAT = d_("QAT", [8, 64, TLOC])
        self.KBT = d_("KBT", [4, 128, NBT * 128])
        self.VB = d_("VB", [NBT, 128, 528])
        self.QBT = d_("QBT", [4, 128, TLOC])
        self.sQAT = d_("sQAT", [8, 64, SR])
        self.sKAT = d_("sKAT", [8, 64, SR])
        self.sVA = d_("sVA", [SR, 528])
        self.sQBT = d_("sQBT", [4, 128, SR])
        self.sKBT = d_("sKBT", [4, 128, SR])
        self.sVB = d_("sVB", [SR, 528])
        self.sLF = d_("sLF", [SR, 8], F32)
        self.AOT = d_("AOT", [8, 128, TLOC])
        self.sAOT = d_("sAOT", [8, 128, SR])
        self.X1 = d_("X1", [TLOC + SR, D], F32)
        self.X2 = d_("X2", [TLOC + SR, D], F32)
        self.QCT = d_("QCT", [8, 128, TLOC])
        self.KCT = d_("KCT", [128, TLOC])
        self.VC = d_("VC", [NLOC, 128, 132])
        self.sQCT = d_("sQCT", [8, 128, SR])
        self.sKCT = d_("sKCT", [128, SR])
        self.sVC = d_("sVC", [SR, 132])

    def setup_consts(self):
        S, A = self.S, self.A
        self.cf = A.alloc("cf", [128, C_W], F32)
        self.identb = A.alloc("identb", [128, 128], BF16)
        self.trimb = A.alloc("trimb", [128, 128], BF16)
        self.sel = A.alloc("sel", [128, 64], F32)
        self.validt = A.alloc("validt", [128, NCTX], F32)
        self.CALL = A.alloc("CALL", [128, NCTX, 8], F32)
        self.CARRY = A.alloc("CARRY", [128, NCTX + 1, 8], F32)
        self.epsb = A.alloc("epsb", [128, 1], F32)
        self.tiny = A.alloc("tiny", [128, 1], F32)
        self.zb = A.alloc("zb", [128, 512], BF16)
        A.mark_persistent()
        S.op("dve", "memset", writes=[self.zb], ap=self.zb[:], constant=0.0)
        S.op(LOADQ, "dma_start", reads=[self.cst], writes=[self.cf], dma="cf", out=self.cf[:], in_=self.cst[:])
        S.op(LOADQ, "dma_start", reads=[self.valid], writes=[self.validt], dma="validt", out=self.validt[:], in_=self.valid[:])
        S.op("dve", "tensor_copy", reads=[self.cf], writes=[self.identb], out=self.identb[:], in_=self.cf[:, C_ID:C_ID + 128])
        S.op("dve", "tensor_copy", reads=[self.cf], writes=[self.trimb], out=self.trimb[:], in_=self.cf[:, C_TRIM:C_TRIM + 128])
        S.op("dve", "memset", writes=[self.sel], ap=self.sel[:], constant=0.0)
        S.op("dve", "memset", writes=[self.sel], ap=self.sel[64:65, :], constant=1.0)
        S.op("dve", "memset", writes=[self.epsb], ap=self.epsb[:], constant=1e-6)
        S.op("dve", "memset", writes=[self.CARRY], ap=self.CARRY[:, 0, :], constant=0.0)

    def load_w_cast(self, dst, dst_ap, src_ap, src_buf, name):
        self.S.op("pool", "dma_start", reads=[src_buf], writes=[dst], dma=name, out=dst_ap, in_=src_ap)

    def norm_T(self, xt, R, gt, hbr, hT_dst_ap, hT_buf, psT, k, junk, ss, evac_eng="act", defer_T=False):
        S = self.S
        ssb = ss[k % len(ss)]
        hb = hbr[k % len(hbr)]

        def st_a():
            S.op("dve", "scalar_tensor_tensor", reads=[xt], writes=[junk, ssb], out=junk[0:R, :], in0=xt[0:R, :], scalar=1.0,
                 in1=xt[0:R, :], op0=ALU.mult, op1=ALU.mult, accum_out=ssb[0:R, :])

        def st_b():
            S.op("act", "activation", reads=[ssb, self.epsb], writes=[ssb], out=ssb[0:R, :], in_=ssb[0:R, :], func=AF.Ln,
                 bias=self.epsb[0:R, :], scale=1.0 / D)
            S.op("act", "activation", reads=[ssb], writes=[ssb], out=ssb[0:R, :], in_=ssb[0:R, :], func=AF.Exp, scale=-0.5)

        def st_c():
            S.op("dve", "scalar_tensor_tensor", reads=[xt, ssb, gt], writes=[hb], out=hb[0:R, :], in0=xt[0:R, :],
                 scalar=ssb[0:R, :], in1=gt[0:R, :], op0=ALU.mult, op1=ALU.mult)

        def st_d():
            self.norm_T_b(hb, R, hT_dst_ap, hT_buf, psT, evac_eng)
        if defer_T:
            return [st_a, st_b, st_c, st_d]
        st_a()
        st_b()
        st_c()
        st_d()
        return ssb

    def norm_T_b(self, hb, R, hT_dst_ap, hT_buf, psT, evac_eng):
        S = self.S
        pv = psT[:].bitcast(BF16).rearrange("p (c r) -> p c r", c=8)
        for c in range(8):
            S.op("pe", "transpose", reads=[hb, self.identb], writes=[psT], out=pv[:, c, 0:R],
                 in_=hb[0:R, c * 128:(c + 1) * 128], identity=self.identb[0:R, 0:R])
        if evac_eng == "act":
            S.op("act", "copy", reads=[psT], writes=[hT_buf], out=hT_dst_ap, in_=pv[:, :, 0:R])
        else:
            S.op("dve", "tensor_copy", reads=[psT], writes=[hT_buf], out=hT_dst_ap, in_=pv[:, :, 0:R])

    def finalize(self, psO, ncols, dst_ap_fn, k, fin, sink_cols=None, rec_eng="act"):
        S = self.S
        ot = fin["ot"][k % 2]
        rec = fin["rec"][k % 2]
        ao = fin["ao"][k % 2]
        psD = fin["psD"]
        S.op("dve", "tensor_copy", reads=[psO], writes=[ot], out=ot[0:65, 0:ncols], in_=psO[0:65, 0:ncols])
        S.op("pe", "matmul", reads=[self.sel, ot], writes=[psD], out=psD[0:64, 0:ncols], lhsT=self.sel[0:65, :],
             rhs=ot[0:65, 0:ncols], start=True, stop=True)
        if sink_cols is not None:
            es, blocks = sink_cols
            for (c0, n, h) in blocks:
                S.op("dve", "tensor_scalar", reads=[psD, es], writes=[rec], out=rec[0:64, c0:c0 + n], in0=psD[0:64, c0:c0 + n],
                     scalar1=es[0:64, h:h + 1], scalar2=1e-30, op0=ALU.add, op1=ALU.max)
        else:
            S.op("dve", "tensor_scalar", reads=[psD], writes=[rec], out=rec[0:64, 0:ncols], in0=psD[0:64, 0:ncols],
                 scalar1=1e-30, scalar2=None, op0=ALU.max)
        if rec_eng == "dve":
            S.op("dve", "reciprocal", reads=[rec], writes=[rec], out=rec[0:64, 0:ncols], in_=rec[0:64, 0:ncols])
        else:
            S.op("act", "activation", reads=[rec], writes=[rec], out=rec[0:64, 0:ncols], in_=rec[0:64, 0:ncols], func=AF.Ln)
            S.op("act", "activation", reads=[rec], writes=[rec], out=rec[0:64, 0:ncols], in_=rec[0:64, 0:ncols], func=AF.Exp, scale=-1.0)
        S.op("dve", "tensor_tensor", reads=[ot, rec], writes=[ao], out=ao[0:64, 0:ncols], in0=ot[0:64, 0:ncols],
             in1=rec[0:64, 0:ncols], op=ALU.mult)
        dst_ap_fn(ao)

    def alloc_fin(self, psD):
        A = self.A
        return dict(ot=A.ring("ot", 2, [128, 512], F32), rec=A.ring("rec", 2, [64, 512], F32),
                    ao=A.ring("ao", 2, [64, 512], BF16), psD=psD)

    def phase1(self):
        S, A, ps = self.S, self.A, self.ps
        S.barrier()
        A.reset()
        W0 = A.alloc("W0", [128, 8, 3080], BF16)
        for c in range(8):
            self.load_w_cast(W0, W0[:, c, :], self.w_in_even[c * 128:(c + 1) * 128, :], self.w_in_even, "w0")
        g0 = A.alloc("g0", [128, D], F32)
        S.op(LOADQ, "dma_start", reads=[self.norm_mix], writes=[g0], dma="g0", out=g0[:], in_=self.norm_mix[0, :].partition_broadcast(128))
        bf = A.alloc("bf", [128, 8], F32)
        S.op(LOADQ, "dma_start", reads=[self.b_forget], writes=[bf], dma="bf", out=bf[:], in_=self.b_forget[0, :].partition_broadcast(128))
        ones8 = A.alloc("ones8", [128, 8], F32)
        S.op("dve", "memset", writes=[ones8], ap=ones8[:], constant=1.0)
        xt = A.ring("xt", 3, [128, D], F32)
        junk = A.alloc("junk", [128, D], BF16)
        ss = A.ring("ss", 2, [128, 1], F32)
        hbr = A.ring("hb", 2, [128, D], BF16)
        hT = A.ring("hT", 2, [128, 8, 128], BF16)
        f32o = {n: A.ring(n, 2, [128, 512], F32) for n in ("kaf", "vaf", "kbf", "vbf")}
        b16 = {n: A.ring(n, 2, [128, 512], BF16) for n in ("qab", "kab", "qbb", "kbb")}
        vaug = {n: A.ring(n, 2, [128, 8, 66], BF16) for n in ("vaa", "vba")}
        for n in vaug:
            for vb_ in vaug[n]:
                S.op("dve", "memset", writes=[vb_], ap=vb_[:], constant=0.0)
        tst = {n: A.ring(n, 2, [128, 4, 128], BF16) for n in ("qbT", "kbT")}
        tsth = {n: A.ring(n, 2, [64, 8, 128], BF16) for n in ("qaT", "kaT")}
        cqT = A.ring("cqT", 2, [8, 128], F32)
        t8 = A.ring("t8", 2, [128, 8], F32)
        lf = A.ring("lf", 2, [128, 8], F32)
        ctile = A.ring("ctile", 2, [128, 8], F32)
        psT = ps[0]
        psG = [ps[1], ps[2], ps[3]]
        psQ = [ps[4], ps[5]]
        psC = ps[6]
        gcount = [0]
        qcount = [0]

        deferred = []

        def flush_deferred():
            while deferred:
                deferred.pop(0)()

        def group(hTb, R, c0, ncol):
            pg = psG[gcount[0] % 3]
            gcount[0] += 1
            for c in range(8):
                S.op("pe", "matmul", reads=[hTb, W0], writes=[pg], out=pg[0:R, 0:ncol], lhsT=hTb[:, c, 0:R],
                     rhs=W0[:, c, c0:c0 + ncol], start=(c == 0), stop=(c == 7))
            flush_deferred()
            return pg

        def transposes(src, R, k, name, dst_buf, dst_ap, evac):
            pq = psQ[qcount[0] % 2]
            qcount[0] += 1
            pv = pq[:].bitcast(BF16)[:, 0:512].rearrange("p (c r) -> p c r", c=4)
            st = tst[name][k % 2]
            for c in range(4):
                S.op("pe", "transpose", reads=[src, self.identb], writes=[pq], out=pv[:, c, 0:R],
                     in_=src[0:R, c * 128:(c + 1) * 128], identity=self.identb[0:R, 0:R])
            if evac == "act":
                S.op("act", "copy", reads=[pq], writes=[st], out=st[:, :, 0:R], in_=pv[:, :, 0:R])
            else:
                S.op("dve", "tensor_copy", reads=[pq], writes=[st], out=st[:, :, 0:R], in_=pv[:, :, 0:R])
            S.op(STOREQ, "dma_start", reads=[st], writes=[dst_buf], dma=name + f"_st{k % 2}", out=dst_ap, in_=st[:, :, 0:R])

        def transposes_h(src, R, k, name, dst_buf, dst_ap):
            pq = psQ[qcount[0] % 2]
            qcount[0] += 1
            pv = pq[:].bitcast(BF16)[:, 0:512].rearrange("p (c r) -> p c r", c=4)
            st = tsth[name][k % 2]
            stv = st[:, :, :].rearrange("d (c s) r -> d c s r", s=2)
            for c in range(4):
                S.op("pe", "transpose", reads=[src, self.identb], writes=[pq], out=pv[:, c, 0:R],
                     in_=src[0:R, c * 128:(c + 1) * 128], identity=self.identb[0:R, 0:R])
            S.op("act", "copy", reads=[pq], writes=[st], out=stv[:, :, 0, 0:R], in_=pv[0:64, :, 0:R])
            S.op("dve", "tensor_copy", reads=[pq], writes=[st], out=stv[:, :, 1, 0:R], in_=pv[64:128, :, 0:R])
            S.op(STOREQ, "dma_start", reads=[st], writes=[dst_buf], dma=name + f"_st{k % 2}", out=dst_ap, in_=st[:, :, 0:R])

        def prep_tile(k, R, xsrc_buf, xsrc_ap):
            x = xt[k % 3]
            S.op(LOADQ, "dma_start", reads=[xsrc_buf], writes=[x], dma=f"xt{k % 3}", out=x[0:R, :], in_=xsrc_ap)
            hTb = hT[k % 2]
            return self.norm_T(x, R, g0, hbr, hTb[:, :, 0:R], hTb, psT, k, junk, ss, evac_eng="act", defer_T=True)

        def do_tile(k, R, t_ctx, do_q, do_band, outs, vcol, dst, hook, hook0=None):
            hTb = hT[k % 2]
            if hook0 is not None:
                hook0()
            if do_q:
                pg = group(hTb, R, 0, 512)
                qab = b16["qab"][k % 2]
                S.op("act", "activation", reads=[pg], writes=[qab], out=qab[0:R, :], in_=pg[0:R, :], func=AF.Copy, scale=0.125)
                deferred.append(lambda qab=qab: transposes_h(qab, R, k, "qaT", dst["QAT"][0], dst["QAT"][1]))
            pg = group(hTb, R, 512, 512)
            kaf = f32o["kaf"][k % 2]
            S.op("dve", "tensor_copy", reads=[pg], writes=[kaf], out=kaf[0:R, :], in_=pg[0:R, :])
            if outs.get("ak") is not None:
                S.op(STOREQ, "dma_start", reads=[kaf], writes=[outs["ak"][0]], dma=f"kaf_st{k % 2}", out=outs["ak"][1], in_=kaf[0:R, :])
            kab = b16["kab"][k % 2]
            S.op("act", "copy", reads=[kaf], writes=[kab], out=kab[0:R, :], in_=kaf[0:R, :])
            deferred.append(lambda kab=kab: transposes_h(kab, R, k, "kaT", dst["KAT"][0], dst["KAT"][1]))
            pg = group(hTb, R, 1024, 512)
            vaf = f32o["vaf"][k % 2]
            S.op("act", "copy", reads=[pg], writes=[vaf], out=vaf[0:R, :], in_=pg[0:R, :])
            if outs.get("av") is not None:
                S.op(STOREQ, "dma_start", reads=[vaf], writes=[outs["av"][0]], dma=f"vaf_st{k % 2}", out=outs["av"][1], in_=vaf[0:R, :])
            vaa = vaug["vaa"][k % 2]
            S.op("act", "copy", reads=[pg], writes=[vaa], out=vaa[0:R, :, 0:64], in_=pg[0:R, :].rearrange("p (h d) -> p h d", h=8))
            S.op("dve", "tensor_scalar", reads=[ones8, self.validt], writes=[vaa], out=vaa[0:R, :, 64], in0=ones8[0:R, :],
                 scalar1=vcol, scalar2=None, op0=ALU.mult)
            S.op(STOREQ, "dma_start", reads=[vaa], writes=[dst["VA"][0]], dma=f"vaa_st{k % 2}", out=dst["VA"][1],
                 in_=vaa[0:R, :, :].rearrange("p h d -> p (h d)"))
            if hook is not None:
                hook()
            if do_band:
                if do_q:
                    pg = group(hTb, R, 1536, 512)
                    qbb = b16["qbb"][k % 2]
                    S.op("dve", "tensor_scalar", reads=[pg], writes=[qbb], out=qbb[0:R, :], in0=pg[0:R, :], scalar1=0.125,
                         scalar2=None, op0=ALU.mult)
                    deferred.append(lambda qbb=qbb: transposes(qbb, R, k, "qbT", dst["QBT"][0], dst["QBT"][1], "dve"))
                pg = group(hTb, R, 2048, 512)
                kbf = f32o["kbf"][k % 2]
                S.op("act", "copy", reads=[pg], writes=[kbf], out=kbf[0:R, :], in_=pg[0:R, :])
                if outs.get("bk") is not None:
                    S.op(STOREQ, "dma_start", reads=[kbf], writes=[outs["bk"][0]], dma=f"kbf_st{k % 2}", out=outs["bk"][1], in_=kbf[0:R, :])
                kbb = b16["kbb"][k % 2]
                S.op("dve", "tensor_copy", reads=[kbf], writes=[kbb], out=kbb[0:R, :], in_=kbf[0:R, :])
                deferred.append(lambda kbb=kbb: transposes(kbb, R, k, "kbT", dst["KBT"][0], dst["KBT"][1], "act"))
                pg = group(hTb, R, 2560, 512)
                vbf = f32o["vbf"][k % 2]
                S.op("dve", "tensor_copy", reads=[pg], writes=[vbf], out=vbf[0:R, :], in_=pg[0:R, :])
                if outs.get("bv") is not None:
                    S.op(STOREQ, "dma_start", reads=[vbf], writes=[outs["bv"][0]], dma=f"vbf_st{k % 2}", out=outs["bv"][1], in_=vbf[0:R, :])
                vba = vaug["vba"][k % 2]
                S.op("act", "copy", reads=[vbf], writes=[vba], out=vba[0:R, :, 0:64], in_=vbf[0:R, :].rearrange("p (h d) -> p h d", h=8))
                S.op("dve", "tensor_scalar", reads=[ones8, self.validt], writes=[vba], out=vba[0:R, :, 64], in0=ones8[0:R, :],
                     scalar1=vcol, scalar2=None, op0=ALU.mult)
                S.op(STOREQ, "dma_start", reads=[vba], writes=[dst["VB"][0]], dma=f"vba_st{k % 2}", out=dst["VB"][1],
                     in_=vba[0:R, :, :].rearrange("p h d -> p (h d)"))
            pg = group(hTb, R, 3072, 8)
            t8b = t8[k % 2]
            lfb = lf[k % 2]
            S.op("dve", "tensor_tensor", reads=[pg, bf], writes=[t8b], out=t8b[0:R, :], in0=pg[0:R, 0:8], in1=bf[0:R, :], op=ALU.add)
            S.op("act", "activation", reads=[t8b], writes=[t8b], out=t8b[0:R, :], in_=t8b[0:R, :], func=AF.Exp, scale=-1.0)
            S.op("act", "activation", reads=[t8b], writes=[t8b], out=t8b[0:R, :], in_=t8b[0:R, :], func=AF.Ln, bias=1.0, scale=1.0)
            S.op("dve", "tensor_scalar", reads=[t8b], writes=[lfb], out=lfb[0:R, :], in0=t8b[0:R, :], scalar1=-1.0, scalar2=None, op0=ALU.mult)
            flush_deferred()
            if outs.get("alf") is not None:
                S.op(STOREQ, "dma_start", reads=[lfb], writes=[outs["alf"][0]], dma=f"lf_st{k % 2}", out=outs["alf"][1], in_=lfb[0:R, :])
            if t_ctx is not None:
                t = t_ctx
                S.op("pe", "matmul", reads=[self.cf, lfb], writes=[psC], out=psC[:, 0:8], lhsT=self.cf[:, C_TRIU:C_TRIU + 128],
                     rhs=lfb[:, :], start=True, stop=True)
                S.op("pe", "matmul", reads=[self.cf, lfb], writes=[psC], out=psC[:, 8:16], lhsT=self.cf[:, C_ONE:C_ONE + 128],
                     rhs=lfb[:, :], start=True, stop=True)
                S.op("dve", "tensor_tensor", reads=[psC, self.CARRY], writes=[self.CALL], out=self.CALL[:, t, :], in0=psC[:, 0:8],
                     in1=self.CARRY[:, t, :], op=ALU.add)
                S.op("dve", "tensor_tensor", reads=[psC, self.CARRY], writes=[self.CARRY], out=self.CARRY[:, t + 1, :], in0=psC[:, 8:16],
                     in1=self.CARRY[:, t, :], op=ALU.add)
                if t >= T0:
                    cq = cqT[k % 2]
                    S.op("pe", "transpose", reads=[self.CALL, self.cf], writes=[psC], out=psC[0:8, 128:256], in_=self.CALL[:, t, :],
                         identity=self.cf[:, C_ID:C_ID + 128])
                    S.op("dve", "tensor_copy", reads=[psC], writes=[cq], out=cq[:, :], in_=psC[0:8, 128:256])
                    S.op(STOREQ, "dma_start", reads=[cq], writes=[self.CQ], dma=f"cq_st{k % 2}", out=self.CQ[:, (t - T0) * 128:(t - T0 + 1) * 128], in_=cq[:, :])

        tiles = []
        for t in range(NCTX):
            loc = t - T0
            outs = {}
            if loc >= 1:
                r0 = (loc - 1) * 128
                outs["ak"] = (self.o_ak, self.o_ak[r0:r0 + 128, :])
                outs["av"] = (self.o_av, self.o_av[r0:r0 + 128, :])
                outs["alf"] = (self.o_alf, self.o_alf[r0:r0 + 128, :])
            if t >= NCTX - 4:
                r0 = (t - (NCTX - 4)) * 128
                outs["bk"] = (self.o_bk, self.o_bk[r0:r0 + 128, :])
                outs["bv"] = (self.o_bv, self.o_bv[r0:r0 + 128, :])
            dst = {"KAT": (self.KAT, self.KAT[:, :, t * 128:(t + 1) * 128].rearrange("h d r -> d h r")),
                   "VA": (self.VA, self.VA[t, :, :])}
            if t >= BT0:
                j = t - BT0
                dst["KBT"] = (self.KBT, self.KBT[:, :, j * 128:(j + 1) * 128].rearrange("c p r -> p c r"))
                dst["VB"] = (self.VB, self.VB[j, :, :])
            if loc >= 0:
                dst["QAT"] = (self.QAT, self.QAT[:, :, loc * 128:(loc + 1) * 128].rearrange("h d r -> d h r"))
                dst["QBT"] = (self.QBT, self.QBT[:, :, loc * 128:(loc + 1) * 128].rearrange("c p r -> p c r"))
            tiles.append(dict(R=128, xb=self.xc, xap=self.xc[t * 128:(t + 1) * 128, :], t=t, do_q=(loc >= 0), do_band=(t >= BT0),
                              outs=outs, vcol=self.validt[:, t:t + 1], dst=dst))
        outs = {"ak": (self.o_sak, self.o_sak[:, :]), "av": (self.o_sav, self.o_sav[:, :]), "alf": (self.o_salf, self.o_salf[:, :]),
                "bk": (self.o_sbk, self.o_sbk[:, :]), "bv": (self.o_sbv, self.o_sbv[:, :])}
        dst = {"KAT": (self.sKAT, self.sKAT[:, :, :].rearrange("h d r -> d h r")), "VA": (self.sVA, self.sVA[:, :]),
               "KBT": (self.sKBT, self.sKBT[:, :, :].rearrange("c p r -> p c r")), "VB": (self.sVB, self.sVB[:, :]),
               "QAT": (self.sQAT, self.sQAT[:, :, :].rearrange("h d r -> d h r")),
               "QBT": (self.sQBT, self.sQBT[:, :, :].rearrange("c p r -> p c r"))}
        tiles.append(dict(R=SR, xb=self.xs, xap=self.xs[:, :], t=None, do_q=True, do_band=True, outs=outs,
                          vcol=self.cf[0:SR, C_ONE:C_ONE + 1], dst=dst))
        for f in prep_tile(0, tiles[0]["R"], tiles[0]["xb"], tiles[0]["xap"]):
            f()
        for k, td in enumerate(tiles):
            hook = None
            hook0 = None
            if k + 1 < len(tiles):
                nt = tiles[k + 1]
                cell = {}

                def hook0(k=k, nt=nt, cell=cell):
                    cell["tb"] = prep_tile(k + 1, nt["R"], nt["xb"], nt["xap"])
                    cell["tb"][0]()

                def hook(cell=cell):
                    cell["tb"][1]()
                    cell["tb"][2]()
                    deferred.append(cell["tb"][3])
            do_tile(k, td["R"], td["t"], td["do_q"], td["do_band"], td["outs"], td["vcol"], td["dst"], hook, hook0)
        k = len(tiles) - 1
        lfb = lf[k % 2]
        S.op(STOREQ, "dma_start", reads=[lfb], writes=[self.sLF], dma=f"lf_st{k % 2}", out=self.sLF[:, :], in_=lfb[0:SR, :])


    def run_steps(self, steps, delay=0, ahead=1):
        if not steps:
            return
        for j in range(min(ahead, len(steps))):
            if steps[j].get("pre"):
                steps[j]["pre"]()
            steps[j]["qk"]()
        pending = []
        for i, st in enumerate(steps):
            if i + ahead < len(steps):
                if steps[i + ahead].get("pre"):
                    steps[i + ahead]["pre"]()
                steps[i + ahead]["qk"]()
            st["ex"]()
            while pending and pending[0][0] <= i:
                pending.pop(0)[1]()
            st["pv"]()
            if st.get("post"):
                if delay == 0:
                    st["post"]()
                else:
                    pending.append((i + delay, st["post"]))
        for _, f in pending:
            f()

    def phase2a(self):
        S, A, ps = self.S, self.A, self.ps
        S.barrier()
        A.reset()
        KT = A.ring("KT", 2, [128, 8192], BF16)
        VAp = A.ring("VAp", 2, [128, NCTX, 256], BF16)
        for b in VAp:
            S.op("dve", "memset", writes=[b], ap=b[:], constant=0.0)
        Qh = A.ring("Qh", 2, [128, 512], BF16)
        rrow = A.ring("rrow", 2, [65, 512], F32)
        pT = A.ring("pT", 4, [128, 512], BF16)
        biasT = A.ring("biasT", 2, [128, NCTX], F32)
        fin = self.alloc_fin(ps[4])
        psS = [ps[0], ps[1], ps[5], ps[6]]
        psO = [ps[2], ps[3]]
        for b in KT:
            S.op("dve", "memset", writes=[b], ap=b[64:128, :], constant=0.0)
            S.op("dve", "memset", writes=[b], ap=b[64:65, :], constant=1.0)
        for b in Qh:
            S.op("dve", "memset", writes=[b], ap=b[64:128, :], constant=0.0)
        qtiles = [(0, 128)] + [(128 + 512 * j, 512) for j in range(8)]
        steps = []
        cnt = {"q": 0, "h": 0, "s": 0}
        for p in range(4):
            va = VAp[p % 2]

            def pre_pair(p=p, va=va):
                for hh in range(4):
                    for s2 in range(2):
                        S.op(LOADQ, "dma_start", reads=[self.VA], writes=[va], dma=f"va{p % 2}", out=va[:, hh * 16:(hh + 1) * 16, s2 * 128:s2 * 128 + 66],
                             in_=self.VA[hh * 16:(hh + 1) * 16, :, p * 132 + s2 * 66:p * 132 + s2 * 66 + 66].rearrange("t k c -> k t c"))
            for s in range(2):
                h = 2 * p + s
                kt = KT[h % 2]

                def pre_head(h=h, kt=kt):
                    S.op(LOADQ, "dma_start", reads=[self.KAT], writes=[kt], dma=f"kt{h % 2}", out=kt[0:64, :], in_=self.KAT[h, :, :])
                first_h = True
                for (col0, N) in qtiles:
                    hk = cnt["h"]
                    cnt["h"] += 1
                    Q = Qh[hk % 2]
                    rr = rrow[hk % 2]
                    bt = biasT[hk % 2]
                    po = psO[hk % 2]
                    kb0 = T0 + col0 // 128
                    nkb = kb0 + N // 128

                    def pre_q(h=h, col0=col0, N=N, Q=Q, rr=rr, bt=bt, nkb=nkb, hk=hk):
                        S.op(LOADQ, "dma_start", reads=[self.QAT], writes=[Q], dma=f"qh{hk % 2}", out=Q[0:64, 0:N], in_=self.QAT[h, :, col0:col0 + N])
                        S.op(LOADQ, "dma_start", reads=[self.CQ], writes=[rr], dma=f"rr{hk % 2}", out=rr[64:65, 0:N], in_=self.CQ[h:h + 1, col0:col0 + N])
                        S.op("dve", "tensor_scalar", reads=[rr, self.CARRY], writes=[Q], out=Q[64:65, 0:N], in0=rr[64:65, 0:N],
                             scalar1=self.CARRY[64:65, nkb, h:h + 1], scalar2=None, op0=ALU.subtract)
                        S.op("dve", "tensor_scalar", reads=[self.CALL, self.CARRY], writes=[bt], out=bt[:, 0:nkb],
                             in0=self.CALL[:, 0:nkb, h], scalar1=self.CARRY[:, nkb, h:h + 1], scalar2=-1.0,
                             op0=ALU.subtract, op1=ALU.mult)
                    for kb in range(nkb):
                        sk = cnt["s"]
                        cnt["s"] += 1
                        pS = psS[sk % 4]
                        pt = pT[sk % 4]
                        c0 = max(0, kb - kb0) * 128
                        pres = []
                        if kb == 0:
                            if first_h:
                                if s == 0:
                                    pres.append(pre_pair)
                                pres.append(pre_head)
                                first_h = False
                            pres.append(pre_q)

                        def qk(kb=kb, kb0=kb0, c0=c0, N=N, pS=pS, kt=kt, Q=Q):
                            lh = kt[:, kb * 128:(kb + 1) * 128]
                            if kb < kb0:
                                S.op("pe", "matmul", reads=[kt, Q], writes=[pS], out=pS[:, 0:N], lhsT=lh, rhs=Q[:, 0:N], start=True, stop=True)
                            else:
                                S.op("pe", "matmul", reads=[kt, Q], writes=[pS], out=pS[:, c0:c0 + 128], lhsT=lh, rhs=Q[:, c0:c0 + 128],
                                     start=True, stop=False)
                                S.op("pe", "matmul", reads=[self.identb, self.trimb], writes=[pS], out=pS[:, c0:c0 + 128], lhsT=self.identb[:, :],
                                     rhs=self.trimb[:, :], start=False, stop=True)
                                if c0 + 128 < N:
                                    S.op("pe", "matmul", reads=[kt, Q], writes=[pS], out=pS[:, c0 + 128:N], lhsT=lh, rhs=Q[:, c0 + 128:N],
                                         start=True, stop=True)

                        def ex(kb=kb, c0=c0, N=N, pS=pS, pt=pt, bt=bt):
                            S.op("act", "activation", reads=[pS, bt], writes=[pt], out=pt[:, c0:N], in_=pS[:, c0:N], func=AF.Exp,
                                 bias=bt[:, kb:kb + 1], scale=1.0)

                        def pv(kb=kb, c0=c0, N=N, pt=pt, po=po, va=va, s=s, nkb=nkb):
                            S.op("pe", "matmul", reads=[va, pt], writes=[po], out=po[:, c0:N], lhsT=va[:, kb, s * 128:(s + 1) * 128],
                                 rhs=pt[:, c0:N], start=(kb == 0), stop=(kb == nkb - 1))
                        st = {"qk": qk, "ex": ex, "pv": pv}
                        if pres:
                            st["pre"] = (lambda pres=pres: [f() for f in pres])
                        if kb == nkb - 1:
                            def post(po=po, N=N, h=h, col0=col0, hk=hk):
                                def dst(ao):
                                    S.op(STOREQ, "dma_start", reads=[ao], writes=[self.AOT], dma=f"ao{hk % 2}", out=self.AOT[h // 2, (h % 2) * 64:(h % 2) * 64 + 64, col0:col0 + N],
                                         in_=ao[0:64, 0:N])
                                self.finalize(po, N, dst, hk, fin, rec_eng="dve")
                            st["post"] = post
                        steps.append(st)
        self.run_steps(steps, delay=4, ahead=3)

    def phase2b(self):
        S, A, ps = self.S, self.A, self.ps
        S.barrier()
        A.reset()
        BBf = A.alloc("BBf", [128, 5, 8, 128], F32)
        BB = A.alloc("BB", [128, 8, 640], BF16)
        S.op(LOADQ, "dma_start", reads=[self.bb], writes=[BBf], dma="bbf", out=BBf[:].rearrange("p a b c -> p (a b c)"), in_=self.bb[:, :])
        for kb in range(5):
            for h in range(8):
                S.op("dve", "tensor_tensor", reads=[BBf, self.cf], writes=[BB], out=BB[:, h, kb * 128:(kb + 1) * 128], in0=BBf[:, kb, h, :],
                     in1=self.cf[:, C_BMASK + kb * 128:C_BMASK + (kb + 1) * 128], op=ALU.add)
        KB = A.ring("KB", 2, [128, NBT * 128], BF16)
        VBp = A.ring("VBp", 2, [128, NBT, 132], BF16)
        Qe = A.ring("Qe", 2, [128, 512], BF16)
        Qo = A.ring("Qo", 2, [128, 512], BF16)
        pTA = A.ring("pTA", 3, [128, 512], BF16)
        pTB = A.ring("pTB", 3, [128, 128], BF16)
        fin = self.alloc_fin(ps[6])
        psSA = [ps[0], ps[1]]
        psSB = [ps[2], ps[3]]
        psO = [ps[4], ps[5]]
        for b in Qe:
            S.op("dve", "memset", writes=[b], ap=b[64:128, :], constant=0.0)
        for b in Qo:
            S.op("dve", "memset", writes=[b], ap=b[0:64, :], constant=0.0)
        quads = [(4 * i, 4) for i in range(8)] + [(32, 1)]
        steps = []
        cnt = {"q": 0, "h": 0, "s": 0}
        for p in range(4):
            kbuf = KB[p % 2]
            vb = VBp[p % 2]

            def pre_pair(p=p, kbuf=kbuf, vb=vb):
                S.op(LOADQ, "dma_start", reads=[self.KBT], writes=[kbuf], dma=f"kb{p % 2}", out=kbuf[:, :], in_=self.KBT[p, :, :])
                for hh in range(2):
                    lo, hi = (0, 19) if hh == 0 else (19, NBT)
                    S.op(LOADQ, "dma_start", reads=[self.VB], writes=[vb], dma=f"vb{p % 2}", out=vb[:, lo:hi, :],
                         in_=self.VB[lo:hi, :, p * 132:(p + 1) * 132].rearrange("t k c -> k t c"))
            first_pair = True
            for (g0, ng) in quads:
                N = 128 * ng
                col0 = g0 * 128
                qk_i = cnt["q"]
                cnt["q"] += 1
                qe = Qe[qk_i % 2]
                qo = Qo[qk_i % 2]

                def pre_q(p=p, col0=col0, N=N, qe=qe, qo=qo, qk_i=qk_i):
                    S.op(LOADQ, "dma_start", reads=[self.QBT], writes=[qe], dma=f"qe{qk_i % 2}", out=qe[0:64, 0:N],
                         in_=self.QBT[p, 0:64, col0:col0 + N])
                    S.op(LOADQ, "dma_start", reads=[self.QBT], writes=[qo], dma=f"qo{qk_i % 2}", out=qo[64:128, 0:N],
                         in_=self.QBT[p, 64:128, col0:col0 + N])
                first_q = True
                for s in range(2):
                    h = 2 * p + s
                    Q = qe if s == 0 else qo
                    hk = cnt["h"]
                    cnt["h"] += 1
                    po = psO[hk % 2]
                    for gi in range(ng):
                        g = g0 + gi
                        sk = cnt["s"]
                        cnt["s"] += 1
                        pA = psSA[sk % 2]
                        pB = psSB[sk % 2]
                        ptA = pTA[sk % 3]
                        ptB = pTB[sk % 3]
                        pres = []
                        if gi == 0 and first_q:
                            if first_pair:
                                pres.append(pre_pair)
                            pres.append(pre_q)
                        first_q = False
                        first_pair = False

                        def qk(g=g, gi=gi, h=h, pA=pA, pB=pB, kbuf=kbuf, Q=Q):
                            S.op("pe", "matmul", reads=[self.identb, BB], writes=[pA], out=pA[:, 0:512], lhsT=self.identb[:, :],
                                 rhs=BB[:, h, 0:512], start=True, stop=False)
                            for b in range(4):
                                S.op("pe", "matmul", reads=[kbuf, Q], writes=[pA], out=pA[:, b * 128:(b + 1) * 128],
                                     lhsT=kbuf[:, (g + b) * 128:(g + b + 1) * 128], rhs=Q[:, gi * 128:(gi + 1) * 128], start=False, stop=(b == 3))
                            S.op("pe", "matmul", reads=[self.identb, BB], writes=[pB], out=pB[:, 0:128], lhsT=self.identb[:, :],
                                 rhs=BB[:, h, 512:640], start=True, stop=False)
                            S.op("pe", "matmul", reads=[kbuf, Q], writes=[pB], out=pB[:, 0:128],
                                 lhsT=kbuf[:, (g + 4) * 128:(g + 5) * 128], rhs=Q[:, gi * 128:(gi + 1) * 128], start=False, stop=True)

                        def ex(pA=pA, pB=pB, ptA=ptA, ptB=ptB):
                            S.op("act", "activation", reads=[pA], writes=[ptA], out=ptA[:, :], in_=pA[:, 0:512], func=AF.Exp)
                            S.op("act", "activation", reads=[pB], writes=[ptB], out=ptB[:, :], in_=pB[:, 0:128], func=AF.Exp)

                        def pv(g=g, gi=gi, s=s, po=po, ptA=ptA, ptB=ptB, vb=vb, N=N, ng=ng):
                            if gi == 0:
                                S.op("pe", "matmul", reads=[self.zb], writes=[po], out=po[0:65, 0:N], lhsT=self.zb[0:1, 0:65],
                                     rhs=self.zb[0:1, 0:N], start=True, stop=False)
                            for b in range(5):
                                rhs = ptA[:, b * 128:(b + 1) * 128] if b < 4 else ptB[:, :]
                                S.op("pe", "matmul", reads=[vb, ptA, ptB], writes=[po], out=po[0:65, gi * 128:(gi + 1) * 128],
                                     lhsT=vb[:, g + b, s * 66:s * 66 + 65], rhs=rhs, start=False, stop=(b == 4 and gi == ng - 1))
                        st = {"qk": qk, "ex": ex, "pv": pv}
                        if pres:
                            st["pre"] = (lambda pres=pres: [f() for f in pres])
                        if gi == ng - 1:
                            def post(po=po, N=N, h=h, col0=col0, hk=hk):
                                def dst(ao):
                                    S.op(STOREQ, "dma_start", reads=[ao], writes=[self.AOT], dma=f"ao{hk % 2}", out=self.AOT[4 + h // 2, (h % 2) * 64:(h % 2) * 64 + 64, col0:col0 + N],
                                         in_=ao[0:64, 0:N])
                                self.finalize(po, N, dst, hk, fin)
                            st["post"] = post
                        steps.append(st)
        self.run_steps(steps, delay=2)

    def phase2c(self):
        S, A, ps = self.S, self.A, self.ps
        S.barrier()
        A.reset()
        kc = A.ring("kc", 2, [128, 8, 512], F32)
        vc = A.ring("vc", 2, [128, 8, 512], F32)
        KTs = A.alloc("KTs", [128, 4, 4112], BF16)
        Vs = A.alloc("Vs", [128, 33, 8, 66], BF16)
        Qbd = A.alloc("Qbd", [128, 4, 32], BF16)
        lfa = A.alloc("lfa", [128, 33, 8], F32)
        CAR = A.alloc("CAR", [128, 34, 8], F32)
        cfull = A.alloc("cfull", [128, 33, 8], F32)
        biasS = A.alloc("biasS", [128, 33, 8], F32)
        biasF = A.alloc("biasF", [128, 33, 8, 16], F32)
        tri16b = A.alloc("tri16b", [16, 32], BF16)
        pT = A.ring("pTs", 2, [128, 512], BF16)
        BSf = A.alloc("BSf", [128, 5, 8, 16], F32)
        BSb = A.alloc("BSb", [128, 5, 8, 16], BF16)
        fin = self.alloc_fin(ps[7])
        psK = [ps[0], ps[1]]
        psS = [ps[2], ps[3]]
        psO = ps[4]
        psC = ps[5]
        psC2 = ps[6]
        S.op("dve", "tensor_copy", reads=[self.cf], writes=[tri16b], out=tri16b[:, 0:16], in_=self.cf[0:16, C_TRI16:C_TRI16 + 16])
        S.op("dve", "tensor_copy", reads=[self.cf], writes=[tri16b], out=tri16b[:, 16:32], in_=self.cf[0:16, C_TRI16:C_TRI16 + 16])
        S.op("dve", "memset", writes=[Vs], ap=Vs[:, :, :, 64], constant=1.0)
        S.op(LOADQ, "dma_start", reads=[self.bs], writes=[BSf], dma="bsf", out=BSf[:].rearrange("p a b c -> p (a b c)"), in_=self.bs[:, :])
        S.op("dve", "tensor_copy", reads=[BSf], writes=[BSb], out=BSb[:], in_=BSf[:])
        kcn = [0]
        ekn = [0]
        skn = [0]

        def load_kv(q, srck, srcv, nblk, blk0, ktcol0, kt_dst, v_dst):
            for ch in range(0, nblk, 8):
                nb = min(8, nblk - ch)
                kb_ = kc[kcn[0] % 2]
                vb_ = vc[kcn[0] % 2]
                S.op(LOADQ, "dma_start", reads=[srck], writes=[kb_], dma=f"kc{kcn[0] % 2}", out=kb_[:, 0:nb, :],
                     in_=srck[q, ch * 128:(ch + nb) * 128, :].rearrange("(b k) c -> k b c", k=128))
                S.op(LOADQ, "dma_start", reads=[srcv], writes=[vb_], dma=f"vc{kcn[0] % 2}", out=vb_[:, 0:nb, :],
                     in_=srcv[q, ch * 128:(ch + nb) * 128, :].rearrange("(b k) c -> k b c", k=128))
                kcn[0] += 1
                for b in range(nb):
                    blk = ch + b
                    pk = psK[ekn[0] % 2]
                    pkv = pk[:].rearrange("p (c r) -> p c r", c=4)
                    for p in range(4):
                        S.op("pe", "transpose", reads=[kb_, self.cf], writes=[pk], out=pkv[:, p, :], in_=kb_[:, b, p * 128:(p + 1) * 128],
                             identity=self.cf[:, C_ID:C_ID + 128])
                    c0 = ktcol0 + blk * 128
                    if ekn[0] % 2 == 0:
                        S.op("act", "copy", reads=[pk], writes=[kt_dst], out=kt_dst[:, :, c0:c0 + 128], in_=pkv[:, :, :])
                    else:
                        S.op("dve", "tensor_copy", reads=[pk], writes=[kt_dst], out=kt_dst[:, :, c0:c0 + 128], in_=pkv[:, :, :])
                    ekn[0] += 1
                    if ekn[0] % 2 == 0:
                        S.op("dve", "tensor_copy", reads=[vb_], writes=[v_dst], out=v_dst[:, blk0 + blk, :, 0:64],
                             in_=vb_[:, b, :].rearrange("p (h d) -> p h d", h=8))
                    else:
                        S.op("act", "copy", reads=[vb_], writes=[v_dst], out=v_dst[:, blk0 + blk, :, 0:64],
                             in_=vb_[:, b, :].rearrange("p (h d) -> p h d", h=8))

        S.op("pe", "matmul", reads=[self.zb], writes=[psO], out=psO[0:65, 0:512], lhsT=self.zb[0:1, 0:65], rhs=self.zb[0:1, 0:512],
             start=True, stop=False)
        for q in range(4):
            r0 = q * 16
            load_kv(q, self.cak, self.cav, 32, 0, 0, KTs, Vs)
            for p in range(4):
                S.op(LOADQ, "dma_start", reads=[self.sKAT], writes=[KTs], dma="kts_new", out=KTs[0:64, p, 4096:4112], in_=self.sKAT[2 * p, :, r0:r0 + 16])
                S.op(LOADQ, "dma_start", reads=[self.sKAT], writes=[KTs], dma="kts_new", out=KTs[64:128, p, 4096:4112], in_=self.sKAT[2 * p + 1, :, r0:r0 + 16])
            S.op(LOADQ, "dma_start", reads=[self.sVA], writes=[Vs], dma="vs_new", out=Vs[0:16, 32, :, :].rearrange("p h d -> p (h d)"),
                 in_=self.sVA[r0:r0 + 16, :])
            S.op("dve", "memset", writes=[Qbd], ap=Qbd[:], constant=0.0)
            for p in range(4):
                S.op(LOADQ, "dma_start", reads=[self.sQAT], writes=[Qbd], dma="qbd", out=Qbd[0:64, p, 0:16], in_=self.sQAT[2 * p, :, r0:r0 + 16])
                S.op(LOADQ, "dma_start", reads=[self.sQAT], writes=[Qbd], dma="qbd", out=Qbd[64:128, p, 16:32], in_=self.sQAT[2 * p + 1, :, r0:r0 + 16])
            S.op("dve", "memset", writes=[lfa], ap=lfa[:, 32, :], constant=0.0)
            S.op(LOADQ, "dma_start", reads=[self.calf], writes=[lfa], dma="lfa", out=lfa[:, 0:32, :],
                 in_=self.calf[q, :, :].rearrange("(b k) h -> k b h", k=128))
            S.op(LOADQ, "dma_start", reads=[self.sLF], writes=[lfa], dma="lfa", out=lfa[0:16, 32, :], in_=self.sLF[r0:r0 + 16, :])
            lfa2 = lfa[:].rearrange("p b h -> p (b h)")
            S.op("pe", "matmul", reads=[self.cf, lfa], writes=[psC], out=psC[:, 0:264], lhsT=self.cf[:, C_TRIU:C_TRIU + 128], rhs=lfa2,
                 start=True, stop=True)
            S.op("pe", "matmul", reads=[self.cf, lfa], writes=[psC2], out=psC2[:, 0:264], lhsT=self.cf[:, C_ONE:C_ONE + 128], rhs=lfa2,
                 start=True, stop=True)
            S.op("dve", "memset", writes=[CAR], ap=CAR[:, 0, :], constant=0.0)
            for b in range(33):
                S.op("dve", "tensor_tensor", reads=[psC2, CAR], writes=[CAR], out=CAR[:, b + 1, :], in0=psC2[:, b * 8:(b + 1) * 8],
                     in1=CAR[:, b, :], op=ALU.add)
            S.op("dve", "tensor_tensor", reads=[psC, CAR], writes=[cfull], out=cfull[:].rearrange("p b h -> p (b h)"), in0=psC[:, 0:264],
                 in1=CAR[:, 0:33, :].rearrange("p b h -> p (b h)"), op=ALU.add)
            for h in range(8):
                S.op("dve", "tensor_scalar", reads=[cfull, CAR], writes=[biasS], out=biasS[:, :, h], in0=cfull[:, :, h],
                     scalar1=CAR[:, 33, h:h + 1], scalar2=-1.0, op0=ALU.subtract, op1=ALU.mult)
            for t_ in range(16):
                S.op("dve", "tensor_copy", reads=[biasS], writes=[biasF], out=biasF[:, :, :, t_], in_=biasS[:, :, :])
            for p in range(4):
                for ch in range(3):
                    b_lo = ch * 16
                    b_hi = min(33, b_lo + 16)
                    pS = psS[skn[0] % 2]
                    pt = pT[skn[0] % 2]
                    skn[0] += 1
                    for b in range(b_lo, b_hi):
                        cc = (b - b_lo) * 32
                        bfv = biasF[:].rearrange("p b h t -> p b (h t)")
                        if b < 32:
                            S.op("pe", "matmul", reads=[KTs, Qbd], writes=[pS], out=pS[:, cc:cc + 32], lhsT=KTs[:, p, b * 128:(b + 1) * 128],
                                 rhs=Qbd[:, p, :], start=True, stop=False)
                            S.op("pe", "matmul", reads=[self.cf, biasF], writes=[pS], out=pS[:, cc:cc + 32], lhsT=self.cf[:, C_ID:C_ID + 128],
                                 rhs=bfv[:, b, p * 32:(p + 1) * 32], start=False, stop=True)
                        else:
                            S.op("pe", "matmul", reads=[KTs, Qbd], writes=[pS], out=pS[0:16, cc:cc + 32], lhsT=KTs[:, p, 4096:4112],
                                 rhs=Qbd[:, p, :], start=True, stop=False)
                            S.op("pe", "matmul", reads=[self.identb, tri16b], writes=[pS], out=pS[0:16, cc:cc + 32], lhsT=self.identb[0:16, 0:16],
                                 rhs=tri16b[:, :], start=False, stop=False)
                            S.op("pe", "matmul", reads=[self.cf, biasF], writes=[pS], out=pS[0:16, cc:cc + 32], lhsT=self.cf[0:16, C_ID:C_ID + 16],
                                 rhs=bfv[0:16, b, p * 32:(p + 1) * 32], start=False, stop=True)
                    nfull = min(b_hi, 32) - b_lo
                    if nfull > 0:
                        S.op("act", "activation", reads=[pS], writes=[pt], out=pt[:, 0:nfull * 32], in_=pS[:, 0:nfull * 32], func=AF.Exp)
                    if b_hi == 33:
                        cc = (32 - b_lo) * 32
                        S.op("act", "activation", reads=[pS], writes=[pt], out=pt[0:16, cc:cc + 32], in_=pS[0:16, cc:cc + 32], func=AF.Exp)
                    for b in range(b_lo, b_hi):
                        cc = (b - b_lo) * 32
                        nk = 128 if b < 32 else 16
                        for s in range(2):
                            h = 2 * p + s
                            oc = (q * 8 + h) * 16
                            S.op("pe", "matmul", reads=[Vs, pt], writes=[psO], out=psO[0:65, oc:oc + 16], lhsT=Vs[0:nk, b, h, 0:65],
                                 rhs=pt[0:nk, cc + s * 16:cc + s * 16 + 16], start=False, stop=False)

        S.op("pe", "matmul", reads=[self.zb], writes=[psO], out=psO[0:65, 0:512], lhsT=self.zb[0:1, 0:65], rhs=self.zb[0:1, 0:512],
             start=False, stop=True)

        def dst_fox(ao):
            S.op(STOREQ, "dma_start", reads=[ao], writes=[self.sAOT], dma="ao0", out=self.sAOT[0:4, :, :].rearrange("p (two d) (q t) -> d q (p two) t", two=2, q=4),
                 in_=ao[0:64, 0:512].rearrange("d (q h t) -> d q h t", q=4, h=8))
        self.finalize(psO, 512, dst_fox, 0, fin)

        KBs = A.alloc("KBs", [128, 4, 528], BF16)
        Vb = A.alloc("Vb", [128, 5, 8, 66], BF16)
        S.op("dve", "memset", writes=[Vb], ap=Vb[:, :, :, 64], constant=1.0)
        for q in range(4):
            r0 = q * 16
            load_kv(q, self.cbk, self.cbv, 4, 0, 0, KBs, Vb)
            for p in range(4):
                S.op(LOADQ, "dma_start", reads=[self.sKBT], writes=[KBs], dma="kts_new", out=KBs[:, p, 512:528], in_=self.sKBT[p, :, r0:r0 + 16])
            S.op(LOADQ, "dma_start", reads=[self.sVB], writes=[Vb], dma="vs_new", out=Vb[0:16, 4, :, :].rearrange("p h d -> p (h d)"),
                 in_=self.sVB[r0:r0 + 16, :])
            S.op("dve", "memset", writes=[Qbd], ap=Qbd[:], constant=0.0)
            for p in range(4):
                S.op(LOADQ, "dma_start", reads=[self.sQBT], writes=[Qbd], dma="qbd", out=Qbd[0:64, p, 0:16], in_=self.sQBT[p, 0:64, r0:r0 + 16])
                S.op(LOADQ, "dma_start", reads=[self.sQBT], writes=[Qbd], dma="qbd", out=Qbd[64:128, p, 16:32], in_=self.sQBT[p, 64:128, r0:r0 + 16])
            for p in range(4):
                pS = psS[skn[0] % 2]
                pt = pT[skn[0] % 2]
                skn[0] += 1
                for b in range(5):
                    nk = 128 if b < 4 else 16
                    cc = b * 32
                    S.op("pe", "matmul", reads=[self.identb, BSb], writes=[pS], out=pS[0:nk, cc:cc + 32], lhsT=self.identb[0:nk, 0:nk],
                         rhs=BSb[:].rearrange("p a b c -> p a (b c)")[0:nk, b, p * 32:(p + 1) * 32], start=True, stop=False)
                    S.op("pe", "matmul", reads=[KBs, Qbd], writes=[pS], out=pS[0:nk, cc:cc + 32], lhsT=KBs[:, p, b * 128:b * 128 + nk],
                         rhs=Qbd[:, p, :], start=False, stop=True)
                S.op("act", "activation", reads=[pS], writes=[pt], out=pt[:, 0:128], in_=pS[:, 0:128], func=AF.Exp)
                S.op("act", "activation", reads=[pS], writes=[pt], out=pt[0:16, 128:160], in_=pS[0:16, 128:160], func=AF.Exp)
                for s in range(2):
                    h = 2 * p + s
                    oc = (q * 8 + h) * 16
                    for b in range(5):
                        nk = 128 if b < 4 else 16
                        cc = b * 32 + s * 16
                        S.op("pe", "matmul", reads=[Vb, pt], writes=[psO], out=psO[0:65, oc:oc + 16], lhsT=Vb[0:nk, b, h, 0:65],
                             rhs=pt[0:nk, cc:cc + 16], start=(b == 0), stop=(b == 4))

        def dst_band(ao):
            S.op(STOREQ, "dma_start", reads=[ao], writes=[self.sAOT], dma="ao1", out=self.sAOT[4:8, :, :].rearrange("p (two d) (q t) -> d q (p two) t", two=2, q=4),
                 in_=ao[0:64, 0:512].rearrange("d (q h t) -> d q h t", q=4, h=8))
        self.finalize(psO, 512, dst_band, 1, fin)

    def row_tiles(self):
        tl = [(128, i * 128, i) for i in range(NLOC)]
        tl.append((SR, TLOC, None))
        return tl

    def phase_outproj(self, layer):
        S, A, ps = self.S, self.A, self.ps
        S.barrier()
        A.reset()
        Wo = A.alloc("Wo", [128, 8, D], BF16)
        wsrc = self.w_out[layer]
        for hh in range(4):
            self.load_w_cast(Wo, Wo[:, hh * 2:(hh + 1) * 2, :], wsrc[hh * 256:(hh + 1) * 256, :].rearrange("(c p) n -> p c n", p=128), wsrc, "wo")
        aot = A.ring("aot", 3, [128, 8, 128], BF16)
        xt = A.ring("xt", 3, [128, D], F32)
        xo = A.ring("xo", 3, [128, D], F32)
        psY = [ps[0], ps[1], ps[2], ps[3]]
        Xout = self.X1
        k = 0
        for (R, row0, loc) in self.row_tiles():
            if layer == 1 and loc == 0:
                continue
            a = aot[k % 3]
            x = xt[k % 3]
            o = xo[k % 3]
            if loc is not None:
                S.op(LOADQ, "dma_start", reads=[self.AOT], writes=[a], dma=f"aot{k % 3}", out=a[:, :, 0:R],
                     in_=self.AOT[:, :, row0:row0 + R].rearrange("c p r -> p c r"))
                if layer == 0:
                    xsrc, xap = self.xc, self.xc[(T0 + loc) * 128:(T0 + loc + 1) * 128, :]
                else:
                    xsrc, xap = self.X2, self.X2[row0:row0 + R, :]
            else:
                S.op(LOADQ, "dma_start", reads=[self.sAOT], writes=[a], dma=f"aot{k % 3}", out=a[:, :, 0:R],
                     in_=self.sAOT[:, :, :].rearrange("c p r -> p c r"))
                if layer == 0:
                    xsrc, xap = self.xs, self.xs[:, :]
                else:
                    xsrc, xap = self.X2, self.X2[row0:row0 + R, :]
            S.op(LOADQ, "dma_start", reads=[xsrc], writes=[x], dma=f"xt{k % 3}", out=x[0:R, :], in_=xap)
            for cg in range(2):
                py = psY[(2 * k + cg) % 4]
                for h in range(8):
                    S.op("pe", "matmul", reads=[a, Wo], writes=[py], out=py[0:R, :], lhsT=a[:, h, 0:R], rhs=Wo[:, h, cg * 512:(cg + 1) * 512],
                         start=(h == 0), stop=(h == 7))
                S.op("dve", "tensor_tensor", reads=[py, x], writes=[o], out=o[0:R, cg * 512:(cg + 1) * 512], in0=py[0:R, :],
                     in1=x[0:R, cg * 512:(cg + 1) * 512], op=ALU.add)
            S.op(STOREQ, "dma_start", reads=[o], writes=[Xout], dma=f"xo{k % 3}", out=Xout[row0:row0 + R, :], in_=o[0:R, :])
            k += 1

    def phase_ffn(self, layer):
        S, A, ps = self.S, self.A, self.ps
        S.barrier()
        A.reset()
        W1 = A.alloc("W1", [128, 8, 2 * DFF], BF16)
        W2 = A.alloc("W2", [128, 22, D], BF16)
        for c in range(8):
            for hf in range(2):
                self.load_w_cast(W1, W1[:, c, hf * DFF:(hf + 1) * DFF], self.w_ffn_in[layer, c * 128:(c + 1) * 128, hf * DFF:(hf + 1) * DFF],
                                 self.w_ffn_in, "w1")
        for j in range(22):
            self.load_w_cast(W2, W2[:, j, :], self.w_ffn_out[layer, j * 128:(j + 1) * 128, :], self.w_ffn_out, "w2")
        gF = A.alloc("gF", [128, D], F32)
        S.op(LOADQ, "dma_start", reads=[self.norm_ffn], writes=[gF], dma="g0", out=gF[:], in_=self.norm_ffn[layer, :].partition_broadcast(128))
        last = (layer == 1)
        if last:
            gN = A.alloc("gN", [128, D], F32)
            S.op(LOADQ, "dma_start", reads=[self.norm_final], writes=[gN], dma="bf", out=gN[:], in_=self.norm_final[0, :].partition_broadcast(128))
        xt = A.ring("xt", 4, [128, D], F32)
        junk = A.alloc("junk", [128, D], BF16)
        ss = A.ring("ss", 4, [128, 1], F32)
        hbr = A.ring("hb", 2, [128, D], BF16)
        hT = A.ring("hT", 2, [128, 8, 256], BF16)
        actT = A.ring("actT", 1, [128, 22, 256], BF16)
        sg = A.ring("sg", 2, [128, 256], F32)
        psT = ps[0]
        psA = [ps[1], ps[2]]
        psB = [ps[3], ps[4]]
        psY = [ps[5], ps[6]]
        Xin, Xout = self.X1, self.X2
        tiles = [t for t in self.row_tiles() if not (last and t[2] == 0)]
        supers = []
        i = 0
        while i < len(tiles):
            if tiles[i][2] is not None and i + 1 < len(tiles) and tiles[i + 1][2] is not None:
                supers.append([tiles[i], tiles[i + 1]])
                i += 2
            else:
                supers.append([tiles[i]])
                i += 1
        kk = [0]
        jn = 0
        yn = 0

        def prep(si):
            sup = supers[si]
            hTb = hT[si % 2]
            xs_ = []
            tbs = []
            off = 0
            for (R, row0, loc) in sup:
                k = kk[0]
                x = xt[k % 4]
                S.op(LOADQ, "dma_start", reads=[Xin], writes=[x], dma=f"xt{k % 4}", out=x[0:R, :], in_=Xin[row0:row0 + R, :])
                tb = self.norm_T(x, R, gF, hbr, hTb[:, :, off:off + R], hTb, psT, k, junk, ss, evac_eng="dve", defer_T=True)
                tbs.append(tb)
                xs_.append((x, R, row0, loc, off, k % 4))
                off += R
                kk[0] += 1
            return xs_, off, tbs

        nxt = prep(0)
        for stg in range(4):
            for f in nxt[2]:
                f[stg]()
        for si, sup in enumerate(supers):
            hTb = hT[si % 2]
            aT = actT[0]
            xs_, N, _ = nxt
            for j in range(22):
                pa = psA[jn % 2]
                pb = psB[jn % 2]
                sgb = sg[jn % 2]
                jn += 1
                for c in range(8):
                    S.op("pe", "matmul", reads=[W1, hTb], writes=[pa], out=pa[:, 0:N], lhsT=W1[:, c, j * 128:(j + 1) * 128], rhs=hTb[:, c, 0:N],
                         start=(c == 0), stop=(c == 7))
                for c in range(8):
                    S.op("pe", "matmul", reads=[W1, hTb], writes=[pb], out=pb[:, 0:N], lhsT=W1[:, c, DFF + j * 128:DFF + (j + 1) * 128],
                         rhs=hTb[:, c, 0:N], start=(c == 0), stop=(c == 7))
                S.op("act", "activation", reads=[pa], writes=[sgb], out=sgb[:, 0:N], in_=pa[:, 0:N], func=AF.Silu)
                S.op("dve", "tensor_tensor", reads=[pb, sgb], writes=[aT], out=aT[:, j, 0:N], in0=pb[:, 0:N], in1=sgb[:, 0:N], op=ALU.mult)
                if si + 1 < len(supers):
                    if j == 1:
                        nxt = prep(si + 1)
                        for f in nxt[2]:
                            f[0]()
                    if j == 5:
                        for f in nxt[2]:
                            f[1]()
                    if j == 8:
                        for f in nxt[2]:
                            f[2]()
                    if j == 14:
                        for f in nxt[2]:
                            f[3]()
            for (x, R, row0, loc, off, slot) in xs_:
                o = x
                for cg in range(2):
                    py = psY[cg]
                    for j in range(22):
                        S.op("pe", "matmul", reads=[aT, W2], writes=[py], out=py[0:R, :], lhsT=aT[:, j, off:off + R], rhs=W2[:, j, cg * 512:(cg + 1) * 512],
                             start=(j == 0), stop=(j == 21))
                    S.op("dve", "tensor_tensor", reads=[py, x], writes=[o], out=o[0:R, cg * 512:(cg + 1) * 512], in0=py[0:R, :],
                         in1=x[0:R, cg * 512:(cg + 1) * 512], op=ALU.add)
                if not last:
                    S.op(STOREQ, "dma_start", reads=[o], writes=[Xout], dma=f"xo_st{slot}", out=Xout[row0:row0 + R, :], in_=o[0:R, :])
                else:
                    ssb = ss[yn % 2]
                    y = x
                    S.op("dve", "scalar_tensor_tensor", reads=[o], writes=[junk, ssb], out=junk[0:R, :], in0=o[0:R, :], scalar=1.0,
                         in1=o[0:R, :], op0=ALU.mult, op1=ALU.mult, accum_out=ssb[0:R, :])
                    S.op("act", "activation", reads=[ssb, self.epsb], writes=[ssb], out=ssb[0:R, :], in_=ssb[0:R, :], func=AF.Ln,
                         bias=self.epsb[0:R, :], scale=1.0 / D)
                    S.op("act", "activation", reads=[ssb], writes=[ssb], out=ssb[0:R, :], in_=ssb[0:R, :], func=AF.Exp, scale=-0.5)
                    S.op("dve", "scalar_tensor_tensor", reads=[o, ssb, gN], writes=[y], out=y[0:R, :], in0=o[0:R, :], scalar=ssb[0:R, :],
                         in1=gN[0:R, :], op0=ALU.mult, op1=ALU.mult)
                    if loc is not None:
                        dstb, dap = self.o_y, self.o_y[(loc - 1) * 128:loc * 128, :]
                    else:
                        dstb, dap = self.o_ys, self.o_ys[:, :]
                    S.op(STOREQ, "dma_start", reads=[y], writes=[dstb], dma=f"xo_st{slot}", out=dap, in_=y[0:R, :])
                yn += 1

    def phase4(self):
        S, A, ps = self.S, self.A, self.ps
        S.barrier()
        A.reset()
        W = A.alloc("Wi1", [128, 8, 1280], BF16)
        for c in range(8):
            self.load_w_cast(W, W[:, c, :], self.w_in_odd[c * 128:(c + 1) * 128, :], self.w_in_odd, "w0")
        g1 = A.alloc("g1", [128, D], F32)
        S.op(LOADQ, "dma_start", reads=[self.norm_mix], writes=[g1], dma="g0", out=g1[:], in_=self.norm_mix[1, :].partition_broadcast(128))
        ones2 = A.alloc("ones2", [128, 2], F32)
        S.op("dve", "memset", writes=[ones2], ap=ones2[:], constant=1.0)
        xt = A.ring("xt", 3, [128, D], F32)
        junk = A.alloc("junk", [128, D], BF16)
        ss = A.ring("ss", 2, [128, 1], F32)
        hbr = A.ring("hb", 2, [128, D], BF16)
        hT = A.ring("hT", 2, [128, 8, 128], BF16)
        qb = A.ring("qb", 2, [128, D], BF16)
        kvf = A.ring("kvf", 2, [128, 256], F32)
        kb16 = A.ring("kb16", 2, [128, 128], BF16)
        vaug = A.ring("vaug", 2, [128, 2, 66], BF16)
        for vb_ in vaug:
            S.op("dve", "memset", writes=[vb_], ap=vb_[:], constant=0.0)
        qT = A.ring("qT", 2, [128, 8, 128], BF16)
        kT = A.ring("kT", 2, [128, 128], BF16)
        psT = ps[0]
        psG = [ps[1], ps[2], ps[3]]
        psQ = [ps[4], ps[5]]
        psK = ps[6]
        gn = 0
        k = 0
        for (R, row0, loc) in self.row_tiles():
            x = xt[k % 3]
            S.op(LOADQ, "dma_start", reads=[self.X2], writes=[x], dma=f"xt{k % 3}", out=x[0:R, :], in_=self.X2[row0:row0 + R, :])
            hTb = hT[k % 2]
            self.norm_T(x, R, g1, hbr, hTb[:, :, 0:R], hTb, psT, k, junk, ss, evac_eng="act")
            do_q = (loc != 0)
            qbb = qb[k % 2]
            if do_q:
                for cg in range(2):
                    pg = psG[gn % 3]
                    gn += 1
                    for c in range(8):
                        S.op("pe", "matmul", reads=[hTb, W], writes=[pg], out=pg[0:R, :], lhsT=hTb[:, c, 0:R], rhs=W[:, c, cg * 512:(cg + 1) * 512],
                             start=(c == 0), stop=(c == 7))
                    qdst = qbb[0:R, :].rearrange("p (a s d) -> p a s d", a=8, s=2)[:, :, cg, :]
                    psrc = pg[0:R, :].rearrange("p (a d) -> p a d", a=8)
                    if cg == 0:
                        S.op("act", "activation", reads=[pg], writes=[qbb], out=qdst, in_=psrc, func=AF.Copy, scale=0.125)
                    else:
                        S.op("dve", "tensor_scalar", reads=[pg], writes=[qbb], out=qdst, in0=psrc, scalar1=0.125, scalar2=None,
                             op0=ALU.mult)
                pq = psQ[k % 2]
                pqv = pq[:].bitcast(BF16).rearrange("p (a r) -> p a r", a=8)
                for a in range(8):
                    S.op("pe", "transpose", reads=[qbb, self.identb], writes=[pq], out=pqv[:, a, 0:R], in_=qbb[0:R, a * 128:(a + 1) * 128],
                         identity=self.identb[0:R, 0:R])
                qTb = qT[k % 2]
                S.op("dve", "tensor_copy", reads=[pq], writes=[qTb], out=qTb[:, :, 0:R], in_=pqv[:, :, 0:R])
                if loc is not None:
                    dstb, dap = self.QCT, self.QCT[:, :, row0:row0 + R].rearrange("a p r -> p a r")
                else:
                    dstb, dap = self.sQCT, self.sQCT[:, :, :].rearrange("a p r -> p a r")
                S.op(STOREQ, "dma_start", reads=[qTb], writes=[dstb], dma=f"qT_st{k % 2}", out=dap, in_=qTb[:, :, 0:R])
            pg = psG[gn % 3]
            gn += 1
            for c in range(8):
                S.op("pe", "matmul", reads=[hTb, W], writes=[pg], out=pg[0:R, 0:256], lhsT=hTb[:, c, 0:R], rhs=W[:, c, 1024:1280],
                     start=(c == 0), stop=(c == 7))
            kv = kvf[k % 2]
            S.op("act", "copy", reads=[pg], writes=[kv], out=kv[0:R, :], in_=pg[0:R, 0:256])
            if loc is None:
                S.op(STOREQ, "dma_start", reads=[kv], writes=[self.o_sck], dma=f"kv_st{k % 2}", out=self.o_sck[:, :], in_=kv[0:R, 0:128])
                S.op(STOREQ, "dma_start", reads=[kv], writes=[self.o_scv], dma=f"kv_st{k % 2}", out=self.o_scv[:, :], in_=kv[0:R, 128:256])
            elif loc == NLOC - 1:
                S.op(STOREQ, "dma_start", reads=[kv], writes=[self.o_ck], dma=f"kv_st{k % 2}", out=self.o_ck[:, :], in_=kv[0:R, 0:128])
                S.op(STOREQ, "dma_start", reads=[kv], writes=[self.o_cv], dma=f"kv_st{k % 2}", out=self.o_cv[:, :], in_=kv[0:R, 128:256])
            kb_ = kb16[k % 2]
            S.op("dve", "tensor_copy", reads=[kv], writes=[kb_], out=kb_[0:R, :], in_=kv[0:R, 0:128])
            S.op("pe", "transpose", reads=[kb_, self.identb], writes=[psK], out=psK[:].bitcast(BF16)[:, 0:R], in_=kb_[0:R, :],
                 identity=self.identb[0:R, 0:R])
            kTb = kT[k % 2]
            S.op("act", "copy", reads=[psK], writes=[kTb], out=kTb[:, 0:R], in_=psK[:].bitcast(BF16)[:, 0:R])
            if loc is not None:
                S.op(STOREQ, "dma_start", reads=[kTb], writes=[self.KCT], dma=f"kT_st{k % 2}", out=self.KCT[:, row0:row0 + R], in_=kTb[:, 0:R])
                vcol = self.validt[:, T0 + loc:T0 + loc + 1]
            else:
                S.op(STOREQ, "dma_start", reads=[kTb], writes=[self.sKCT], dma=f"kT_st{k % 2}", out=self.sKCT[:, :], in_=kTb[:, 0:R])
                vcol = self.cf[0:SR, C_ONE:C_ONE + 1]
            va = vaug[k % 2]
            S.op("dve", "tensor_scalar", reads=[kv, self.validt], writes=[va], out=va[0:R, :, 0:64],
                 in0=kv[0:R, 128:256].rearrange("p (h d) -> p h d", h=2), scalar1=vcol[0:R, :], scalar2=None, op0=ALU.mult)
            S.op("dve", "tensor_scalar", reads=[ones2, self.validt], writes=[va], out=va[0:R, :, 64], in0=ones2[0:R, :], scalar1=vcol[0:R, :],
                 scalar2=None, op0=ALU.mult)
            if loc is not None:
                S.op(STOREQ, "dma_start", reads=[va], writes=[self.VC], dma=f"va_st{k % 2}", out=self.VC[loc, :, :], in_=va[0:R, :, :].rearrange("p h d -> p (h d)"))
            else:
                S.op(STOREQ, "dma_start", reads=[va], writes=[self.sVC], dma=f"va_st{k % 2}", out=self.sVC[:, :], in_=va[0:R, :, :].rearrange("p h d -> p (h d)"))
            k += 1


    def phase5(self):
        S, A, ps = self.S, self.A, self.ps
        S.barrier()
        A.reset()
        KC = A.alloc("KC", [128, TLOC], BF16)
        VCt = A.alloc("VCt", [128, NLOC, 132], BF16)
        S.op(LOADQ, "dma_start", reads=[self.KCT], writes=[KC], dma="kc_l", out=KC[:, :], in_=self.KCT[:, :])
        S.op(LOADQ, "dma_start", reads=[self.VC], writes=[VCt], dma="vc_l", out=VCt[:, :, :], in_=self.VC[:, :, :].rearrange("t k c -> k t c"))
        TSf = A.alloc("TSf", [128, 2, 16, 128], F32)
        TB = A.alloc("TB", [128, 2, 2048], BF16)
        S.op(LOADQ, "dma_start", reads=[self.ts], writes=[TSf], dma="tsf", out=TSf[:].rearrange("p a b c -> p (a b c)"), in_=self.ts[:, :])
        for kb in range(2):
            for h in range(16):
                S.op("dve", "tensor_tensor", reads=[TSf, self.cf], writes=[TB], out=TB[:, kb, h * 128:(h + 1) * 128], in0=TSf[:, kb, h, :],
                     in1=self.cf[:, C_SMASK + kb * 128:C_SMASK + (kb + 1) * 128], op=ALU.add)
        es = A.alloc("es", [128, 16], F32)
        S.op(LOADQ, "dma_start", reads=[self.sinks], writes=[es], dma="es_l", out=es[:], in_=self.sinks[0, :].partition_broadcast(128))
        S.op("act", "activation", reads=[es], writes=[es], out=es[:], in_=es[:], func=AF.Exp)
        self.es = es
        Qe = A.ring("Qe5", 2, [128, 1024], BF16)
        Qo = A.ring("Qo5", 2, [128, 1024], BF16)
        pT = A.ring("pT5", 3, [128, 512], BF16)
        VO = A.alloc("VO", [128, NLOC, 64], BF16)
        ones64 = A.alloc("ones64", [128, 64], F32)
        S.op("dve", "memset", writes=[ones64], ap=ones64[:], constant=1.0)
        for blk in range(NLOC):
            S.op("dve", "tensor_scalar", reads=[ones64, self.validt], writes=[VO], out=VO[:, blk, :], in0=ones64[:, :],
                 scalar1=self.validt[:, T0 + blk:T0 + blk + 1], scalar2=None, op0=ALU.mult)
        es0 = A.alloc("es0", [1, 16], F32)
        S.op(LOADQ, "dma_start", reads=[self.sinks], writes=[es0], dma="g0", out=es0[:], in_=self.sinks[0:1, :])
        S.op("act", "activation", reads=[es0], writes=[es0], out=es0[:], in_=es0[:], func=AF.Exp)
        esrow = A.alloc("esrow", [1, 2048], F32)
        eshi = A.alloc("eshi", [1, 2048], BF16)
        eslo = A.alloc("eslo", [1, 2048], BF16)
        onesb = A.alloc("onesb", [1, 64], BF16)
        S.op("dve", "memset", writes=[onesb], ap=onesb[:], constant=1.0)
        for h in range(16):
            S.op("dve", "tensor_scalar", reads=[self.cf, es0], writes=[esrow], out=esrow[0:1, h * 128:(h + 1) * 128], in0=self.cf[0:1, C_ONE:C_ONE + 128],
                 scalar1=es0[0:1, h:h + 1], scalar2=None, op0=ALU.mult)
        S.op("dve", "tensor_copy", reads=[esrow], writes=[eshi], out=eshi[:], in_=esrow[:])
        S.op("dve", "tensor_tensor", reads=[esrow, eshi], writes=[esrow], out=esrow[:], in0=esrow[:], in1=eshi[:], op=ALU.subtract)
        S.op("dve", "tensor_copy", reads=[esrow], writes=[eslo], out=eslo[:], in_=esrow[:])
        rec5 = A.ring("rec5", 2, [64, 512], F32)
        ao5 = A.ring("ao5", 2, [64, 512], BF16)
        psS = [ps[0], ps[1], ps[6]]
        psO = [ps[2], ps[3]]
        psD = [ps[4], ps[5]]
        for b in Qe:
            S.op("dve", "memset", writes=[b], ap=b[64:128, :], constant=0.0)
        for b in Qo:
            S.op("dve", "memset", writes=[b], ap=b[0:64, :], constant=0.0)
        steps = []
        cnt = {"h": 0, "s": 0}
        for gi in range(1, NLOC):
            col0 = gi * 128
            qe = Qe[gi % 2]
            qo = Qo[gi % 2]

            def pre_q(col0=col0, qe=qe, qo=qo, gi=gi):
                S.op(LOADQ, "dma_start", reads=[self.QCT], writes=[qe], dma=f"qe{gi % 2}", out=qe[0:64, :].rearrange("p (a r) -> p a r", a=8),
                     in_=self.QCT[:, 0:64, col0:col0 + 128].rearrange("a p r -> p a r"))
                S.op(LOADQ, "dma_start", reads=[self.QCT], writes=[qo], dma=f"qo{gi % 2}", out=qo[64:128, :].rearrange("p (a r) -> p a r", a=8),
                     in_=self.QCT[:, 64:128, col0:col0 + 128].rearrange("a p r -> p a r"))
            first = True
            for s_ in range(2):
                Q = qe if s_ == 0 else qo
                for u in range(2):
                    hk = cnt["h"]
                    cnt["h"] += 1
                    po = psO[hk % 2]
                    pd = psD[hk % 2]
                    h0 = 8 * s_ + 4 * u
                    for kb in range(2):
                        sk = cnt["s"]
                        cnt["s"] += 1
                        pS = psS[sk % 3]
                        pt = pT[sk % 3]
                        blk = gi - 1 + kb

                        def qk(kb=kb, h0=h0, pS=pS, Q=Q, u=u, blk=blk):
                            S.op("pe", "matmul", reads=[self.identb, TB], writes=[pS], out=pS[:, 0:512], lhsT=self.identb[:, :],
                                 rhs=TB[:, kb, h0 * 128:h0 * 128 + 512], start=True, stop=False)
                            S.op("pe", "matmul", reads=[KC, Q], writes=[pS], out=pS[:, 0:512], lhsT=KC[:, blk * 128:(blk + 1) * 128],
                                 rhs=Q[:, u * 512:(u + 1) * 512], start=False, stop=True)

                        def ex(pS=pS, pt=pt):
                            S.op("act", "activation", reads=[pS], writes=[pt], out=pt[:, :], in_=pS[:, 0:512], func=AF.Exp)

                        def pv(kb=kb, po=po, pd=pd, pt=pt, s_=s_, blk=blk, h0=h0):
                            S.op("pe", "matmul", reads=[VCt, pt], writes=[po], out=po[0:64, 0:512], lhsT=VCt[:, blk, s_ * 66:s_ * 66 + 64], rhs=pt[:, :],
                                 start=(kb == 0), stop=(kb == 1))
                            S.op("pe", "matmul", reads=[VO, pt], writes=[pd], out=pd[0:64, 0:512], lhsT=VO[:, blk, :], rhs=pt[:, :],
                                 start=(kb == 0), stop=False)
                            if kb == 1:
                                S.op("pe", "matmul", reads=[onesb, eshi], writes=[pd], out=pd[0:64, 0:512], lhsT=onesb[0:1, :],
                                     rhs=eshi[0:1, h0 * 128:h0 * 128 + 512], start=False, stop=False)
                                S.op("pe", "matmul", reads=[onesb, eslo], writes=[pd], out=pd[0:64, 0:512], lhsT=onesb[0:1, :],
                                     rhs=eslo[0:1, h0 * 128:h0 * 128 + 512], start=False, stop=True)
                        st = {"qk": qk, "ex": ex, "pv": pv}
                        if first:
                            st["pre"] = pre_q
                            first = False
                        if kb == 1:
                            def post(po=po, pd=pd, h0=h0, col0=col0, hk=hk):
                                rec = rec5[hk % 2]
                                ao = ao5[hk % 2]
                                S.op("act", "activation", reads=[pd], writes=[rec], out=rec[:, :], in_=pd[0:64, 0:512], func=AF.Ln)
                                S.op("act", "activation", reads=[rec], writes=[rec], out=rec[:, :], in_=rec[:, :], func=AF.Exp, scale=-1.0)
                                S.op("dve", "tensor_tensor", reads=[po, rec], writes=[ao], out=ao[:, :], in0=po[0:64, 0:512], in1=rec[:, :], op=ALU.mult)
                                S.op(STOREQ, "dma_start", reads=[ao], writes=[self.AOT], dma=f"ao{hk % 2}",
                                     out=self.AOT[h0 // 2:h0 // 2 + 2, :, col0:col0 + 128].rearrange("p (two d) r -> d (p two) r", two=2),
                                     in_=ao[0:64, 0:512].rearrange("d (h r) -> d h r", h=4))
                            st["post"] = post
                        steps.append(st)
        self.run_steps(steps, delay=2, ahead=2)

    def phase5s(self):
        S, A, ps = self.S, self.A, self.ps
        S.barrier()
        es = self.es
        kcf = A.alloc("kcf", [128, 128], F32)
        vcf = A.alloc("vcf", [128, 128], F32)
        KCs = A.alloc("KCs", [128, 144], BF16)
        Vc = A.alloc("Vc", [128, 2, 2, 66], BF16)
        Qbd = A.alloc("Qbd5", [128, 8, 32], BF16)
        TSSf = A.alloc("TSSf", [128, 2, 16, 16], F32)
        TSSb = A.alloc("TSSb", [128, 2, 256], BF16)
        pt = A.alloc("pt5s", [128, 512], BF16)
        fin = self.alloc_fin(ps[7])
        psK = ps[5]
        psS = ps[6]
        psO = [ps[0], ps[1]]
        S.op(LOADQ, "dma_start", reads=[self.tss], writes=[TSSf], dma="tsf", out=TSSf[:].rearrange("p a b c -> p (a b c)"), in_=self.tss[:, :])
        for blk in range(2):
            S.op("dve", "tensor_copy", reads=[TSSf], writes=[TSSb], out=TSSb[:, blk, :].rearrange("p (a s t) -> p a s t", a=8, s=2),
                 in_=TSSf[:, blk, :, :].rearrange("p (s a) t -> p a s t", s=2))
        S.op("dve", "memset", writes=[Vc], ap=Vc[:, :, :, 64], constant=1.0)
        for q in range(4):
            r0 = q * 16
            S.op(LOADQ, "dma_start", reads=[self.cck], writes=[kcf], dma="kcf", out=kcf[:, :], in_=self.cck[q, :, :])
            S.op(LOADQ, "dma_start", reads=[self.ccv], writes=[vcf], dma="vcf", out=vcf[:, :], in_=self.ccv[q, :, :])
            S.op("pe", "transpose", reads=[kcf, self.cf], writes=[psK], out=psK[:, 0:128], in_=kcf[:, :], identity=self.cf[:, C_ID:C_ID + 128])
            S.op("act", "copy", reads=[psK], writes=[KCs], out=KCs[:, 0:128], in_=psK[:, 0:128])
            S.op(LOADQ, "dma_start", reads=[self.sKCT], writes=[KCs], dma="kcs_new", out=KCs[:, 128:144], in_=self.sKCT[:, r0:r0 + 16])
            S.op("dve", "tensor_copy", reads=[vcf], writes=[Vc], out=Vc[:, 0, :, 0:64], in_=vcf[:, :].rearrange("p (h d) -> p h d", h=2))
            S.op(LOADQ, "dma_start", reads=[self.sVC], writes=[Vc], dma="vc_new", out=Vc[0:16, 1, :, :].rearrange("p h d -> p (h d)"),
                 in_=self.sVC[r0:r0 + 16, :])
            S.op("dve", "memset", writes=[Qbd], ap=Qbd[:], constant=0.0)
            S.op(LOADQ, "dma_start", reads=[self.sQCT], writes=[Qbd], dma="qbd5", out=Qbd[0:64, :, 0:16],
                 in_=self.sQCT[:, 0:64, r0:r0 + 16].rearrange("a p r -> p a r"))
            S.op(LOADQ, "dma_start", reads=[self.sQCT], writes=[Qbd], dma="qbd5", out=Qbd[64:128, :, 16:32],
                 in_=self.sQCT[:, 64:128, r0:r0 + 16].rearrange("a p r -> p a r"))
            Q2 = Qbd[:].rearrange("p a c -> p (a c)")
            S.op("pe", "matmul", reads=[self.identb, TSSb], writes=[psS], out=psS[:, 0:256], lhsT=self.identb[:, :], rhs=TSSb[:, 0, :],
                 start=True, stop=False)
            S.op("pe", "matmul", reads=[KCs, Qbd], writes=[psS], out=psS[:, 0:256], lhsT=KCs[:, 0:128], rhs=Q2, start=False, stop=True)
            S.op("pe", "matmul", reads=[self.identb, TSSb], writes=[psS], out=psS[0:16, 256:512], lhsT=self.identb[0:16, 0:16], rhs=TSSb[0:16, 1, :],
                 start=True, stop=False)
            S.op("pe", "matmul", reads=[KCs, Qbd], writes=[psS], out=psS[0:16, 256:512], lhsT=KCs[:, 128:144], rhs=Q2, start=False, stop=True)
            S.op("act", "activation", reads=[psS], writes=[pt], out=pt[:, 0:256], in_=psS[:, 0:256], func=AF.Exp)
            S.op("act", "activation", reads=[psS], writes=[pt], out=pt[0:16, 256:512], in_=psS[0:16, 256:512], func=AF.Exp)
            po = psO[q // 2]
            for a in range(8):
                for s in range(2):
                    h = 8 * s + a
                    oc = ((q % 2) * 16 + h) * 16
                    cc = a * 32 + s * 16
                    S.op("pe", "matmul", reads=[Vc, pt], writes=[po], out=po[0:65, oc:oc + 16], lhsT=Vc[:, 0, s, 0:65], rhs=pt[:, cc:cc + 16],
                         start=True, stop=False)
                    S.op("pe", "matmul", reads=[Vc, pt], writes=[po], out=po[0:65, oc:oc + 16], lhsT=Vc[0:16, 1, s, 0:65], rhs=pt[0:16, 256 + cc:256 + cc + 16],
                         start=False, stop=True)
        for bk in range(2):
            def dst(ao, bk=bk):
                for qq in range(2):
                    c0 = (2 * bk + qq) * 16
                    S.op(STOREQ, "dma_start", reads=[ao], writes=[self.sAOT], dma=f"ao{bk}",
                         out=self.sAOT[:, :, c0:c0 + 16].rearrange("p (two d) t -> d (p two) t", two=2),
                         in_=ao[0:64, qq * 256:(qq + 1) * 256].rearrange("d (h t) -> d h t", h=16))
            blocks = [((qq * 16 + h) * 16, 16, h) for qq in range(2) for h in range(16)]
            self.finalize(psO[bk], 512, dst, bk, fin, sink_cols=(es, blocks))


_CACHE = {}


def get_nc(phases):
    key = tuple(sorted(phases))
    if key not in _CACHE:
        _CACHE[key] = Builder(set(phases)).nc
    return _CACHE[key]


def kernel(x_prompt, x_sample, cache_a_k, cache_a_v, cache_a_logf, cache_b_k, cache_b_v,
           cache_c_k, cache_c_v, norm_mix, norm_ffn, norm_final, w_in_even, b_forget,
           rel_bias_b, w_out_even, w_in_odd, sinks_c, w_out_odd, t5_bias, w_ffn_in, w_ffn_out):
    phases = [int(p) for p in os.environ.get("K_PHASES", "1,2,3,4,5,6").split(",")]
    f = lambda a: np.ascontiguousarray(np.asarray(a, dtype=np.float32))
    x_prompt = f(x_prompt)
    x_sample = f(x_sample)
    nc = get_nc(phases)
    cst = host_consts()
    bb, bs, ts, tss = host_bias_tables(f(rel_bias_b), f(t5_bias))
    shared = {
        "cst": cst, "norm_mix": f(norm_mix), "norm_ffn": f(norm_ffn), "norm_final": f(norm_final).reshape(1, D),
        "w_in_even": f(w_in_even)[0], "b_forget": f(b_forget).reshape(1, 8), "w_out_even": f(w_out_even)[0],
        "w_out_odd": f(w_out_odd)[0], "w_in_odd": f(w_in_odd)[0], "sinks_c": f(sinks_c).reshape(1, 16),
        "w_ffn_in": f(w_ffn_in), "w_ffn_out": f(w_ffn_out),
        "bb": bb.reshape(128, -1), "bs": bs.reshape(128, -1), "ts": ts.reshape(128, -1), "tss": tss.reshape(128, -1),
    }
    in_maps = []
    for c in range(8):
        b, half = c // 2, c % 2
        if half == 1:
            xc = x_prompt[b]
            valid = np.ones((128, NCTX), np.float32)
        else:
            xc = np.concatenate([np.zeros((4096, D), np.float32), x_prompt[b, 0:4096]], axis=0)
            valid = np.ones((128, NCTX), np.float32)
            valid[:, 0:32] = 0.0
        sl = slice(4 * c, 4 * c + 4)
        m = dict(shared)
        m.update({
            "xc": np.ascontiguousarray(xc), "xs": np.ascontiguousarray(x_sample[sl].reshape(SR, D)), "valid": valid,
            "cak": f(cache_a_k[0, sl]).reshape(4, 4096, 512), "cav": f(cache_a_v[0, sl]).reshape(4, 4096, 512),
            "calf": f(cache_a_logf[0, sl]).reshape(4, 4096, 8),
            "cbk": f(cache_b_k[0, sl]).reshape(4, 512, 512), "cbv": f(cache_b_v[0, sl]).reshape(4, 512, 512),
            "cck": f(cache_c_k[0, sl]).reshape(4, 128, 128), "ccv": f(cache_c_v[0, sl]).reshape(4, 128, 128),
        })
        in_maps.append(m)
    res = run_bass_kernel_spmd(nc, in_maps, core_ids=list(range(8)))
    R = res.results
    y_prompt = np.zeros((4, 8192, D), np.float32)
    a_k = np.zeros((1, 4, 8192, 8, 64), np.float32)
    a_v = np.zeros((1, 4, 8192, 8, 64), np.float32)
    a_lf = np.zeros((1, 4, 8192, 8), np.float32)
    b_k = np.zeros((1, 4, 512, 8, 64), np.float32)
    b_v = np.zeros((1, 4, 512, 8, 64), np.float32)
    c_k = np.zeros((1, 4, 128, 2, 64), np.float32)
    c_v = np.zeros((1, 4, 128, 2, 64), np.float32)
    y_s = np.zeros((32, 16, D), np.float32)
    sak = np.zeros((1, 32, 16, 8, 64), np.float32)
    sav = np.zeros((1, 32, 16, 8, 64), np.float32)
    salf = np.zeros((1, 32, 16, 8), np.float32)
    sbk = np.zeros((1, 32, 16, 8, 64), np.float32)
    sbv = np.zeros((1, 32, 16, 8, 64), np.float32)
    sck = np.zeros((1, 32, 16, 2, 64), np.float32)
    scv = np.zeros((1, 32, 16, 2, 64), np.float32)
    for c in range(8):
        b, half = c // 2, c % 2
        r = R[c]
        rows = slice(half * 4096, (half + 1) * 4096)
        y_prompt[b, rows] = r["o_y"]
        a_k[0, b, rows] = r["o_ak"].reshape(4096, 8, 64)
        a_v[0, b, rows] = r["o_av"].reshape(4096, 8, 64)
        a_lf[0, b, rows] = r["o_alf"]
        if half == 1:
            b_k[0, b] = r["o_bk"].reshape(512, 8, 64)
            b_v[0, b] = r["o_bv"].reshape(512, 8, 64)
            c_k[0, b] = r["o_ck"].reshape(128, 2, 64)
            c_v[0, b] = r["o_cv"].reshape(128, 2, 64)
        sl = slice(4 * c, 4 * c + 4)
        y_s[sl] = r["o_ys"].reshape(4, 16, D)
        sak[0, sl] = r["o_sak"].reshape(4, 16, 8, 64)
        sav[0, sl] = r["o_sav"].reshape(4, 16, 8, 64)
        salf[0, sl] = r["o_salf"].reshape(4, 16, 8)
        sbk[0, sl] = r["o_sbk"].reshape(4, 16, 8, 64)
        sbv[0, sl] = r["o_sbv"].reshape(4, 16, 8, 64)
        sck[0, sl] = r["o_sck"].reshape(4, 16, 2, 64)
        scv[0, sl] = r["o_scv"].reshape(4, 16, 2, 64)
    return (y_prompt, y_s, a_k, a_v, a_lf, b_k, b_v, c_k, c_v, sak, sav, salf, sbk, sbv, sck, scv)
```

```python
import os
import numpy as np
import concourse.bass as bass
import concourse.mybir as mybir
from concourse.bass_utils import run_bass_kernel_spmd

F32 = mybir.dt.float32
BF16 = mybir.dt.bfloat16
AF = mybir.ActivationFunctionType
ALU = mybir.AluOpType
AX = mybir.AxisListType

SAME_ENGINE_SYNC = True
NEG = -30000.0
LOADQ = "sp"
STOREQ = "pool"


class Buf:
    __slots__ = ("name", "t", "last_w", "readers")

    def __init__(self, name, t):
        self.name = name
        self.t = t
        self.last_w = None
        self.readers = []

    def __getitem__(self, idx):
        return self.t[idx]


class Op:
    __slots__ = ("eng", "meth", "kw", "deps", "is_dma", "sem", "val", "signal")

    def __init__(self, eng, meth, kw, is_dma):
        self.eng = eng
        self.meth = meth
        self.kw = kw
        self.is_dma = is_dma
        self.deps = []
        self.sem = None
        self.val = None
        self.signal = False


class Sched:
    ENGS = ("pe", "act", "dve", "pool", "sp")

    def __init__(self, nc):
        self.nc = nc
        self.ops = {e: [] for e in self.ENGS}
        self.dma_cnt = {}
        self.last_dma = {}
        self.pending = {e: [] for e in self.ENGS}
        self.nops = 0

    def dram(self, name, shape, dtype, kind="Internal"):
        return Buf(name, self.nc.dram_tensor(name, list(shape), dtype, kind=kind).ap())

    def op(self, eng, meth, reads=(), writes=(), dma=None, **kw):
        o = Op(eng, meth, kw, dma is not None)
        deps = list(self.pending[eng])
        self.pending[eng] = []
        for b in reads:
            if b.last_w is not None:
                deps.append(b.last_w)
        for b in writes:
            if b.last_w is not None:
                deps.append(b.last_w)
            deps.extend(b.readers)
        seen = set()
        for d in deps:
            if id(d) in seen or d is o:
                continue
            seen.add(id(d))
            if (not d.is_dma) and (not o.is_dma) and d.eng == eng:
                if eng == "pe" or not SAME_ENGINE_SYNC:
                    continue
            o.deps.append(d)
            d.signal = True
        for b in reads:
            b.readers.append(o)
        for b in writes:
            b.last_w = o
            b.readers = []
        if dma is not None:
            c = self.dma_cnt.get(dma, 0) + 16
            self.dma_cnt[dma] = c
            o.sem = ("dma", dma)
            o.val = c
            self.last_dma[dma] = o
        self.ops[eng].append(o)
        self.nops += 1
        return o

    def barrier(self):
        lasts = []
        for e in self.ENGS:
            for o in reversed(self.ops[e]):
                if not o.is_dma:
                    lasts.append(o)
                    break
        lasts.extend(self.last_dma.values())
        for e in self.ENGS:
            self.pending[e] = list(lasts)

    def emit(self):
        nc = self.nc
        sems = {}
        for e in self.ENGS:
            sems[("eng", e)] = nc.alloc_semaphore(name=f"s_{e}")
        for k in self.dma_cnt:
            sems[("dma", k)] = nc.alloc_semaphore(name=f"d_{k}")
        for e in self.ENGS:
            n = 0
            for o in self.ops[e]:
                if (not o.is_dma) and o.signal:
                    n += 1
                    o.sem = ("eng", e)
                    o.val = n
        final = dict(self.dma_cnt)

        def run(ename, eng):
            waited = {}
            for o in self.ops[ename]:
                need = {}
                for d in o.deps:
                    if need.get(d.sem, 0) < d.val:
                        need[d.sem] = d.val
                for k, v in need.items():
                    if waited.get(k, 0) < v:
                        eng.wait_ge(sems[k], v)
                        waited[k] = v
                ins = getattr(eng, o.meth)(**o.kw)
                if o.is_dma:
                    ins.then_inc(sems[o.sem], 16)
                elif o.signal:
                    ins.then_inc(sems[o.sem], 1)
            if ename == "sp":
                for k, v in final.items():
                    eng.wait_ge(sems[("dma", k)], v)

        with nc.Block() as block:
            @block.tensor
            def _(e):
                run("pe", e)

            @block.scalar
            def _(e):
                run("act", e)

            @block.vector
            def _(e):
                run("dve", e)

            @block.gpsimd
            def _(e):
                run("pool", e)

            @block.sync
            def _(e):
                run("sp", e)


class Arena:
    def __init__(self, nc, nbytes):
        self.t = nc.alloc_sbuf_tensor("arena", [128, nbytes // 2], BF16)
        self.cap = nbytes
        self.base = 0
        self.off = 0
        self.n = 0

    def mark_persistent(self):
        self.base = self.off

    def reset(self):
        self.off = self.base

    def alloc(self, name, shape, dtype):
        per = 1
        for s in shape[1:]:
            per *= s
        esz = 4 if dtype == F32 else 2
        nb = per * esz
        off = (self.off + 31) // 32 * 32
        assert off + nb <= self.cap, (name, off, nb, self.cap)
        self.off = off + nb
        v = self.t[0:shape[0], off // 2:(off + nb) // 2]
        if dtype == F32:
            v = v.bitcast(F32)
        if len(shape) == 3:
            v = v.rearrange("p (a b) -> p a b", a=shape[1])
        elif len(shape) == 4:
            v = v.rearrange("p (a b c) -> p a b c", a=shape[1], b=shape[2])
        self.n += 1
        return Buf(f"{name}_{self.n}", v)

    def ring(self, name, n, shape, dtype):
        return [self.alloc(f"{name}{i}", shape, dtype) for i in range(n)]


D = 1024
NCTX = 64
T0 = 31
NLOC = 33
TLOC = NLOC * 128
BT0 = 27
NBT = NCTX - BT0
DFF = 2816
SR = 64

C_ID = 0
C_TRIU = 128
C_ONE = 256
C_TRIM = 384
C_BMASK = 512
C_SMASK = 512 + 640
C_TRI16 = C_SMASK + 256
C_W = C_TRI16 + 16


def host_consts():
    c = np.zeros((128, C_W), np.float32)
    c[:, C_ID:C_ID + 128] = np.eye(128)
    i = np.arange(128)
    c[:, C_TRIU:C_TRIU + 128] = (i[:, None] <= i[None, :])
    c[:, C_ONE:C_ONE + 128] = 1.0
    c[:, C_TRIM:C_TRIM + 128] = np.where(i[:, None] <= i[None, :], 0.0, NEG)
    q = np.arange(128)
    for kb in range(5):
        kpos = kb * 128 + np.arange(128)
        cq = 8 + q // 64
        ck = kpos // 64
        ok = (ck[:, None] >= cq[None, :] - 8) & (ck[:, None] <= cq[None, :])
        c[:, C_BMASK + kb * 128:C_BMASK + (kb + 1) * 128] = np.where(ok, 0.0, NEG)
    for kb in range(2):
        kpos = kb * 128 + np.arange(128)
        cq = 2 + q // 64
        ck = kpos // 64
        ok = (ck[:, None] >= cq[None, :] - 2) & (ck[:, None] <= cq[None, :])
        c[:, C_SMASK + kb * 128:C_SMASK + (kb + 1) * 128] = np.where(ok, 0.0, NEG)
    j = np.arange(16)
    c[0:16, C_TRI16:C_TRI16 + 16] = np.where(j[:, None] <= j[None, :], 0.0, NEG)
    return c


def t5_bucket_np(rel_mem):
    nb = 16
    max_exact = 8
    n = np.abs(rel_mem)
    large = max_exact + (np.log(np.maximum(n, 1).astype(np.float32) / max_exact)
                         / np.float32(np.log(128 / max_exact)) * (nb - max_exact)).astype(np.int32)
    large = np.minimum(large, nb - 1)
    return np.where(rel_mem > 0, nb, 0) + np.where(n < max_exact, n, large)


def host_bias_tables(rel_bias_b, t5_bias):
    tb = rel_bias_b[0]
    q = np.arange(128)
    bb = np.zeros((128, 5, 8, 128), np.float32)
    for kb in range(5):
        k = kb * 128 + np.arange(128)
        rel = (512 + q)[None, :] - k[:, None]
        idx = np.clip(rel, -128, 128) + 128
        bb[:, kb] = np.transpose(tb[idx], (0, 2, 1))
    bs = np.zeros((128, 5, 8, 16), np.float32)
    t = np.arange(16)
    for kb in range(5):
        j = kb * 128 + np.arange(128)
        rel = t[None, :] - j[:, None] + 512
        idx = np.clip(rel, -128, 128) + 128
        bs[:, kb] = np.transpose(tb[idx], (0, 2, 1))
    ts = np.zeros((128, 2, 16, 128), np.float32)
    for kb in range(2):
        k = kb * 128 + np.arange(128)
        rel = (128 + q)[None, :] - k[:, None]
        idx = t5_bucket_np(-rel)
        ts[:, kb] = np.transpose(t5_bias[idx], (0, 2, 1))
    tss = np.zeros((128, 2, 16, 16), np.float32)
    for kb in range(2):
        j = kb * 128 + np.arange(128)
        rel = t[None, :] - j[:, None] + 128
        idx = t5_bucket_np(-rel)
        tss[:, kb] = np.transpose(t5_bias[idx], (0, 2, 1))
    return bb, bs, ts, tss


class Builder:
    def __init__(self, phases):
        self.phases = phases
        self.nc = bass.Bass("TRN2", target_bir_lowering=False)
        self.S = Sched(self.nc)
        self.A = Arena(self.nc, 204000)
        self.ps = [Buf(f"ps{i}", self.nc.alloc_psum_tensor(f"ps{i}", [128, 512], F32)[:]) for i in range(8)]
        self.declare_io()
        self.setup_consts()
        if 1 in phases:
            self.phase1()
        if 2 in phases:
            self.phase2a()
            self.phase2b()
            self.phase2c()
        if 3 in phases:
            self.phase_outproj(0)
            self.phase_ffn(0)
        if 4 in phases:
            self.phase4()
        if 5 in phases:
            self.phase5()
            self.phase5s()
        if 6 in phases:
            self.phase_outproj(1)
            self.phase_ffn(1)
        self.S.emit()

    def declare_io(self):
        S = self.S
        i_ = lambda n, s: S.dram(n, s, F32, kind="ExternalInput")
        o_ = lambda n, s: S.dram(n, s, F32, kind="ExternalOutput")
        self.xc = i_("xc", [8192, D])
        self.xs = i_("xs", [SR, D])
        self.valid = i_("valid", [128, NCTX])
        self.cst = i_("cst", [128, C_W])
        self.cak = i_("cak", [4, 4096, 512])
        self.cav = i_("cav", [4, 4096, 512])
        self.calf = i_("calf", [4, 4096, 8])
        self.cbk = i_("cbk", [4, 512, 512])
        self.cbv = i_("cbv", [4, 512, 512])
        self.cck = i_("cck", [4, 128, 128])
        self.ccv = i_("ccv", [4, 128, 128])
        self.norm_mix = i_("norm_mix", [2, D])
        self.norm_ffn = i_("norm_ffn", [2, D])
        self.norm_final = i_("norm_final", [1, D])
        self.w_in_even = i_("w_in_even", [D, 3080])
        self.b_forget = i_("b_forget", [1, 8])
        self.w_out = [i_("w_out_even", [D, D]), i_("w_out_odd", [D, D])]
        self.w_in_odd = i_("w_in_odd", [D, 1280])
        self.sinks = i_("sinks_c", [1, 16])
        self.w_ffn_in = i_("w_ffn_in", [2, D, 2 * DFF])
        self.w_ffn_out = i_("w_ffn_out", [2, DFF, D])
        self.bb = i_("bb", [128, 5 * 8 * 128])
        self.bs = i_("bs", [128, 5 * 8 * 16])
        self.ts = i_("ts", [128, 2 * 16 * 128])
        self.tss = i_("tss", [128, 2 * 16 * 16])
        self.o_y = o_("o_y", [4096, D])
        self.o_ys = o_("o_ys", [SR, D])
        self.o_ak = o_("o_ak", [4096, 512])
        self.o_av = o_("o_av", [4096, 512])
        self.o_alf = o_("o_alf", [4096, 8])
        self.o_bk = o_("o_bk", [512, 512])
        self.o_bv = o_("o_bv", [512, 512])
        self.o_ck = o_("o_ck", [128, 128])
        self.o_cv = o_("o_cv", [128, 128])
        self.o_sak = o_("o_sak", [SR, 512])
        self.o_sav = o_("o_sav", [SR, 512])
        self.o_salf = o_("o_salf", [SR, 8])
        self.o_sbk = o_("o_sbk", [SR, 512])
        self.o_sbv = o_("o_sbv", [SR, 512])
        self.o_sck = o_("o_sck", [SR, 128])
        self.o_scv = o_("o_scv", [SR, 128])
        d_ = lambda n, s, dt=BF16: S.dram(n, s, dt)
        self.KAT = d_("KAT", [8, 64, 8192])
        self.CQ = d_("CQ", [8, TLOC], F32)
        self.VA = d_("VA", [NCTX, 128, 528])
        self.QAT = d_("QAT", [8, 64, TLOC])
        self.KBT = d_("KBT", [4, 128, NBT * 128])
        self.VB = d_("VB", [NBT, 128, 528])
        self.QBT = d_("QBT", [4, 128, TLOC])
        self.sQAT = d_("sQAT", [8, 64, SR])
        self.sKAT = d_("sKAT", [8, 64, SR])
        self.sVA = d_("sVA", [SR, 528])
        self.sQBT = d_("sQBT", [4, 128, SR])
        self.sKBT = d_("sKBT", [4, 128, SR])
        self.sVB = d_("sVB", [SR, 528])
        self.sLF = d_("sLF", [SR, 8], F32)
        self.AOT = d_("AOT", [8, 128, TLOC])
        self.sAOT = d_("sAOT", [8, 128, SR])
        self.X1 = d_("X1", [TLOC + SR, D], F32)
        self.X2 = d_("X2", [TLOC + SR, D], F32)
        self.QCT = d_("QCT", [8, 128, TLOC])
        self.KCT = d_("KCT", [128, TLOC])
        self.VC = d_("VC", [NLOC, 128, 132])
        self.sQCT = d_("sQCT", [8, 128, SR])
        self.sKCT = d_("sKCT", [128, SR])
        self.sVC = d_("sVC", [SR, 132])

    def setup_consts(self):
        S, A = self.S, self.A
        self.cf = A.alloc("cf", [128, C_W], F32)
        self.identb = A.alloc("identb", [128, 128], BF16)
        self.trimb = A.alloc("trimb", [128, 128], BF16)
        self.sel = A.alloc("sel", [128, 64], F32)
        self.validt = A.alloc("validt", [128, NCTX], F32)
        self.CALL = A.alloc("CALL", [128, NCTX, 8], F32)
        self.CARRY = A.alloc("CARRY", [128, NCTX + 1, 8], F32)
        self.epsb = A.alloc("epsb", [128, 1], F32)
        self.tiny = A.alloc("tiny", [128, 1], F32)
        self.zb = A.alloc("zb", [128, 512], BF16)
        A.mark_persistent()
        S.op("dve", "memset", writes=[self.zb], ap=self.zb[:], constant=0.0)
        S.op(LOADQ, "dma_start", reads=[self.cst], writes=[self.cf], dma="cf", out=self.cf[:], in_=self.cst[:])
        S.op(LOADQ, "dma_start", reads=[self.valid], writes=[self.validt], dma="validt", out=self.validt[:], in_=self.valid[:])
        S.op("dve", "tensor_copy", reads=[self.cf], writes=[self.identb], out=self.identb[:], in_=self.cf[:, C_ID:C_ID + 128])
        S.op("dve", "tensor_copy", reads=[self.cf], writes=[self.trimb], out=self.trimb[:], in_=self.cf[:, C_TRIM:C_TRIM + 128])
        S.op("dve", "memset", writes=[self.sel], ap=self.sel[:], constant=0.0)
        S.op("dve", "memset", writes=[self.sel], ap=self.sel[64:65, :], constant=1.0)
        S.op("dve", "memset", writes=[self.epsb], ap=self.epsb[:], constant=1e-6)
        S.op("dve", "memset", writes=[self.CARRY], ap=self.CARRY[:, 0, :], constant=0.0)

    def load_w_cast(self, dst, dst_ap, src_ap, src_buf, name):
        self.S.op("pool", "dma_start", reads=[src_buf], writes=[dst], dma=name, out=dst_ap, in_=src_ap)

    def norm_T(self, xt, R, gt, hbr, hT_dst_ap, hT_buf, psT, k, junk, ss, evac_eng="act", defer_T=False):
        S = self.S
        ssb = ss[k % len(ss)]
        hb = hbr[k % len(hbr)]

        def st_a():
            S.op("dve", "scalar_tensor_tensor", reads=[xt], writes=[junk, ssb], out=junk[0:R, :], in0=xt[0:R, :], scalar=1.0,
                 in1=xt[0:R, :], op0=ALU.mult, op1=ALU.mult, accum_out=ssb[0:R, :])

        def st_b():
            S.op("act", "activation", reads=[ssb, self.epsb], writes=[ssb], out=ssb[0:R, :], in_=ssb[0:R, :], func=AF.Ln,
                 bias=self.epsb[0:R, :], scale=1.0 / D)
            S.op("act", "activation", reads=[ssb], writes=[ssb], out=ssb[0:R, :], in_=ssb[0:R, :], func=AF.Exp, scale=-0.5)

        def st_c():
            S.op("dve", "scalar_tensor_tensor", reads=[xt, ssb, gt], writes=[hb], out=hb[0:R, :], in0=xt[0:R, :],
                 scalar=ssb[0:R, :], in1=gt[0:R, :], op0=ALU.mult, op1=ALU.mult)

        def st_d():
            self.norm_T_b(hb, R, hT_dst_ap, hT_buf, psT, evac_eng)
        if defer_T:
            return [st_a, st_b, st_c, st_d]
        st_a()
        st_b()
        st_c()
        st_d()
        return ssb

    def norm_T_b(self, hb, R, hT_dst_ap, hT_buf, psT, evac_eng):
        S = self.S
        pv = psT[:].bitcast(BF16).rearrange("p (c r) -> p c r", c=8)
        for c in range(8):
            S.op("pe", "transpose", reads=[hb, self.identb], writes=[psT], out=pv[:, c, 0:R],
                 in_=hb[0:R, c * 128:(c + 1) * 128], identity=self.identb[0:R, 0:R])
        if evac_eng == "act":
            S.op("act", "copy", reads=[psT], writes=[hT_buf], out=hT_dst_ap, in_=pv[:, :, 0:R])
        else:
            S.op("dve", "tensor_copy", reads=[psT], writes=[hT_buf], out=hT_dst_ap, in_=pv[:, :, 0:R])

    def finalize(self, psO, ncols, dst_ap_fn, k, fin, sink_cols=None, rec_eng="act"):
        S = self.S
        ot = fin["ot"][k % 2]
        rec = fin["rec"][k % 2]
        ao = fin["ao"][k % 2]
        psD = fin["psD"]
        S.op("dve", "tensor_copy", reads=[psO], writes=[ot], out=ot[0:65, 0:ncols], in_=psO[0:65, 0:ncols])
        S.op("pe", "matmul", reads=[self.sel, ot], writes=[psD], out=psD[0:64, 0:ncols], lhsT=self.sel[0:65, :],
             rhs=ot[0:65, 0:ncols], start=True, stop=True)
        if sink_cols is not None:
            es, blocks = sink_cols
            for (c0, n, h) in blocks:
                S.op("dve", "tensor_scalar", reads=[psD, es], writes=[rec], out=rec[0:64, c0:c0 + n], in0=psD[0:64, c0:c0 + n],
                     scalar1=es[0:64, h:h + 1], scalar2=1e-30, op0=ALU.add, op1=ALU.max)
        else:
            S.op("dve", "tensor_scalar", reads=[psD], writes=[rec], out=rec[0:64, 0:ncols], in0=psD[0:64, 0:ncols],
                 scalar1=1e-30, scalar2=None, op0=ALU.max)
        if rec_eng == "dve":
            S.op("dve", "reciprocal", reads=[rec], writes=[rec], out=rec[0:64, 0:ncols], in_=rec[0:64, 0:ncols])
        else:
            S.op("act", "activation", reads=[rec], writes=[rec], out=rec[0:64, 0:ncols], in_=rec[0:64, 0:ncols], func=AF.Ln)
            S.op("act", "activation", reads=[rec], writes=[rec], out=rec[0:64, 0:ncols], in_=rec[0:64, 0:ncols], func=AF.Exp, scale=-1.0)
        S.op("dve", "tensor_tensor", reads=[ot, rec], writes=[ao], out=ao[0:64, 0:ncols], in0=ot[0:64, 0:ncols],
             in1=rec[0:64, 0:ncols], op=ALU.mult)
        dst_ap_fn(ao)

    def alloc_fin(self, psD):
        A = self.A
        return dict(ot=A.ring("ot", 2, [128, 512], F32), rec=A.ring("rec", 2, [64, 512], F32),
                    ao=A.ring("ao", 2, [64, 512], BF16), psD=psD)

    def phase1(self):
        S, A, ps = self.S, self.A, self.ps
        S.barrier()
        A.reset()
        W0 = A.alloc("W0", [128, 8, 3080], BF16)
        for c in range(8):
            self.load_w_cast(W0, W0[:, c, :], self.w_in_even[c * 128:(c + 1) * 128, :], self.w_in_even, "w0")
        g0 = A.alloc("g0", [128, D], F32)
        S.op(LOADQ, "dma_start", reads=[self.norm_mix], writes=[g0], dma="g0", out=g0[:], in_=self.norm_mix[0, :].partition_broadcast(128))
        bf = A.alloc("bf", [128, 8], F32)
        S.op(LOADQ, "dma_start", reads=[self.b_forget], writes=[bf], dma="bf", out=bf[:], in_=self.b_forget[0, :].partition_broadcast(128))
        ones8 = A.alloc("ones8", [128, 8], F32)
        S.op("dve", "memset", writes=[ones8], ap=ones8[:], constant=1.0)
        xt = A.ring("xt", 3, [128, D], F32)
        junk = A.alloc("junk", [128, D], BF16)
        ss = A.ring("ss", 2, [128, 1], F32)
        hbr = A.ring("hb", 2, [128, D], BF16)
        hT = A.ring("hT", 2, [128, 8, 128], BF16)
        f32o = {n: A.ring(n, 2, [128, 512], F32) for n in ("kaf", "vaf", "kbf", "vbf")}
        b16 = {n: A.ring(n, 2, [128, 512], BF16) for n in ("qab", "kab", "qbb", "kbb")}
        vaug = {n: A.ring(n, 2, [128, 8, 66], BF16) for n in ("vaa", "vba")}
        for n in vaug:
            for vb_ in vaug[n]:
                S.op("dve", "memset", writes=[vb_], ap=vb_[:], constant=0.0)
        tst = {n: A.ring(n, 2, [128, 4, 128], BF16) for n in ("qbT", "kbT")}
        tsth = {n: A.ring(n, 2, [64, 8, 128], BF16) for n in ("qaT", "kaT")}
        cqT = A.ring("cqT", 2, [8, 128], F32)
        t8 = A.ring("t8", 2, [128, 8], F32)
        lf = A.ring("lf", 2, [128, 8], F32)
        ctile = A.ring("ctile", 2, [128, 8], F32)
        psT = ps[0]
        psG = [ps[1], ps[2], ps[3], ps[7]]
        psQ = [ps[4], ps[5]]
        psC = ps[6]
        gcount = [0]
        qcount = [0]

        deferred = []

        def flush_deferred():
            while deferred:
                deferred.pop(0)()

        def group(hTb, R, c0, ncol):
            pg = psG[gcount[0] % 4]
            gcount[0] += 1
            for c in range(8):
                S.op("pe", "matmul", reads=[hTb, W0], writes=[pg], out=pg[0:R, 0:ncol], lhsT=hTb[:, c, 0:R],
                     rhs=W0[:, c, c0:c0 + ncol], start=(c == 0), stop=(c == 7))
            flush_deferred()
            return pg

        def transposes(src, R, k, name, dst_buf, dst_ap, evac):
            pq = psQ[qcount[0] % 2]
            qcount[0] += 1
            pv = pq[:].bitcast(BF16)[:, 0:512].rearrange("p (c r) -> p c r", c=4)
            st = tst[name][k % 2]
            for c in range(4):
                S.op("pe", "transpose", reads=[src, self.identb], writes=[pq], out=pv[:, c, 0:R],
                     in_=src[0:R, c * 128:(c + 1) * 128], identity=self.identb[0:R, 0:R])
            if evac == "act":
                S.op("act", "copy", reads=[pq], writes=[st], out=st[:, :, 0:R], in_=pv[:, :, 0:R])
            else:
                S.op("dve", "tensor_copy", reads=[pq], writes=[st], out=st[:, :, 0:R], in_=pv[:, :, 0:R])
            S.op(STOREQ, "dma_start", reads=[st], writes=[dst_buf], dma=name + f"_st{k % 2}", out=dst_ap, in_=st[:, :, 0:R])

        def transposes_h(src, R, k, name, dst_buf, dst_ap):
            pq = psQ[qcount[0] % 2]
            qcount[0] += 1
            pv = pq[:].bitcast(BF16)[:, 0:512].rearrange("p (c r) -> p c r", c=4)
            st = tsth[name][k % 2]
            stv = st[:, :, :].rearrange("d (c s) r -> d c s r", s=2)
            for c in range(4):
                S.op("pe", "transpose", reads=[src, self.identb], writes=[pq], out=pv[:, c, 0:R],
                     in_=src[0:R, c * 128:(c + 1) * 128], identity=self.identb[0:R, 0:R])
            S.op("act", "copy", reads=[pq], writes=[st], out=stv[:, :, 0, 0:R], in_=pv[0:64, :, 0:R])
            S.op("dve", "tensor_copy", reads=[pq], writes=[st], out=stv[:, :, 1, 0:R], in_=pv[64:128, :, 0:R])
            S.op(STOREQ, "dma_start", reads=[st], writes=[dst_buf], dma=name + f"_st{k % 2}", out=dst_ap, in_=st[:, :, 0:R])

        def prep_tile(k, R, xsrc_buf, xsrc_ap):
            x = xt[k % 3]
            S.op(LOADQ, "dma_start", reads=[xsrc_buf], writes=[x], dma=f"xt{k % 3}", out=x[0:R, :], in_=xsrc_ap)
            hTb = hT[k % 2]
            return self.norm_T(x, R, g0, hbr, hTb[:, :, 0:R], hTb, psT, k, junk, ss, evac_eng="act", defer_T=True)

        def do_tile(k, R, t_ctx, do_q, do_band, outs, vcol, dst, hook, hook0=None):
            hTb = hT[k % 2]
            if hook0 is not None:
                hook0()
            if do_q:
                pg = group(hTb, R, 0, 512)
                qab = b16["qab"][k % 2]
                S.op("act", "activation", reads=[pg], writes=[qab], out=qab[0:R, :], in_=pg[0:R, :], func=AF.Copy, scale=0.125)
                deferred.append(lambda qab=qab: transposes_h(qab, R, k, "qaT", dst["QAT"][0], dst["QAT"][1]))
            pg = group(hTb, R, 512, 512)
            kaf = f32o["kaf"][k % 2]
            S.op("dve", "tensor_copy", reads=[pg], writes=[kaf], out=kaf[0:R, :], in_=pg[0:R, :])
            if outs.get("ak") is not None:
                S.op(STOREQ, "dma_start", reads=[kaf], writes=[outs["ak"][0]], dma=f"kaf_st{k % 2}", out=outs["ak"][1], in_=kaf[0:R, :])
            kab = b16["kab"][k % 2]
            S.op("act", "copy", reads=[kaf], writes=[kab], out=kab[0:R, :], in_=kaf[0:R, :])
            deferred.append(lambda kab=kab: transposes_h(kab, R, k, "kaT", dst["KAT"][0], dst["KAT"][1]))
            pg = group(hTb, R, 1024, 512)
            vaf = f32o["vaf"][k % 2]
            S.op("act", "copy", reads=[pg], writes=[vaf], out=vaf[0:R, :], in_=pg[0:R, :])
            if outs.get("av") is not None:
                S.op(STOREQ, "dma_start", reads=[vaf], writes=[outs["av"][0]], dma=f"vaf_st{k % 2}", out=outs["av"][1], in_=vaf[0:R, :])
            vaa = vaug["vaa"][k % 2]
            S.op("act", "copy", reads=[pg], writes=[vaa], out=vaa[0:R, :, 0:64], in_=pg[0:R, :].rearrange("p (h d) -> p h d", h=8))
            S.op("dve", "tensor_scalar", reads=[ones8, self.validt], writes=[vaa], out=vaa[0:R, :, 64], in0=ones8[0:R, :],
                 scalar1=vcol, scalar2=None, op0=ALU.mult)
            S.op(STOREQ, "dma_start", reads=[vaa], writes=[dst["VA"][0]], dma=f"vaa_st{k % 2}", out=dst["VA"][1],
                 in_=vaa[0:R, :, :].rearrange("p h d -> p (h d)"))
            if hook is not None:
                hook()
            if do_band:
                if do_q:
                    pg = group(hTb, R, 1536, 512)
                    qbb = b16["qbb"][k % 2]
                    S.op("dve", "tensor_scalar", reads=[pg], writes=[qbb], out=qbb[0:R, :], in0=pg[0:R, :], scalar1=0.125,
                         scalar2=None, op0=ALU.mult)
                    deferred.append(lambda qbb=qbb: transposes(qbb, R, k, "qbT", dst["QBT"][0], dst["QBT"][1], "dve"))
                pg = group(hTb, R, 2048, 512)
                kbf = f32o["kbf"][k % 2]
                S.op("act", "copy", reads=[pg], writes=[kbf], out=kbf[0:R, :], in_=pg[0:R, :])
                if outs.get("bk") is not None:
                    S.op(STOREQ, "dma_start", reads=[kbf], writes=[outs["bk"][0]], dma=f"kbf_st{k % 2}", out=outs["bk"][1], in_=kbf[0:R, :])
                kbb = b16["kbb"][k % 2]
                S.op("dve", "tensor_copy", reads=[kbf], writes=[kbb], out=kbb[0:R, :], in_=kbf[0:R, :])
                deferred.append(lambda kbb=kbb: transposes(kbb, R, k, "kbT", dst["KBT"][0], dst["KBT"][1], "act"))
                pg = group(hTb, R, 2560, 512)
                vbf = f32o["vbf"][k % 2]
                S.op("dve", "tensor_copy", reads=[pg], writes=[vbf], out=vbf[0:R, :], in_=pg[0:R, :])
                if outs.get("bv") is not None:
                    S.op(STOREQ, "dma_start", reads=[vbf], writes=[outs["bv"][0]], dma=f"vbf_st{k % 2}", out=outs["bv"][1], in_=vbf[0:R, :])
                vba = vaug["vba"][k % 2]
                S.op("act", "copy", reads=[vbf], writes=[vba], out=vba[0:R, :, 0:64], in_=vbf[0:R, :].rearrange("p (h d) -> p h d", h=8))
                S.op("dve", "tensor_scalar", reads=[ones8, self.validt], writes=[vba], out=vba[0:R, :, 64], in0=ones8[0:R, :],
                     scalar1=vcol, scalar2=None, op0=ALU.mult)
                S.op(STOREQ, "dma_start", reads=[vba], writes=[dst["VB"][0]], dma=f"vba_st{k % 2}", out=dst["VB"][1],
                     in_=vba[0:R, :, :].rearrange("p h d -> p (h d)"))
            pg = group(hTb, R, 3072, 8)
            t8b = t8[k % 2]
            lfb = lf[k % 2]
            S.op("dve", "tensor_tensor", reads=[pg, bf], writes=[t8b], out=t8b[0:R, :], in0=pg[0:R, 0:8], in1=bf[0:R, :], op=ALU.add)
            S.op("act", "activation", reads=[t8b], writes=[t8b], out=t8b[0:R, :], in_=t8b[0:R, :], func=AF.Exp, scale=-1.0)
            S.op("act", "activation", reads=[t8b], writes=[t8b], out=t8b[0:R, :], in_=t8b[0:R, :], func=AF.Ln, bias=1.0, scale=1.0)
            S.op("dve", "tensor_scalar", reads=[t8b], writes=[lfb], out=lfb[0:R, :], in0=t8b[0:R, :], scalar1=-1.0, scalar2=None, op0=ALU.mult)
            flush_deferred()
            if outs.get("alf") is not None:
                S.op(STOREQ, "dma_start", reads=[lfb], writes=[outs["alf"][0]], dma=f"lf_st{k % 2}", out=outs["alf"][1], in_=lfb[0:R, :])
            if t_ctx is not None:
                deferred.append(lambda t=t_ctx, lfb=lfb, k=k: cumsum_block(t, lfb, k))

        def cumsum_block(t, lfb, k):
            if True:
                S.op("pe", "matmul", reads=[self.cf, lfb], writes=[psC], out=psC[:, 0:8], lhsT=self.cf[:, C_TRIU:C_TRIU + 128],
                     rhs=lfb[:, :], start=True, stop=True)
                S.op("pe", "matmul", reads=[self.cf, lfb], writes=[psC], out=psC[:, 8:16], lhsT=self.cf[:, C_ONE:C_ONE + 128],
                     rhs=lfb[:, :], start=True, stop=True)
                S.op("dve", "tensor_tensor", reads=[psC, self.CARRY], writes=[self.CALL], out=self.CALL[:, t, :], in0=psC[:, 0:8],
                     in1=self.CARRY[:, t, :], op=ALU.add)
                S.op("dve", "tensor_tensor", reads=[psC, self.CARRY], writes=[self.CARRY], out=self.CARRY[:, t + 1, :], in0=psC[:, 8:16],
                     in1=self.CARRY[:, t, :], op=ALU.add)
                if t >= T0:
                    cq = cqT[k % 2]
                    S.op("pe", "transpose", reads=[self.CALL, self.cf], writes=[psC], out=psC[0:8, 128:256], in_=self.CALL[:, t, :],
                         identity=self.cf[:, C_ID:C_ID + 128])
                    S.op("dve", "tensor_copy", reads=[psC], writes=[cq], out=cq[:, :], in_=psC[0:8, 128:256])
                    S.op(STOREQ, "dma_start", reads=[cq], writes=[self.CQ], dma=f"cq_st{k % 2}", out=self.CQ[:, (t - T0) * 128:(t - T0 + 1) * 128], in_=cq[:, :])

        tiles = []
        for t in range(NCTX):
            loc = t - T0
            outs = {}
            if loc >= 1:
                r0 = (loc - 1) * 128
                outs["ak"] = (self.o_ak, self.o_ak[r0:r0 + 128, :])
                outs["av"] = (self.o_av, self.o_av[r0:r0 + 128, :])
                outs["alf"] = (self.o_alf, self.o_alf[r0:r0 + 128, :])
            if t >= NCTX - 4:
                r0 = (t - (NCTX - 4)) * 128
                outs["bk"] = (self.o_bk, self.o_bk[r0:r0 + 128, :])
                outs["bv"] = (self.o_bv, self.o_bv[r0:r0 + 128, :])
            dst = {"KAT": (self.KAT, self.KAT[:, :, t * 128:(t + 1) * 128].rearrange("h d r -> d h r")),
                   "VA": (self.VA, self.VA[t, :, :])}
            if t >= BT0:
                j = t - BT0
                dst["KBT"] = (self.KBT, self.KBT[:, :, j * 128:(j + 1) * 128].rearrange("c p r -> p c r"))
                dst["VB"] = (self.VB, self.VB[j, :, :])
            if loc >= 0:
                dst["QAT"] = (self.QAT, self.QAT[:, :, loc * 128:(loc + 1) * 128].rearrange("h d r -> d h r"))
                dst["QBT"] = (self.QBT, self.QBT[:, :, loc * 128:(loc + 1) * 128].rearrange("c p r -> p c r"))
            tiles.append(dict(R=128, xb=self.xc, xap=self.xc[t * 128:(t + 1) * 128, :], t=t, do_q=(loc >= 0), do_band=(t >= BT0),
                              outs=outs, vcol=self.validt[:, t:t + 1], dst=dst))
        outs = {"ak": (self.o_sak, self.o_sak[:, :]), "av": (self.o_sav, self.o_sav[:, :]), "alf": (self.o_salf, self.o_salf[:, :]),
                "bk": (self.o_sbk, self.o_sbk[:, :]), "bv": (self.o_sbv, self.o_sbv[:, :])}
        dst = {"KAT": (self.sKAT, self.sKAT[:, :, :].rearrange("h d r -> d h r")), "VA": (self.sVA, self.sVA[:, :]),
               "KBT": (self.sKBT, self.sKBT[:, :, :].rearrange("c p r -> p c r")), "VB": (self.sVB, self.sVB[:, :]),
               "QAT": (self.sQAT, self.sQAT[:, :, :].rearrange("h d r -> d h r")),
               "QBT": (self.sQBT, self.sQBT[:, :, :].rearrange("c p r -> p c r"))}
        tiles.append(dict(R=SR, xb=self.xs, xap=self.xs[:, :], t=None, do_q=True, do_band=True, outs=outs,
                          vcol=self.cf[0:SR, C_ONE:C_ONE + 1], dst=dst))
        for f in prep_tile(0, tiles[0]["R"], tiles[0]["xb"], tiles[0]["xap"]):
            f()
        for k, td in enumerate(tiles):
            hook = None
            hook0 = None
            if k + 1 < len(tiles):
                nt = tiles[k + 1]
                cell = {}

                def hook0(k=k, nt=nt, cell=cell):
                    cell["tb"] = prep_tile(k + 1, nt["R"], nt["xb"], nt["xap"])
                    cell["tb"][0]()
                    cell["tb"][1]()
                    cell["tb"][2]()

                def hook(cell=cell):
                    cell["tb"][3]()
            do_tile(k, td["R"], td["t"], td["do_q"], td["do_band"], td["outs"], td["vcol"], td["dst"], hook, hook0)
        k = len(tiles) - 1
        lfb = lf[k % 2]
        S.op(STOREQ, "dma_start", reads=[lfb], writes=[self.sLF], dma=f"lf_st{k % 2}", out=self.sLF[:, :], in_=lfb[0:SR, :])


    def run_steps(self, steps, delay=0, ahead=1):
        if not steps:
            return
        for j in range(min(ahead, len(steps))):
            if steps[j].get("pre"):
                steps[j]["pre"]()
            steps[j]["qk"]()
        pending = []
        for i, st in enumerate(steps):
            if i + ahead < len(steps):
                if steps[i + ahead].get("pre"):
                    steps[i + ahead]["pre"]()
                steps[i + ahead]["qk"]()
            st["ex"]()
            while pending and pending[0][0] <= i:
                pending.pop(0)[1]()
            st["pv"]()
            if st.get("post"):
                if delay == 0:
                    st["post"]()
                else:
                    pending.append((i + delay, st["post"]))
        for _, f in pending:
            f()

    def phase2a(self):
        S, A, ps = self.S, self.A, self.ps
        S.barrier()
        A.reset()
        KT = A.ring("KT", 2, [128, 8192], BF16)
        VAp = A.ring("VAp", 2, [128, NCTX, 256], BF16)
        for b in VAp:
            S.op("dve", "memset", writes=[b], ap=b[:], constant=0.0)
        Qh = A.ring("Qh", 2, [128, 512], BF16)
        rrow = A.ring("rrow", 2, [65, 512], F32)
        pT = A.ring("pT", 4, [128, 512], BF16)
        biasT = A.ring("biasT", 2, [128, NCTX], F32)
        fin = self.alloc_fin(ps[4])
        psS = [ps[0], ps[1], ps[5], ps[6]]
        psO = [ps[2], ps[3]]
        for b in KT:
            S.op("dve", "memset", writes=[b], ap=b[64:128, :], constant=0.0)
            S.op("dve", "memset", writes=[b], ap=b[64:65, :], constant=1.0)
        for b in Qh:
            S.op("dve", "memset", writes=[b], ap=b[64:128, :], constant=0.0)
        qtiles = [(0, 128)] + [(128 + 512 * j, 512) for j in range(8)]
        steps = []
        cnt = {"q": 0, "h": 0, "s": 0}
        for p in range(4):
            va = VAp[p % 2]

            def pre_pair(p=p, va=va):
                for hh in range(4):
                    for s2 in range(2):
                        S.op(LOADQ, "dma_start", reads=[self.VA], writes=[va], dma=f"va{p % 2}", out=va[:, hh * 16:(hh + 1) * 16, s2 * 128:s2 * 128 + 66],
                             in_=self.VA[hh * 16:(hh + 1) * 16, :, p * 132 + s2 * 66:p * 132 + s2 * 66 + 66].rearrange("t k c -> k t c"))
            for s in range(2):
                h = 2 * p + s
                kt = KT[h % 2]

                def pre_head(h=h, kt=kt):
                    S.op(LOADQ, "dma_start", reads=[self.KAT], writes=[kt], dma=f"kt{h % 2}", out=kt[0:64, :], in_=self.KAT[h, :, :])
                first_h = True
                for (col0, N) in qtiles:
                    hk = cnt["h"]
                    cnt["h"] += 1
                    Q = Qh[hk % 2]
                    rr = rrow[hk % 2]
                    bt = biasT[hk % 2]
                    po = psO[hk % 2]
                    kb0 = T0 + col0 // 128
                    nkb = kb0 + N // 128

                    def pre_q(h=h, col0=col0, N=N, Q=Q, rr=rr, bt=bt, nkb=nkb, hk=hk):
                        S.op(LOADQ, "dma_start", reads=[self.QAT], writes=[Q], dma=f"qh{hk % 2}", out=Q[0:64, 0:N], in_=self.QAT[h, :, col0:col0 + N])
                        S.op(LOADQ, "dma_start", reads=[self.CQ], writes=[rr], dma=f"rr{hk % 2}", out=rr[64:65, 0:N], in_=self.CQ[h:h + 1, col0:col0 + N])
                        S.op("dve", "tensor_scalar", reads=[rr, self.CARRY], writes=[Q], out=Q[64:65, 0:N], in0=rr[64:65, 0:N],
                             scalar1=self.CARRY[64:65, nkb, h:h + 1], scalar2=None, op0=ALU.subtract)
                        S.op("dve", "tensor_scalar", reads=[self.CALL, self.CARRY], writes=[bt], out=bt[:, 0:nkb],
                             in0=self.CALL[:, 0:nkb, h], scalar1=self.CARRY[:, nkb, h:h + 1], scalar2=-1.0,
                             op0=ALU.subtract, op1=ALU.mult)
                    for kb in range(nkb):
                        sk = cnt["s"]
                        cnt["s"] += 1
                        pS = psS[sk % 4]
                        pt = pT[sk % 4]
                        c0 = max(0, kb - kb0) * 128
                        pres = []
                        if kb == 0:
                            if first_h:
                                if s == 0:
                                    pres.append(pre_pair)
                                pres.append(pre_head)
                                first_h = False
                            pres.append(pre_q)

                        def qk(kb=kb, kb0=kb0, c0=c0, N=N, pS=pS, kt=kt, Q=Q):
                            lh = kt[:, kb * 128:(kb + 1) * 128]
                            if kb < kb0:
                                S.op("pe", "matmul", reads=[kt, Q], writes=[pS], out=pS[:, 0:N], lhsT=lh, rhs=Q[:, 0:N], start=True, stop=True)
                            else:
                                S.op("pe", "matmul", reads=[kt, Q], writes=[pS], out=pS[:, c0:c0 + 128], lhsT=lh, rhs=Q[:, c0:c0 + 128],
                                     start=True, stop=False)
                                S.op("pe", "matmul", reads=[self.identb, self.trimb], writes=[pS], out=pS[:, c0:c0 + 128], lhsT=self.identb[:, :],
                                     rhs=self.trimb[:, :], start=False, stop=True)
                                if c0 + 128 < N:
                                    S.op("pe", "matmul", reads=[kt, Q], writes=[pS], out=pS[:, c0 + 128:N], lhsT=lh, rhs=Q[:, c0 + 128:N],
                                         start=True, stop=True)

                        def ex(kb=kb, c0=c0, N=N, pS=pS, pt=pt, bt=bt):
                            S.op("act", "activation", reads=[pS, bt], writes=[pt], out=pt[:, c0:N], in_=pS[:, c0:N], func=AF.Exp,
                                 bias=bt[:, kb:kb + 1], scale=1.0)

                        def pv(kb=kb, c0=c0, N=N, pt=pt, po=po, va=va, s=s, nkb=nkb):
                            S.op("pe", "matmul", reads=[va, pt], writes=[po], out=po[:, c0:N], lhsT=va[:, kb, s * 128:(s + 1) * 128],
                                 rhs=pt[:, c0:N], start=(kb == 0), stop=(kb == nkb - 1))
                        st = {"qk": qk, "ex": ex, "pv": pv}
                        if pres:
                            st["pre"] = (lambda pres=pres: [f() for f in pres])
                        if kb == nkb - 1:
                            def post(po=po, N=N, h=h, col0=col0, hk=hk):
                                def dst(ao):
                                    S.op(STOREQ, "dma_start", reads=[ao], writes=[self.AOT], dma=f"ao{hk % 2}", out=self.AOT[h // 2, (h % 2) * 64:(h % 2) * 64 + 64, col0:col0 + N],
                                         in_=ao[0:64, 0:N])
                                self.finalize(po, N, dst, hk, fin, rec_eng="dve")
                            st["post"] = post
                        steps.append(st)
        self.run_steps(steps, delay=4, ahead=3)

    def phase2b(self):
        S, A, ps = self.S, self.A, self.ps
        S.barrier()
        A.reset()
        BBf = A.alloc("BBf", [128, 5, 8, 128], F32)
        BB = A.alloc("BB", [128, 8, 640], BF16)
        S.op(LOADQ, "dma_start", reads=[self.bb], writes=[BBf], dma="bbf", out=BBf[:].rearrange("p a b c -> p (a b c)"), in_=self.bb[:, :])
        for kb in range(5):
            for h in range(8):
                S.op("dve", "tensor_tensor", reads=[BBf, self.cf], writes=[BB], out=BB[:, h, kb * 128:(kb + 1) * 128], in0=BBf[:, kb, h, :],
                     in1=self.cf[:, C_BMASK + kb * 128:C_BMASK + (kb + 1) * 128], op=ALU.add)
        KB = A.ring("KB", 2, [128, NBT * 128], BF16)
        VBp = A.ring("VBp", 2, [128, NBT, 132], BF16)
        Qe = A.ring("Qe", 2, [128, 512], BF16)
        Qo = A.ring("Qo", 2, [128, 512], BF16)
        pTA = A.ring("pTA", 3, [128, 512], BF16)
        pTB = A.ring("pTB", 3, [128, 128], BF16)
        fin = self.alloc_fin(ps[6])
        psSA = [ps[0], ps[1]]
        psSB = [ps[2], ps[3]]
        psO = [ps[4], ps[5]]
        for b in Qe:
            S.op("dve", "memset", writes=[b], ap=b[64:128, :], constant=0.0)
        for b in Qo:
            S.op("dve", "memset", writes=[b], ap=b[0:64, :], constant=0.0)
        quads = [(4 * i, 4) for i in range(8)] + [(32, 1)]
        steps = []
        cnt = {"q": 0, "h": 0, "s": 0}
        for p in range(4):
            kbuf = KB[p % 2]
            vb = VBp[p % 2]

            def pre_pair(p=p, kbuf=kbuf, vb=vb):
                S.op(LOADQ, "dma_start", reads=[self.KBT], writes=[kbuf], dma=f"kb{p % 2}", out=kbuf[:, :], in_=self.KBT[p, :, :])
                for hh in range(2):
                    lo, hi = (0, 19) if hh == 0 else (19, NBT)
                    S.op(LOADQ, "dma_start", reads=[self.VB], writes=[vb], dma=f"vb{p % 2}", out=vb[:, lo:hi, :],
                         in_=self.VB[lo:hi, :, p * 132:(p + 1) * 132].rearrange("t k c -> k t c"))
            first_pair = True
            for (g0, ng) in quads:
                N = 128 * ng
                col0 = g0 * 128
                qk_i = cnt["q"]
                cnt["q"] += 1
                qe = Qe[qk_i % 2]
                qo = Qo[qk_i % 2]

                def pre_q(p=p, col0=col0, N=N, qe=qe, qo=qo, qk_i=qk_i):
                    S.op(LOADQ, "dma_start", reads=[self.QBT], writes=[qe], dma=f"qe{qk_i % 2}", out=qe[0:64, 0:N],
                         in_=self.QBT[p, 0:64, col0:col0 + N])
                    S.op(LOADQ, "dma_start", reads=[self.QBT], writes=[qo], dma=f"qo{qk_i % 2}", out=qo[64:128, 0:N],
                         in_=self.QBT[p, 64:128, col0:col0 + N])
                first_q = True
                for s in range(2):
                    h = 2 * p + s
                    Q = qe if s == 0 else qo
                    hk = cnt["h"]
                    cnt["h"] += 1
                    po = psO[hk % 2]
                    for gi in range(ng):
                        g = g0 + gi
                        sk = cnt["s"]
                        cnt["s"] += 1
                        pA = psSA[sk % 2]
                        pB = psSB[sk % 2]
                        ptA = pTA[sk % 3]
                        ptB = pTB[sk % 3]
                        pres = []
                        if gi == 0 and first_q:
                            if first_pair:
                                pres.append(pre_pair)
                            pres.append(pre_q)
                        first_q = False
                        first_pair = False

                        def qk(g=g, gi=gi, h=h, pA=pA, pB=pB, kbuf=kbuf, Q=Q):
                            S.op("pe", "matmul", reads=[self.identb, BB], writes=[pA], out=pA[:, 0:512], lhsT=self.identb[:, :],
                                 rhs=BB[:, h, 0:512], start=True, stop=False)
                            for b in range(4):
                                S.op("pe", "matmul", reads=[kbuf, Q], writes=[pA], out=pA[:, b * 128:(b + 1) * 128],
                                     lhsT=kbuf[:, (g + b) * 128:(g + b + 1) * 128], rhs=Q[:, gi * 128:(gi + 1) * 128], start=False, stop=(b == 3))
                            S.op("pe", "matmul", reads=[self.identb, BB], writes=[pB], out=pB[:, 0:128], lhsT=self.identb[:, :],
                                 rhs=BB[:, h, 512:640], start=True, stop=False)
                            S.op("pe", "matmul", reads=[kbuf, Q], writes=[pB], out=pB[:, 0:128],
                                 lhsT=kbuf[:, (g + 4) * 128:(g + 5) * 128], rhs=Q[:, gi * 128:(gi + 1) * 128], start=False, stop=True)

                        def ex(pA=pA, pB=pB, ptA=ptA, ptB=ptB):
                            S.op("act", "activation", reads=[pA], writes=[ptA], out=ptA[:, :], in_=pA[:, 0:512], func=AF.Exp)
                            S.op("act", "activation", reads=[pB], writes=[ptB], out=ptB[:, :], in_=pB[:, 0:128], func=AF.Exp)

                        def pv(g=g, gi=gi, s=s, po=po, ptA=ptA, ptB=ptB, vb=vb, N=N, ng=ng):
                            if gi == 0:
                                S.op("pe", "matmul", reads=[self.zb], writes=[po], out=po[0:65, 0:N], lhsT=self.zb[0:1, 0:65],
                                     rhs=self.zb[0:1, 0:N], start=True, stop=False)
                            for b in range(5):
                                rhs = ptA[:, b * 128:(b + 1) * 128] if b < 4 else ptB[:, :]
                                S.op("pe", "matmul", reads=[vb, ptA, ptB], writes=[po], out=po[0:65, gi * 128:(gi + 1) * 128],
                                     lhsT=vb[:, g + b, s * 66:s * 66 + 65], rhs=rhs, start=False, stop=(b == 4 and gi == ng - 1))
                        st = {"qk": qk, "ex": ex, "pv": pv}
                        if pres:
                            st["pre"] = (lambda pres=pres: [f() for f in pres])
                        if gi == ng - 1:
                            def post(po=po, N=N, h=h, col0=col0, hk=hk):
                                def dst(ao):
                                    S.op(STOREQ, "dma_start", reads=[ao], writes=[self.AOT], dma=f"ao{hk % 2}", out=self.AOT[4 + h // 2, (h % 2) * 64:(h % 2) * 64 + 64, col0:col0 + N],
                                         in_=ao[0:64, 0:N])
                                self.finalize(po, N, dst, hk, fin)
                            st["post"] = post
                        steps.append(st)
        self.run_steps(steps, delay=2)

    def phase2c(self):
        S, A, ps = self.S, self.A, self.ps
        S.barrier()
        A.reset()
        kc = A.ring("kc", 2, [128, 8, 512], F32)
        vc = A.ring("vc", 2, [128, 8, 512], F32)
        KTs = A.alloc("KTs", [128, 4, 4112], BF16)
        Vs = A.alloc("Vs", [128, 33, 8, 66], BF16)
        Qbd = A.alloc("Qbd", [128, 4, 32], BF16)
        lfa = A.alloc("lfa", [128, 33, 8], F32)
        CAR = A.alloc("CAR", [128, 34, 8], F32)
        cfull = A.alloc("cfull", [128, 33, 8], F32)
        biasS = A.alloc("biasS", [128, 33, 8], F32)
        biasF = A.alloc("biasF", [128, 33, 8, 16], F32)
        tri16b = A.alloc("tri16b", [16, 32], BF16)
        pT = A.ring("pTs", 2, [128, 512], BF16)
        BSf = A.alloc("BSf", [128, 5, 8, 16], F32)
        BSb = A.alloc("BSb", [128, 5, 8, 16], BF16)
        fin = self.alloc_fin(ps[7])
        psK = [ps[0], ps[1]]
        psS = [ps[2], ps[3]]
        psO = ps[4]
        psC = ps[5]
        psC2 = ps[6]
        S.op("dve", "tensor_copy", reads=[self.cf], writes=[tri16b], out=tri16b[:, 0:16], in_=self.cf[0:16, C_TRI16:C_TRI16 + 16])
        S.op("dve", "tensor_copy", reads=[self.cf], writes=[tri16b], out=tri16b[:, 16:32], in_=self.cf[0:16, C_TRI16:C_TRI16 + 16])
        S.op("dve", "memset", writes=[Vs], ap=Vs[:, :, :, 64], constant=1.0)
        S.op(LOADQ, "dma_start", reads=[self.bs], writes=[BSf], dma="bsf", out=BSf[:].rearrange("p a b c -> p (a b c)"), in_=self.bs[:, :])
        S.op("dve", "tensor_copy", reads=[BSf], writes=[BSb], out=BSb[:], in_=BSf[:])
        kcn = [0]
        ekn = [0]
        skn = [0]

        def load_kv(q, srck, srcv, nblk, blk0, ktcol0, kt_dst, v_dst):
            for ch in range(0, nblk, 8):
                nb = min(8, nblk - ch)
                kb_ = kc[kcn[0] % 2]
                vb_ = vc[kcn[0] % 2]
                S.op(LOADQ, "dma_start", reads=[srck], writes=[kb_], dma=f"kc{kcn[0] % 2}", out=kb_[:, 0:nb, :],
                     in_=srck[q, ch * 128:(ch + nb) * 128, :].rearrange("(b k) c -> k b c", k=128))
                S.op(LOADQ, "dma_start", reads=[srcv], writes=[vb_], dma=f"vc{kcn[0] % 2}", out=vb_[:, 0:nb, :],
                     in_=srcv[q, ch * 128:(ch + nb) * 128, :].rearrange("(b k) c -> k b c", k=128))
                kcn[0] += 1
                for b in range(nb):
                    blk = ch + b
                    pk = psK[ekn[0] % 2]
                    pkv = pk[:].rearrange("p (c r) -> p c r", c=4)
                    for p in range(4):
                        S.op("pe", "transpose", reads=[kb_, self.cf], writes=[pk], out=pkv[:, p, :], in_=kb_[:, b, p * 128:(p + 1) * 128],
                             identity=self.cf[:, C_ID:C_ID + 128])
                    c0 = ktcol0 + blk * 128
                    if ekn[0] % 2 == 0:
                        S.op("act", "copy", reads=[pk], writes=[kt_dst], out=kt_dst[:, :, c0:c0 + 128], in_=pkv[:, :, :])
                    else:
                        S.op("dve", "tensor_copy", reads=[pk], writes=[kt_dst], out=kt_dst[:, :, c0:c0 + 128], in_=pkv[:, :, :])
                    ekn[0] += 1
                    if ekn[0] % 2 == 0:
                        S.op("dve", "tensor_copy", reads=[vb_], writes=[v_dst], out=v_dst[:, blk0 + blk, :, 0:64],
                             in_=vb_[:, b, :].rearrange("p (h d) -> p h d", h=8))
                    else:
                        S.op("act", "copy", reads=[vb_], writes=[v_dst], out=v_dst[:, blk0 + blk, :, 0:64],
                             in_=vb_[:, b, :].rearrange("p (h d) -> p h d", h=8))

        S.op("pe", "matmul", reads=[self.zb], writes=[psO], out=psO[0:65, 0:512], lhsT=self.zb[0:1, 0:65], rhs=self.zb[0:1, 0:512],
             start=True, stop=False)
        for q in range(4):
            r0 = q * 16
            load_kv(q, self.cak, self.cav, 32, 0, 0, KTs, Vs)
            for p in range(4):
                S.op(LOADQ, "dma_start", reads=[self.sKAT], writes=[KTs], dma="kts_new", out=KTs[0:64, p, 4096:4112], in_=self.sKAT[2 * p, :, r0:r0 + 16])
                S.op(LOADQ, "dma_start", reads=[self.sKAT], writes=[KTs], dma="kts_new", out=KTs[64:128, p, 4096:4112], in_=self.sKAT[2 * p + 1, :, r0:r0 + 16])
            S.op(LOADQ, "dma_start", reads=[self.sVA], writes=[Vs], dma="vs_new", out=Vs[0:16, 32, :, :].rearrange("p h d -> p (h d)"),
                 in_=self.sVA[r0:r0 + 16, :])
            S.op("dve", "memset", writes=[Qbd], ap=Qbd[:], constant=0.0)
            for p in range(4):
                S.op(LOADQ, "dma_start", reads=[self.sQAT], writes=[Qbd], dma="qbd", out=Qbd[0:64, p, 0:16], in_=self.sQAT[2 * p, :, r0:r0 + 16])
                S.op(LOADQ, "dma_start", reads=[self.sQAT], writes=[Qbd], dma="qbd", out=Qbd[64:128, p, 16:32], in_=self.sQAT[2 * p + 1, :, r0:r0 + 16])
            S.op("dve", "memset", writes=[lfa], ap=lfa[:, 32, :], constant=0.0)
            S.op(LOADQ, "dma_start", reads=[self.calf], writes=[lfa], dma="lfa", out=lfa[:, 0:32, :],
                 in_=self.calf[q, :, :].rearrange("(b k) h -> k b h", k=128))
            S.op(LOADQ, "dma_start", reads=[self.sLF], writes=[lfa], dma="lfa", out=lfa[0:16, 32, :], in_=self.sLF[r0:r0 + 16, :])
            lfa2 = lfa[:].rearrange("p b h -> p (b h)")
            S.op("pe", "matmul", reads=[self.cf, lfa], writes=[psC], out=psC[:, 0:264], lhsT=self.cf[:, C_TRIU:C_TRIU + 128], rhs=lfa2,
                 start=True, stop=True)
            S.op("pe", "matmul", reads=[self.cf, lfa], writes=[psC2], out=psC2[:, 0:264], lhsT=self.cf[:, C_ONE:C_ONE + 128], rhs=lfa2,
                 start=True, stop=True)
            S.op("dve", "memset", writes=[CAR], ap=CAR[:, 0, :], constant=0.0)
            for b in range(33):
                S.op("dve", "tensor_tensor", reads=[psC2, CAR], writes=[CAR], out=CAR[:, b + 1, :], in0=psC2[:, b * 8:(b + 1) * 8],
                     in1=CAR[:, b, :], op=ALU.add)
            S.op("dve", "tensor_tensor", reads=[psC, CAR], writes=[cfull], out=cfull[:].rearrange("p b h -> p (b h)"), in0=psC[:, 0:264],
                 in1=CAR[:, 0:33, :].rearrange("p b h -> p (b h)"), op=ALU.add)
            for h in range(8):
                S.op("dve", "tensor_scalar", reads=[cfull, CAR], writes=[biasS], out=biasS[:, :, h], in0=cfull[:, :, h],
                     scalar1=CAR[:, 33, h:h + 1], scalar2=-1.0, op0=ALU.subtract, op1=ALU.mult)
            for t_ in range(16):
                S.op("dve", "tensor_copy", reads=[biasS], writes=[biasF], out=biasF[:, :, :, t_], in_=biasS[:, :, :])
            for p in range(4):
                for ch in range(3):
                    b_lo = ch * 16
                    b_hi = min(33, b_lo + 16)
                    pS = psS[skn[0] % 2]
                    pt = pT[skn[0] % 2]
                    skn[0] += 1
                    for b in range(b_lo, b_hi):
                        cc = (b - b_lo) * 32
                        bfv = biasF[:].rearrange("p b h t -> p b (h t)")
                        if b < 32:
                            S.op("pe", "matmul", reads=[KTs, Qbd], writes=[pS], out=pS[:, cc:cc + 32], lhsT=KTs[:, p, b * 128:(b + 1) * 128],
                                 rhs=Qbd[:, p, :], start=True, stop=False)
                            S.op("pe", "matmul", reads=[self.cf, biasF], writes=[pS], out=pS[:, cc:cc + 32], lhsT=self.cf[:, C_ID:C_ID + 128],
                                 rhs=bfv[:, b, p * 32:(p + 1) * 32], start=False, stop=True)
                        else:
                            S.op("pe", "matmul", reads=[KTs, Qbd], writes=[pS], out=pS[0:16, cc:cc + 32], lhsT=KTs[:, p, 4096:4112],
                                 rhs=Qbd[:, p, :], start=True, stop=False)
                            S.op("pe", "matmul", reads=[self.identb, tri16b], writes=[pS], out=pS[0:16, cc:cc + 32], lhsT=self.identb[0:16, 0:16],
                                 rhs=tri16b[:, :], start=False, stop=False)
                            S.op("pe", "matmul", reads=[self.cf, biasF], writes=[pS], out=pS[0:16, cc:cc + 32], lhsT=self.cf[0:16, C_ID:C_ID + 16],
                                 rhs=bfv[0:16, b, p * 32:(p + 1) * 32], start=False, stop=True)
                    nfull = min(b_hi, 32) - b_lo
                    if nfull > 0:
                        S.op("act", "activation", reads=[pS], writes=[pt], out=pt[:, 0:nfull * 32], in_=pS[:, 0:nfull * 32], func=AF.Exp)
                    if b_hi == 33:
                        cc = (32 - b_lo) * 32
                        S.op("act", "activation", reads=[pS], writes=[pt], out=pt[0:16, cc:cc + 32], in_=pS[0:16, cc:cc + 32], func=AF.Exp)
                    for b in range(b_lo, b_hi):
                        cc = (b - b_lo) * 32
                        nk = 128 if b < 32 else 16
                        for s in range(2):
                            h = 2 * p + s
                            oc = (q * 8 + h) * 16
                            S.op("pe", "matmul", reads=[Vs, pt], writes=[psO], out=psO[0:65, oc:oc + 16], lhsT=Vs[0:nk, b, h, 0:65],
                                 rhs=pt[0:nk, cc + s * 16:cc + s * 16 + 16], start=False, stop=False)

        S.op("pe", "matmul", reads=[self.zb], writes=[psO], out=psO[0:65, 0:512], lhsT=self.zb[0:1, 0:65], rhs=self.zb[0:1, 0:512],
             start=False, stop=True)

        def dst_fox(ao):
            S.op(STOREQ, "dma_start", reads=[ao], writes=[self.sAOT], dma="ao0", out=self.sAOT[0:4, :, :].rearrange("p (two d) (q t) -> d q (p two) t", two=2, q=4),
                 in_=ao[0:64, 0:512].rearrange("d (q h t) -> d q h t", q=4, h=8))
        self.finalize(psO, 512, dst_fox, 0, fin)

        KBs = A.alloc("KBs", [128, 4, 528], BF16)
        Vb = A.alloc("Vb", [128, 5, 8, 66], BF16)
        S.op("dve", "memset", writes=[Vb], ap=Vb[:, :, :, 64], constant=1.0)
        for q in range(4):
            r0 = q * 16
            load_kv(q, self.cbk, self.cbv, 4, 0, 0, KBs, Vb)
            for p in range(4):
                S.op(LOADQ, "dma_start", reads=[self.sKBT], writes=[KBs], dma="kts_new", out=KBs[:, p, 512:528], in_=self.sKBT[p, :, r0:r0 + 16])
            S.op(LOADQ, "dma_start", reads=[self.sVB], writes=[Vb], dma="vs_new", out=Vb[0:16, 4, :, :].rearrange("p h d -> p (h d)"),
                 in_=self.sVB[r0:r0 + 16, :])
            S.op("dve", "memset", writes=[Qbd], ap=Qbd[:], constant=0.0)
            for p in range(4):
                S.op(LOADQ, "dma_start", reads=[self.sQBT], writes=[Qbd], dma="qbd", out=Qbd[0:64, p, 0:16], in_=self.sQBT[p, 0:64, r0:r0 + 16])
                S.op(LOADQ, "dma_start", reads=[self.sQBT], writes=[Qbd], dma="qbd", out=Qbd[64:128, p, 16:32], in_=self.sQBT[p, 64:128, r0:r0 + 16])
            for p in range(4):
                pS = psS[skn[0] % 2]
                pt = pT[skn[0] % 2]
                skn[0] += 1
                for b in range(5):
                    nk = 128 if b < 4 else 16
                    cc = b * 32
                    S.op("pe", "matmul", reads=[self.identb, BSb], writes=[pS], out=pS[0:nk, cc:cc + 32], lhsT=self.identb[0:nk, 0:nk],
                         rhs=BSb[:].rearrange("p a b c -> p a (b c)")[0:nk, b, p * 32:(p + 1) * 32], start=True, stop=False)
                    S.op("pe", "matmul", reads=[KBs, Qbd], writes=[pS], out=pS[0:nk, cc:cc + 32], lhsT=KBs[:, p, b * 128:b * 128 + nk],
                         rhs=Qbd[:, p, :], start=False, stop=True)
                S.op("act", "activation", reads=[pS], writes=[pt], out=pt[:, 0:128], in_=pS[:, 0:128], func=AF.Exp)
                S.op("act", "activation", reads=[pS], writes=[pt], out=pt[0:16, 128:160], in_=pS[0:16, 128:160], func=AF.Exp)
                for s in range(2):
                    h = 2 * p + s
                    oc = (q * 8 + h) * 16
                    for b in range(5):
                        nk = 128 if b < 4 else 16
                        cc = b * 32 + s * 16
                        S.op("pe", "matmul", reads=[Vb, pt], writes=[psO], out=psO[0:65, oc:oc + 16], lhsT=Vb[0:nk, b, h, 0:65],
                             rhs=pt[0:nk, cc:cc + 16], start=(b == 0), stop=(b == 4))

        def dst_band(ao):
            S.op(STOREQ, "dma_start", reads=[ao], writes=[self.sAOT], dma="ao1", out=self.sAOT[4:8, :, :].rearrange("p (two d) (q t) -> d q (p two) t", two=2, q=4),
                 in_=ao[0:64, 0:512].rearrange("d (q h t) -> d q h t", q=4, h=8))
        self.finalize(psO, 512, dst_band, 1, fin)

    def row_tiles(self):
        tl = [(128, i * 128, i) for i in range(NLOC)]
        tl.append((SR, TLOC, None))
        return tl

    def phase_outproj(self, layer):
        S, A, ps = self.S, self.A, self.ps
        S.barrier()
        A.reset()
        Wo = A.alloc("Wo", [128, 8, D], BF16)
        wsrc = self.w_out[layer]
        for hh in range(4):
            self.load_w_cast(Wo, Wo[:, hh * 2:(hh + 1) * 2, :], wsrc[hh * 256:(hh + 1) * 256, :].rearrange("(c p) n -> p c n", p=128), wsrc, "wo")
        aot = A.ring("aot", 3, [128, 8, 128], BF16)
        xt = A.ring("xt", 3, [128, D], F32)
        xo = A.ring("xo", 3, [128, D], F32)
        psY = [ps[0], ps[1], ps[2], ps[3]]
        Xout = self.X1
        k = 0
        for (R, row0, loc) in self.row_tiles():
            if layer == 1 and loc == 0:
                continue
            a = aot[k % 3]
            x = xt[k % 3]
            o = xo[k % 3]
            if loc is not None:
                S.op(LOADQ, "dma_start", reads=[self.AOT], writes=[a], dma=f"aot{k % 3}", out=a[:, :, 0:R],
                     in_=self.AOT[:, :, row0:row0 + R].rearrange("c p r -> p c r"))
                if layer == 0:
                    xsrc, xap = self.xc, self.xc[(T0 + loc) * 128:(T0 + loc + 1) * 128, :]
                else:
                    xsrc, xap = self.X2, self.X2[row0:row0 + R, :]
            else:
                S.op(LOADQ, "dma_start", reads=[self.sAOT], writes=[a], dma=f"aot{k % 3}", out=a[:, :, 0:R],
                     in_=self.sAOT[:, :, :].rearrange("c p r -> p c r"))
                if layer == 0:
                    xsrc, xap = self.xs, self.xs[:, :]
                else:
                    xsrc, xap = self.X2, self.X2[row0:row0 + R, :]
            S.op(LOADQ, "dma_start", reads=[xsrc], writes=[x], dma=f"xt{k % 3}", out=x[0:R, :], in_=xap)
            for cg in range(2):
                py = psY[(2 * k + cg) % 4]
                for h in range(8):
                    S.op("pe", "matmul", reads=[a, Wo], writes=[py], out=py[0:R, :], lhsT=a[:, h, 0:R], rhs=Wo[:, h, cg * 512:(cg + 1) * 512],
                         start=(h == 0), stop=(h == 7))
                S.op("dve", "tensor_tensor", reads=[py, x], writes=[o], out=o[0:R, cg * 512:(cg + 1) * 512], in0=py[0:R, :],
                     in1=x[0:R, cg * 512:(cg + 1) * 512], op=ALU.add)
            S.op(STOREQ, "dma_start", reads=[o], writes=[Xout], dma=f"xo{k % 3}", out=Xout[row0:row0 + R, :], in_=o[0:R, :])
            k += 1

    def phase_ffn(self, layer):
        S, A, ps = self.S, self.A, self.ps
        S.barrier()
        A.reset()
        W1 = A.alloc("W1", [128, 8, 2 * DFF], BF16)
        W2 = A.alloc("W2", [128, 22, D], BF16)
        for c in range(8):
            for hf in range(2):
                self.load_w_cast(W1, W1[:, c, hf * DFF:(hf + 1) * DFF], self.w_ffn_in[layer, c * 128:(c + 1) * 128, hf * DFF:(hf + 1) * DFF],
                                 self.w_ffn_in, "w1")
        for j in range(22):
            self.load_w_cast(W2, W2[:, j, :], self.w_ffn_out[layer, j * 128:(j + 1) * 128, :], self.w_ffn_out, "w2")
        gF = A.alloc("gF", [128, D], F32)
        S.op(LOADQ, "dma_start", reads=[self.norm_ffn], writes=[gF], dma="g0", out=gF[:], in_=self.norm_ffn[layer, :].partition_broadcast(128))
        last = (layer == 1)
        if last:
            gN = A.alloc("gN", [128, D], F32)
            S.op(LOADQ, "dma_start", reads=[self.norm_final], writes=[gN], dma="bf", out=gN[:], in_=self.norm_final[0, :].partition_broadcast(128))
        xt = A.ring("xt", 4, [128, D], F32)
        junk = A.alloc("junk", [128, D], BF16)
        ss = A.ring("ss", 4, [128, 1], F32)
        hbr = A.ring("hb", 2, [128, D], BF16)
        hT = A.ring("hT", 2, [128, 8, 256], BF16)
        actT = A.ring("actT", 1, [128, 22, 256], BF16)
        sg = A.ring("sg", 2, [128, 256], F32)
        psT = ps[0]
        psA = [ps[1], ps[2]]
        psB = [ps[3], ps[4]]
        psY = [ps[5], ps[6]]
        Xin, Xout = self.X1, self.X2
        tiles = [t for t in self.row_tiles() if not (last and t[2] == 0)]
        supers = []
        i = 0
        while i < len(tiles):
            if tiles[i][2] is not None and i + 1 < len(tiles) and tiles[i + 1][2] is not None:
                supers.append([tiles[i], tiles[i + 1]])
                i += 2
            else:
                supers.append([tiles[i]])
                i += 1
        kk = [0]
        jn = 0
        yn = 0

        def prep(si):
            sup = supers[si]
            hTb = hT[si % 2]
            xs_ = []
            tbs = []
            off = 0
            for (R, row0, loc) in sup:
                k = kk[0]
                x = xt[k % 4]
                S.op(LOADQ, "dma_start", reads=[Xin], writes=[x], dma=f"xt{k % 4}", out=x[0:R, :], in_=Xin[row0:row0 + R, :])
                tb = self.norm_T(x, R, gF, hbr, hTb[:, :, off:off + R], hTb, psT, k, junk, ss, evac_eng="dve", defer_T=True)
                tbs.append(tb)
                xs_.append((x, R, row0, loc, off, k % 4))
                off += R
                kk[0] += 1
            return xs_, off, tbs

        nxt = prep(0)
        for stg in range(4):
            for f in nxt[2]:
                f[stg]()
        for si, sup in enumerate(supers):
            hTb = hT[si % 2]
            aT = actT[0]
            xs_, N, _ = nxt
            for j in range(22):
                pa = psA[jn % 2]
                pb = psB[jn % 2]
                sgb = sg[jn % 2]
                jn += 1
                for c in range(8):
                    S.op("pe", "matmul", reads=[W1, hTb], writes=[pa], out=pa[:, 0:N], lhsT=W1[:, c, j * 128:(j + 1) * 128], rhs=hTb[:, c, 0:N],
                         start=(c == 0), stop=(c == 7))
                for c in range(8):
                    S.op("pe", "matmul", reads=[W1, hTb], writes=[pb], out=pb[:, 0:N], lhsT=W1[:, c, DFF + j * 128:DFF + (j + 1) * 128],
                         rhs=hTb[:, c, 0:N], start=(c == 0), stop=(c == 7))
                S.op("act", "activation", reads=[pa], writes=[sgb], out=sgb[:, 0:N], in_=pa[:, 0:N], func=AF.Silu)
                S.op("dve", "tensor_tensor", reads=[pb, sgb], writes=[aT], out=aT[:, j, 0:N], in0=pb[:, 0:N], in1=sgb[:, 0:N], op=ALU.mult)
                if si + 1 < len(supers):
                    if j == 1:
                        nxt = prep(si + 1)
                        for f in nxt[2]:
                            f[0]()
                    if j == 5:
                        for f in nxt[2]:
                            f[1]()
                    if j == 8:
                        for f in nxt[2]:
                            f[2]()
                    if j == 14:
                        for f in nxt[2]:
                            f[3]()
            for (x, R, row0, loc, off, slot) in xs_:
                o = x
                for cg in range(2):
                    py = psY[cg]
                    for j in range(22):
                        S.op("pe", "matmul", reads=[aT, W2], writes=[py], out=py[0:R, :], lhsT=aT[:, j, off:off + R], rhs=W2[:, j, cg * 512:(cg + 1) * 512],
                             start=(j == 0), stop=(j == 21))
                    S.op("dve", "tensor_tensor", reads=[py, x], writes=[o], out=o[0:R, cg * 512:(cg + 1) * 512], in0=py[0:R, :],
                         in1=x[0:R, cg * 512:(cg + 1) * 512], op=ALU.add)
                if not last:
                    S.op(STOREQ, "dma_start", reads=[o], writes=[Xout], dma=f"xo_st{slot}", out=Xout[row0:row0 + R, :], in_=o[0:R, :])
                else:
                    ssb = ss[yn % 2]
                    y = x
                    S.op("dve", "scalar_tensor_tensor", reads=[o], writes=[junk, ssb], out=junk[0:R, :], in0=o[0:R, :], scalar=1.0,
                         in1=o[0:R, :], op0=ALU.mult, op1=ALU.mult, accum_out=ssb[0:R, :])
                    S.op("act", "activation", reads=[ssb, self.epsb], writes=[ssb], out=ssb[0:R, :], in_=ssb[0:R, :], func=AF.Ln,
                         bias=self.epsb[0:R, :], scale=1.0 / D)
                    S.op("act", "activation", reads=[ssb], writes=[ssb], out=ssb[0:R, :], in_=ssb[0:R, :], func=AF.Exp, scale=-0.5)
                    S.op("dve", "scalar_tensor_tensor", reads=[o, ssb, gN], writes=[y], out=y[0:R, :], in0=o[0:R, :], scalar=ssb[0:R, :],
                         in1=gN[0:R, :], op0=ALU.mult, op1=ALU.mult)
                    if loc is not None:
                        dstb, dap = self.o_y, self.o_y[(loc - 1) * 128:loc * 128, :]
                    else:
                        dstb, dap = self.o_ys, self.o_ys[:, :]
                    S.op(STOREQ, "dma_start", reads=[y], writes=[dstb], dma=f"xo_st{slot}", out=dap, in_=y[0:R, :])
                yn += 1

    def phase4(self):
        S, A, ps = self.S, self.A, self.ps
        S.barrier()
        A.reset()
        W = A.alloc("Wi1", [128, 8, 1280], BF16)
        for c in range(8):
            self.load_w_cast(W, W[:, c, :], self.w_in_odd[c * 128:(c + 1) * 128, :], self.w_in_odd, "w0")
        g1 = A.alloc("g1", [128, D], F32)
        S.op(LOADQ, "dma_start", reads=[self.norm_mix], writes=[g1], dma="g0", out=g1[:], in_=self.norm_mix[1, :].partition_broadcast(128))
        ones2 = A.alloc("ones2", [128, 2], F32)
        S.op("dve", "memset", writes=[ones2], ap=ones2[:], constant=1.0)
        xt = A.ring("xt", 3, [128, D], F32)
        junk = A.alloc("junk", [128, D], BF16)
        ss = A.ring("ss", 2, [128, 1], F32)
        hbr = A.ring("hb", 2, [128, D], BF16)
        hT = A.ring("hT", 2, [128, 8, 128], BF16)
        qb = A.ring("qb", 2, [128, D], BF16)
        kvf = A.ring("kvf", 2, [128, 256], F32)
        kb16 = A.ring("kb16", 2, [128, 128], BF16)
        vaug = A.ring("vaug", 2, [128, 2, 66], BF16)
        for vb_ in vaug:
            S.op("dve", "memset", writes=[vb_], ap=vb_[:], constant=0.0)
        qT = A.ring("qT", 2, [128, 8, 128], BF16)
        kT = A.ring("kT", 2, [128, 128], BF16)
        psT = ps[0]
        psG = [ps[1], ps[2], ps[3]]
        psQ = [ps[4], ps[5]]
        psK = ps[6]
        gn = 0
        k = 0
        for (R, row0, loc) in self.row_tiles():
            x = xt[k % 3]
            S.op(LOADQ, "dma_start", reads=[self.X2], writes=[x], dma=f"xt{k % 3}", out=x[0:R, :], in_=self.X2[row0:row0 + R, :])
            hTb = hT[k % 2]
            self.norm_T(x, R, g1, hbr, hTb[:, :, 0:R], hTb, psT, k, junk, ss, evac_eng="act")
            do_q = (loc != 0)
            qbb = qb[k % 2]
            if do_q:
                for cg in range(2):
                    pg = psG[gn % 3]
                    gn += 1
                    for c in range(8):
                        S.op("pe", "matmul", reads=[hTb, W], writes=[pg], out=pg[0:R, :], lhsT=hTb[:, c, 0:R], rhs=W[:, c, cg * 512:(cg + 1) * 512],
                             start=(c == 0), stop=(c == 7))
                    qdst = qbb[0:R, :].rearrange("p (a s d) -> p a s d", a=8, s=2)[:, :, cg, :]
                    psrc = pg[0:R, :].rearrange("p (a d) -> p a d", a=8)
                    if cg == 0:
                        S.op("act", "activation", reads=[pg], writes=[qbb], out=qdst, in_=psrc, func=AF.Copy, scale=0.125)
                    else:
                        S.op("dve", "tensor_scalar", reads=[pg], writes=[qbb], out=qdst, in0=psrc, scalar1=0.125, scalar2=None,
                             op0=ALU.mult)
                pq = psQ[k % 2]
                pqv = pq[:].bitcast(BF16).rearrange("p (a r) -> p a r", a=8)
                for a in range(8):
                    S.op("pe", "transpose", reads=[qbb, self.identb], writes=[pq], out=pqv[:, a, 0:R], in_=qbb[0:R, a * 128:(a + 1) * 128],
                         identity=self.identb[0:R, 0:R])
                qTb = qT[k % 2]
                S.op("dve", "tensor_copy", reads=[pq], writes=[qTb], out=qTb[:, :, 0:R], in_=pqv[:, :, 0:R])
                if loc is not None:
                    dstb, dap = self.QCT, self.QCT[:, :, row0:row0 + R].rearrange("a p r -> p a r")
                else:
                    dstb, dap = self.sQCT, self.sQCT[:, :, :].rearrange("a p r -> p a r")
                S.op(STOREQ, "dma_start", reads=[qTb], writes=[dstb], dma=f"qT_st{k % 2}", out=dap, in_=qTb[:, :, 0:R])
            pg = psG[gn % 3]
            gn += 1
            for c in range(8):
                S.op("pe", "matmul", reads=[hTb, W], writes=[pg], out=pg[0:R, 0:256], lhsT=hTb[:, c, 0:R], rhs=W[:, c, 1024:1280],
                     start=(c == 0), stop=(c == 7))
            kv = kvf[k % 2]
            S.op("act", "copy", reads=[pg], writes=[kv], out=kv[0:R, :], in_=pg[0:R, 0:256])
            if loc is None:
                S.op(STOREQ, "dma_start", reads=[kv], writes=[self.o_sck], dma=f"kv_st{k % 2}", out=self.o_sck[:, :], in_=kv[0:R, 0:128])
                S.op(STOREQ, "dma_start", reads=[kv], writes=[self.o_scv], dma=f"kv_st{k % 2}", out=self.o_scv[:, :], in_=kv[0:R, 128:256])
            elif loc == NLOC - 1:
                S.op(STOREQ, "dma_start", reads=[kv], writes=[self.o_ck], dma=f"kv_st{k % 2}", out=self.o_ck[:, :], in_=kv[0:R, 0:128])
                S.op(STOREQ, "dma_start", reads=[kv], writes=[self.o_cv], dma=f"kv_st{k % 2}", out=self.o_cv[:, :], in_=kv[0:R, 128:256])
            kb_ = kb16[k % 2]
            S.op("dve", "tensor_copy", reads=[kv], writes=[kb_], out=kb_[0:R, :], in_=kv[0:R, 0:128])
            S.op("pe", "transpose", reads=[kb_, self.identb], writes=[psK], out=psK[:].bitcast(BF16)[:, 0:R], in_=kb_[0:R, :],
                 identity=self.identb[0:R, 0:R])
            kTb = kT[k % 2]
            S.op("act", "copy", reads=[psK], writes=[kTb], out=kTb[:, 0:R], in_=psK[:].bitcast(BF16)[:, 0:R])
            if loc is not None:
                S.op(STOREQ, "dma_start", reads=[kTb], writes=[self.KCT], dma=f"kT_st{k % 2}", out=self.KCT[:, row0:row0 + R], in_=kTb[:, 0:R])
                vcol = self.validt[:, T0 + loc:T0 + loc + 1]
            else:
                S.op(STOREQ, "dma_start", reads=[kTb], writes=[self.sKCT], dma=f"kT_st{k % 2}", out=self.sKCT[:, :], in_=kTb[:, 0:R])
                vcol = self.cf[0:SR, C_ONE:C_ONE + 1]
            va = vaug[k % 2]
            S.op("dve", "tensor_scalar", reads=[kv, self.validt], writes=[va], out=va[0:R, :, 0:64],
                 in0=kv[0:R, 128:256].rearrange("p (h d) -> p h d", h=2), scalar1=vcol[0:R, :], scalar2=None, op0=ALU.mult)
            S.op("dve", "tensor_scalar", reads=[ones2, self.validt], writes=[va], out=va[0:R, :, 64], in0=ones2[0:R, :], scalar1=vcol[0:R, :],
                 scalar2=None, op0=ALU.mult)
            if loc is not None:
                S.op(STOREQ, "dma_start", reads=[va], writes=[self.VC], dma=f"va_st{k % 2}", out=self.VC[loc, :, :], in_=va[0:R, :, :].rearrange("p h d -> p (h d)"))
            else:
                S.op(STOREQ, "dma_start", reads=[va], writes=[self.sVC], dma=f"va_st{k % 2}", out=self.sVC[:, :], in_=va[0:R, :, :].rearrange("p h d -> p (h d)"))
            k += 1


    def phase5(self):
        S, A, ps = self.S, self.A, self.ps
        S.barrier()
        A.reset()
        KC = A.alloc("KC", [128, TLOC], BF16)
        VCt = A.alloc("VCt", [128, NLOC, 132], BF16)
        S.op(LOADQ, "dma_start", reads=[self.KCT], writes=[KC], dma="kc_l", out=KC[:, :], in_=self.KCT[:, :])
        S.op(LOADQ, "dma_start", reads=[self.VC], writes=[VCt], dma="vc_l", out=VCt[:, :, :], in_=self.VC[:, :, :].rearrange("t k c -> k t c"))
        TSf = A.alloc("TSf", [128, 2, 16, 128], F32)
        TB = A.alloc("TB", [128, 2, 2048], BF16)
        S.op(LOADQ, "dma_start", reads=[self.ts], writes=[TSf], dma="tsf", out=TSf[:].rearrange("p a b c -> p (a b c)"), in_=self.ts[:, :])
        for kb in range(2):
            for h in range(16):
                S.op("dve", "tensor_tensor", reads=[TSf, self.cf], writes=[TB], out=TB[:, kb, h * 128:(h + 1) * 128], in0=TSf[:, kb, h, :],
                     in1=self.cf[:, C_SMASK + kb * 128:C_SMASK + (kb + 1) * 128], op=ALU.add)
        es = A.alloc("es", [128, 16], F32)
        S.op(LOADQ, "dma_start", reads=[self.sinks], writes=[es], dma="es_l", out=es[:], in_=self.sinks[0, :].partition_broadcast(128))
        S.op("act", "activation", reads=[es], writes=[es], out=es[:], in_=es[:], func=AF.Exp)
        self.es = es
        Qe = A.ring("Qe5", 2, [128, 1024], BF16)
        Qo = A.ring("Qo5", 2, [128, 1024], BF16)
        pT = A.ring("pT5", 3, [128, 512], BF16)
        VO = A.alloc("VO", [128, NLOC, 64], BF16)
        ones64 = A.alloc("ones64", [128, 64], F32)
        S.op("dve", "memset", writes=[ones64], ap=ones64[:], constant=1.0)
        for blk in range(NLOC):
            S.op("dve", "tensor_scalar", reads=[ones64, self.validt], writes=[VO], out=VO[:, blk, :], in0=ones64[:, :],
                 scalar1=self.validt[:, T0 + blk:T0 + blk + 1], scalar2=None, op0=ALU.mult)
        es0 = A.alloc("es0", [1, 16], F32)
        S.op(LOADQ, "dma_start", reads=[self.sinks], writes=[es0], dma="g0", out=es0[:], in_=self.sinks[0:1, :])
        S.op("act", "activation", reads=[es0], writes=[es0], out=es0[:], in_=es0[:], func=AF.Exp)
        esrow = A.alloc("esrow", [1, 2048], F32)
        eshi = A.alloc("eshi", [1, 2048], BF16)
        eslo = A.alloc("eslo", [1, 2048], BF16)
        onesb = A.alloc("onesb", [1, 64], BF16)
        S.op("dve", "memset", writes=[onesb], ap=onesb[:], constant=1.0)
        for h in range(16):
            S.op("dve", "tensor_scalar", reads=[self.cf, es0], writes=[esrow], out=esrow[0:1, h * 128:(h + 1) * 128], in0=self.cf[0:1, C_ONE:C_ONE + 128],
                 scalar1=es0[0:1, h:h + 1], scalar2=None, op0=ALU.mult)
        S.op("dve", "tensor_copy", reads=[esrow], writes=[eshi], out=eshi[:], in_=esrow[:])
        S.op("dve", "tensor_tensor", reads=[esrow, eshi], writes=[esrow], out=esrow[:], in0=esrow[:], in1=eshi[:], op=ALU.subtract)
        S.op("dve", "tensor_copy", reads=[esrow], writes=[eslo], out=eslo[:], in_=esrow[:])
        rec5 = A.ring("rec5", 2, [64, 512], F32)
        ao5 = A.ring("ao5", 2, [64, 512], BF16)
        psS = [ps[0], ps[1], ps[6]]
        psO = [ps[2], ps[3]]
        psD = [ps[4], ps[5]]
        for b in Qe:
            S.op("dve", "memset", writes=[b], ap=b[64:128, :], constant=0.0)
        for b in Qo:
            S.op("dve", "memset", writes=[b], ap=b[0:64, :], constant=0.0)
        steps = []
        cnt = {"h": 0, "s": 0}
        for gi in range(1, NLOC):
            col0 = gi * 128
            qe = Qe[gi % 2]
            qo = Qo[gi % 2]

            def pre_q(col0=col0, qe=qe, qo=qo, gi=gi):
                S.op(LOADQ, "dma_start", reads=[self.QCT], writes=[qe], dma=f"qe{gi % 2}", out=qe[0:64, :].rearrange("p (a r) -> p a r", a=8),
                     in_=self.QCT[:, 0:64, col0:col0 + 128].rearrange("a p r -> p a r"))
                S.op(LOADQ, "dma_start", reads=[self.QCT], writes=[qo], dma=f"qo{gi % 2}", out=qo[64:128, :].rearrange("p (a r) -> p a r", a=8),
                     in_=self.QCT[:, 64:128, col0:col0 + 128].rearrange("a p r -> p a r"))
            first = True
            for s_ in range(2):
                Q = qe if s_ == 0 else qo
                for u in range(2):
                    hk = cnt["h"]
                    cnt["h"] += 1
                    po = psO[hk % 2]
                    pd = psD[hk % 2]
                    h0 = 8 * s_ + 4 * u
                    for kb in range(2):
                        sk = cnt["s"]
                        cnt["s"] += 1
                        pS = psS[sk % 3]
                        pt = pT[sk % 3]
                        blk = gi - 1 + kb

                        def qk(kb=kb, h0=h0, pS=pS, Q=Q, u=u, blk=blk):
                            S.op("pe", "matmul", reads=[self.identb, TB], writes=[pS], out=pS[:, 0:512], lhsT=self.identb[:, :],
                                 rhs=TB[:, kb, h0 * 128:h0 * 128 + 512], start=True, stop=False)
                            S.op("pe", "matmul", reads=[KC, Q], writes=[pS], out=pS[:, 0:512], lhsT=KC[:, blk * 128:(blk + 1) * 128],
                                 rhs=Q[:, u * 512:(u + 1) * 512], start=False, stop=True)

                        def ex(pS=pS, pt=pt):
                            S.op("act", "activation", reads=[pS], writes=[pt], out=pt[:, :], in_=pS[:, 0:512], func=AF.Exp)

                        def pv(kb=kb, po=po, pd=pd, pt=pt, s_=s_, blk=blk, h0=h0):
                            S.op("pe", "matmul", reads=[VCt, pt], writes=[po], out=po[0:64, 0:512], lhsT=VCt[:, blk, s_ * 66:s_ * 66 + 64], rhs=pt[:, :],
                                 start=(kb == 0), stop=(kb == 1))
                            S.op("pe", "matmul", reads=[VO, pt], writes=[pd], out=pd[0:64, 0:512], lhsT=VO[:, blk, :], rhs=pt[:, :],
                                 start=(kb == 0), stop=False)
                            if kb == 1:
                                S.op("pe", "matmul", reads=[onesb, eshi], writes=[pd], out=pd[0:64, 0:512], lhsT=onesb[0:1, :],
                                     rhs=eshi[0:1, h0 * 128:h0 * 128 + 512], start=False, stop=False)
                                S.op("pe", "matmul", reads=[onesb, eslo], writes=[pd], out=pd[0:64, 0:512], lhsT=onesb[0:1, :],
                                     rhs=eslo[0:1, h0 * 128:h0 * 128 + 512], start=False, stop=True)
                        st = {"qk": qk, "ex": ex, "pv": pv}
                        if first:
                            st["pre"] = pre_q
                            first = False
                        if kb == 1:
                            def post(po=po, pd=pd, h0=h0, col0=col0, hk=hk):
                                rec = rec5[hk % 2]
                                ao = ao5[hk % 2]
                                S.op("act", "activation", reads=[pd], writes=[rec], out=rec[:, :], in_=pd[0:64, 0:512], func=AF.Ln)
                                S.op("act", "activation", reads=[rec], writes=[rec], out=rec[:, :], in_=rec[:, :], func=AF.Exp, scale=-1.0)
                                S.op("dve", "tensor_tensor", reads=[po, rec], writes=[ao], out=ao[:, :], in0=po[0:64, 0:512], in1=rec[:, :], op=ALU.mult)
                                S.op(STOREQ, "dma_start", reads=[ao], writes=[self.AOT], dma=f"ao{hk % 2}",
                                     out=self.AOT[h0 // 2:h0 // 2 + 2, :, col0:col0 + 128].rearrange("p (two d) r -> d (p two) r", two=2),
                                     in_=ao[0:64, 0:512].rearrange("d (h r) -> d h r", h=4))
                            st["post"] = post
                        steps.append(st)
        self.run_steps(steps, delay=2, ahead=2)

    def phase5s(self):
        S, A, ps = self.S, self.A, self.ps
        S.barrier()
        es = self.es
        kcf = A.alloc("kcf", [128, 128], F32)
        vcf = A.alloc("vcf", [128, 128], F32)
        KCs = A.alloc("KCs", [128, 144], BF16)
        Vc = A.alloc("Vc", [128, 2, 2, 66], BF16)
        Qbd = A.alloc("Qbd5", [128, 8, 32], BF16)
        TSSf = A.alloc("TSSf", [128, 2, 16, 16], F32)
        TSSb = A.alloc("TSSb", [128, 2, 256], BF16)
        pt = A.alloc("pt5s", [128, 512], BF16)
        fin = self.alloc_fin(ps[7])
        psK = ps[5]
        psS = ps[6]
        psO = [ps[0], ps[1]]
        S.op(LOADQ, "dma_start", reads=[self.tss], writes=[TSSf], dma="tsf", out=TSSf[:].rearrange("p a b c -> p (a b c)"), in_=self.tss[:, :])
        for blk in range(2):
            S.op("dve", "tensor_copy", reads=[TSSf], writes=[TSSb], out=TSSb[:, blk, :].rearrange("p (a s t) -> p a s t", a=8, s=2),
                 in_=TSSf[:, blk, :, :].rearrange("p (s a) t -> p a s t", s=2))
        S.op("dve", "memset", writes=[Vc], ap=Vc[:, :, :, 64], constant=1.0)
        for q in range(4):
            r0 = q * 16
            S.op(LOADQ, "dma_start", reads=[self.cck], writes=[kcf], dma="kcf", out=kcf[:, :], in_=self.cck[q, :, :])
            S.op(LOADQ, "dma_start", reads=[self.ccv], writes=[vcf], dma="vcf", out=vcf[:, :], in_=self.ccv[q, :, :])
            S.op("pe", "transpose", reads=[kcf, self.cf], writes=[psK], out=psK[:, 0:128], in_=kcf[:, :], identity=self.cf[:, C_ID:C_ID + 128])
            S.op("act", "copy", reads=[psK], writes=[KCs], out=KCs[:, 0:128], in_=psK[:, 0:128])
            S.op(LOADQ, "dma_start", reads=[self.sKCT], writes=[KCs], dma="kcs_new", out=KCs[:, 128:144], in_=self.sKCT[:, r0:r0 + 16])
            S.op("dve", "tensor_copy", reads=[vcf], writes=[Vc], out=Vc[:, 0, :, 0:64], in_=vcf[:, :].rearrange("p (h d) -> p h d", h=2))
            S.op(LOADQ, "dma_start", reads=[self.sVC], writes=[Vc], dma="vc_new", out=Vc[0:16, 1, :, :].rearrange("p h d -> p (h d)"),
                 in_=self.sVC[r0:r0 + 16, :])
            S.op("dve", "memset", writes=[Qbd], ap=Qbd[:], constant=0.0)
            S.op(LOADQ, "dma_start", reads=[self.sQCT], writes=[Qbd], dma="qbd5", out=Qbd[0:64, :, 0:16],
                 in_=self.sQCT[:, 0:64, r0:r0 + 16].rearrange("a p r -> p a r"))
            S.op(LOADQ, "dma_start", reads=[self.sQCT], writes=[Qbd], dma="qbd5", out=Qbd[64:128, :, 16:32],
                 in_=self.sQCT[:, 64:128, r0:r0 + 16].rearrange("a p r -> p a r"))
            Q2 = Qbd[:].rearrange("p a c -> p (a c)")
            S.op("pe", "matmul", reads=[self.identb, TSSb], writes=[psS], out=psS[:, 0:256], lhsT=self.identb[:, :], rhs=TSSb[:, 0, :],
                 start=True, stop=False)
            S.op("pe", "matmul", reads=[KCs, Qbd], writes=[psS], out=psS[:, 0:256], lhsT=KCs[:, 0:128], rhs=Q2, start=False, stop=True)
            S.op("pe", "matmul", reads=[self.identb, TSSb], writes=[psS], out=psS[0:16, 256:512], lhsT=self.identb[0:16, 0:16], rhs=TSSb[0:16, 1, :],
                 start=True, stop=False)
            S.op("pe", "matmul", reads=[KCs, Qbd], writes=[psS], out=psS[0:16, 256:512], lhsT=KCs[:, 128:144], rhs=Q2, start=False, stop=True)
            S.op("act", "activation", reads=[psS], writes=[pt], out=pt[:, 0:256], in_=psS[:, 0:256], func=AF.Exp)
            S.op("act", "activation", reads=[psS], writes=[pt], out=pt[0:16, 256:512], in_=psS[0:16, 256:512], func=AF.Exp)
            po = psO[q // 2]
            for a in range(8):
                for s in range(2):
                    h = 8 * s + a
                    oc = ((q % 2) * 16 + h) * 16
                    cc = a * 32 + s * 16
                    S.op("pe", "matmul", reads=[Vc, pt], writes=[po], out=po[0:65, oc:oc + 16], lhsT=Vc[:, 0, s, 0:65], rhs=pt[:, cc:cc + 16],
                         start=True, stop=False)
                    S.op("pe", "matmul", reads=[Vc, pt], writes=[po], out=po[0:65, oc:oc + 16], lhsT=Vc[0:16, 1, s, 0:65], rhs=pt[0:16, 256 + cc:256 + cc + 16],
                         start=False, stop=True)
        for bk in range(2):
            def dst(ao, bk=bk):
                for qq in range(2):
                    c0 = (2 * bk + qq) * 16
                    S.op(STOREQ, "dma_start", reads=[ao], writes=[self.sAOT], dma=f"ao{bk}",
                         out=self.sAOT[:, :, c0:c0 + 16].rearrange("p (two d) t -> d (p two) t", two=2),
                         in_=ao[0:64, qq * 256:(qq + 1) * 256].rearrange("d (h t) -> d h t", h=16))
            blocks = [((qq * 16 + h) * 16, 16, h) for qq in range(2) for h in range(16)]
            self.finalize(psO[bk], 512, dst, bk, fin, sink_cols=(es, blocks))


_CACHE = {}


def get_nc(phases):
    key = tuple(sorted(phases))
    if key not in _CACHE:
        _CACHE[key] = Builder(set(phases)).nc
    return _CACHE[key]


def kernel(x_prompt, x_sample, cache_a_k, cache_a_v, cache_a_logf, cache_b_k, cache_b_v,
           cache_c_k, cache_c_v, norm_mix, norm_ffn, norm_final, w_in_even, b_forget,
           rel_bias_b, w_out_even, w_in_odd, sinks_c, w_out_odd, t5_bias, w_ffn_in, w_ffn_out):
    phases = [int(p) for p in os.environ.get("K_PHASES", "1,2,3,4,5,6").split(",")]
    f = lambda a: np.ascontiguousarray(np.asarray(a, dtype=np.float32))
    x_prompt = f(x_prompt)
    x_sample = f(x_sample)
    nc = get_nc(phases)
    cst = host_consts()
    bb, bs, ts, tss = host_bias_tables(f(rel_bias_b), f(t5_bias))
    shared = {
        "cst": cst, "norm_mix": f(norm_mix), "norm_ffn": f(norm_ffn), "norm_final": f(norm_final).reshape(1, D),
        "w_in_even": f(w_in_even)[0], "b_forget": f(b_forget).reshape(1, 8), "w_out_even": f(w_out_even)[0],
        "w_out_odd": f(w_out_odd)[0], "w_in_odd": f(w_in_odd)[0], "sinks_c": f(sinks_c).reshape(1, 16),
        "w_ffn_in": f(w_ffn_in), "w_ffn_out": f(w_ffn_out),
        "bb": bb.reshape(128, -1), "bs": bs.reshape(128, -1), "ts": ts.reshape(128, -1), "tss": tss.reshape(128, -1),
    }
    in_maps = []
    for c in range(8):
        b, half = c // 2, c % 2
        if half == 1:
            xc = x_prompt[b]
            valid = np.ones((128, NCTX), np.float32)
        else:
            xc = np.concatenate([np.zeros((4096, D), np.float32), x_prompt[b, 0:4096]], axis=0)
            valid = np.ones((128, NCTX), np.float32)
            valid[:, 0:32] = 0.0
        sl = slice(4 * c, 4 * c + 4)
        m = dict(shared)
        m.update({
            "xc": np.ascontiguousarray(xc), "xs": np.ascontiguousarray(x_sample[sl].reshape(SR, D)), "valid": valid,
            "cak": f(cache_a_k[0, sl]).reshape(4, 4096, 512), "cav": f(cache_a_v[0, sl]).reshape(4, 4096, 512),
            "calf": f(cache_a_logf[0, sl]).reshape(4, 4096, 8),
            "cbk": f(cache_b_k[0, sl]).reshape(4, 512, 512), "cbv": f(cache_b_v[0, sl]).reshape(4, 512, 512),
            "cck": f(cache_c_k[0, sl]).reshape(4, 128, 128), "ccv": f(cache_c_v[0, sl]).reshape(4, 128, 128),
        })
        in_maps.append(m)
    res = run_bass_kernel_spmd(nc, in_maps, core_ids=list(range(8)))
    R = res.results
    y_prompt = np.zeros((4, 8192, D), np.float32)
    a_k = np.zeros((1, 4, 8192, 8, 64), np.float32)
    a_v = np.zeros((1, 4, 8192, 8, 64), np.float32)
    a_lf = np.zeros((1, 4, 8192, 8), np.float32)
    b_k = np.zeros((1, 4, 512, 8, 64), np.float32)
    b_v = np.zeros((1, 4, 512, 8, 64), np.float32)
    c_k = np.zeros((1, 4, 128, 2, 64), np.float32)
    c_v = np.zeros((1, 4, 128, 2, 64), np.float32)
    y_s = np.zeros((32, 16, D), np.float32)
    sak = np.zeros((1, 32, 16, 8, 64), np.float32)
    sav = np.zeros((1, 32, 16, 8, 64), np.float32)
    salf = np.zeros((1, 32, 16, 8), np.float32)
    sbk = np.zeros((1, 32, 16, 8, 64), np.float32)
    sbv = np.zeros((1, 32, 16, 8, 64), np.float32)
    sck = np.zeros((1, 32, 16, 2, 64), np.float32)
    scv = np.zeros((1, 32, 16, 2, 64), np.float32)
    for c in range(8):
        b, half = c // 2, c % 2
        r = R[c]
        rows = slice(half * 4096, (half + 1) * 4096)
        y_prompt[b, rows] = r["o_y"]
        a_k[0, b, rows] = r["o_ak"].reshape(4096, 8, 64)
        a_v[0, b, rows] = r["o_av"].reshape(4096, 8, 64)
        a_lf[0, b, rows] = r["o_alf"]
        if half == 1:
            b_k[0, b] = r["o_bk"].reshape(512, 8, 64)
            b_v[0, b] = r["o_bv"].reshape(512, 8, 64)
            c_k[0, b] = r["o_ck"].reshape(128, 2, 64)
            c_v[0, b] = r["o_cv"].reshape(128, 2, 64)
        sl = slice(4 * c, 4 * c + 4)
        y_s[sl] = r["o_ys"].reshape(4, 16, D)
        sak[0, sl] = r["o_sak"].reshape(4, 16, 8, 64)
        sav[0, sl] = r["o_sav"].reshape(4, 16, 8, 64)
        salf[0, sl] = r["o_salf"].reshape(4, 16, 8)
        sbk[0, sl] = r["o_sbk"].reshape(4, 16, 8, 64)
        sbv[0, sl] = r["o_sbv"].reshape(4, 16, 8, 64)
        sck[0, sl] = r["o_sck"].reshape(4, 16, 2, 64)
        scv[0, sl] = r["o_scv"].reshape(4, 16, 2, 64)
    return (y_prompt, y_s, a_k, a_v, a_lf, b_k, b_v, c_k, c_v, sak, sav, salf, sbk, sbv, sck, scv)
```
